# Optimizing a Trainium2 kernel written in Bass

```python
import math
import jax, jax.numpy as jnp
from jax import lax
import numpy as np

D_MODEL = 2048
BATCH = 2
SEQ = 4096
DEPTH = 1

CTX_LEN = 256
GRID_W = 64
MIX_WIDTH = D_MODEL
ATTN_WIDTH = MIX_WIDTH // 2
CONV_WIDTH = MIX_WIDTH - ATTN_WIDTH
HEAD_DIM = 128
N_HEADS = ATTN_WIDTH // HEAD_DIM
N_KV_HEADS = 2
GROUP = N_HEADS // N_KV_HEADS
KV_WIDTH = N_KV_HEADS * HEAD_DIM
CONV_GROUPS = CONV_WIDTH // HEAD_DIM
CONV_K = 3
Q_BLOCK = 128
ROPE_THETA = 10000.0
EPS = 1e-6
IN_COLS = ATTN_WIDTH + 2 * KV_WIDTH + ATTN_WIDTH + 4 * CONV_WIDTH

kernel_name = "hybrid_gqa_shortconv_dit_layer"


def rms_norm(x, g):
    xf = x.astype(jnp.float32)
    y = xf * lax.rsqrt(jnp.mean(xf * xf, axis=-1, keepdims=True) + EPS)
    return (y * g.astype(jnp.float32)).astype(x.dtype)


def adaln_params(cond, w_mod, b_mod):
    m = jax.nn.silu(cond) @ w_mod + b_mod
    return jnp.split(m, 3, axis=-1)


def axial_rope(x, row, col):
    half = HEAD_DIM // 2
    quarter = half // 2
    inv = ROPE_THETA ** (-jnp.arange(quarter, dtype=jnp.float32) / quarter)

    def rot(xa, pos):
        ang = pos.astype(jnp.float32)[:, None] * inv[None, :]
        cos = jnp.cos(ang)[None, :, None, :]
        sin = jnp.sin(ang)[None, :, None, :]
        xa = xa.astype(jnp.float32)
        x1, x2 = xa[..., :quarter], xa[..., quarter:]
        return jnp.concatenate([x1 * cos - x2 * sin, x2 * cos + x1 * sin], axis=-1)

    out = jnp.concatenate([rot(x[..., :half], row), rot(x[..., half:], col)], axis=-1)
    return out.astype(x.dtype)


def short_conv(u, w):
    L = u.shape[1]
    up = jnp.pad(u, ((0, 0), (1, 1), (0, 0)))
    return up[:, :L] * w[0] + up[:, 1:L + 1] * w[1] + up[:, 2:] * w[2]


def split_proj(p):
    sizes = [ATTN_WIDTH, KV_WIDTH, KV_WIDTH, ATTN_WIDTH,
             CONV_WIDTH, CONV_WIDTH, CONV_WIDTH, CONV_WIDTH]
    idx = np.cumsum(sizes)[:-1].tolist()
    return jnp.split(p, idx, axis=-1)


def qk_heads(q, k, v, q_g, k_g):
    B, L = q.shape[:2]
    q = rms_norm(q.reshape(B, L, N_HEADS, HEAD_DIM), q_g)
    k = rms_norm(k.reshape(B, L, N_KV_HEADS, HEAD_DIM), k_g)
    v = v.reshape(B, L, N_KV_HEADS, HEAD_DIM)
    return q, k, v


def latent_attention(q, k_lat, v_lat, k_ctx, v_ctx):
    B, S = q.shape[:2]
    scale = 1.0 / math.sqrt(HEAD_DIM)
    k_all = jnp.concatenate([k_ctx, k_lat], axis=1)
    v_all = jnp.concatenate([v_ctx, v_lat], axis=1)
    n_blk = S // Q_BLOCK
    qb = q.reshape(B, n_blk, Q_BLOCK, N_KV_HEADS, GROUP, HEAD_DIM).transpose(1, 0, 3, 4, 2, 5)

    def one_block(q_blk):
        s = jnp.einsum('bkgqd,bskd->bkgqs', q_blk, k_all).astype(jnp.float32) * scale
        p = jax.nn.softmax(s, axis=-1).astype(v_all.dtype)
        return jnp.einsum('bkgqs,bskd->bkgqd', p, v_all)

    o = lax.map(one_block, qb)
    return o.transpose(1, 0, 4, 2, 3, 5).reshape(B, S, ATTN_WIDTH)


def context_attention(q, k, v):
    B, L = q.shape[:2]
    scale = 1.0 / math.sqrt(HEAD_DIM)
    qg = q.reshape(B, L, N_KV_HEADS, GROUP, HEAD_DIM)
    s = jnp.einsum('bqkgd,bskd->bkgqs', qg, k).astype(jnp.float32) * scale
    p = jax.nn.softmax(s, axis=-1).astype(v.dtype)
    o = jnp.einsum('bkgqs,bskd->bqkgd', p, v)
    return o.reshape(B, L, ATTN_WIDTH)


def conv_branch(b, cg, h, gate_c, w):
    return jax.nn.silu(gate_c) * (b * short_conv(cg * h, w))


def setup_inputs(seed: int = 0) -> dict:
    key = jax.random.key(seed)
    ks = jax.random.split(key, 13)
    D = D_MODEL
    x = jax.random.normal(ks[0], (BATCH, SEQ, D), jnp.float32)
    c = jax.random.normal(ks[1], (BATCH, D), jnp.float32)
    ctx = jax.random.normal(ks[2], (BATCH, CTX_LEN, D), jnp.float32)
    c_ctx = 0.5 * jax.random.normal(ks[3], (D,), jnp.float32)
    w_mod = 0.3 * D ** -0.5 * jax.random.normal(ks[4], (DEPTH, D, 3 * D), jnp.float32)
    b_mod = 0.02 * jax.random.normal(ks[5], (DEPTH, 3 * D), jnp.float32)
    norm_g = 1.0 + 0.02 * jax.random.normal(ks[6], (DEPTH, D), jnp.float32)
    w_in = D ** -0.5 * jax.random.normal(ks[7], (DEPTH, D, IN_COLS), jnp.float32)
    q_norm_g = 1.0 + 0.02 * jax.random.normal(ks[8], (DEPTH, HEAD_DIM), jnp.float32)
    k_norm_g = 1.0 + 0.02 * jax.random.normal(ks[9], (DEPTH, HEAD_DIM), jnp.float32)
    conv_w = CONV_K ** -0.5 * jax.random.normal(ks[10], (DEPTH, CONV_K, CONV_WIDTH), jnp.float32)
    w_out = MIX_WIDTH ** -0.5 * jax.random.normal(ks[11], (DEPTH, MIX_WIDTH, D), jnp.float32)
    final_norm_g = 1.0 + 0.02 * jax.random.normal(ks[12], (D,), jnp.float32)
    return {"x": x, "c": c, "ctx": ctx, "c_ctx": c_ctx, "w_mod": w_mod, "b_mod": b_mod,
            "norm_g": norm_g, "w_in": w_in, "q_norm_g": q_norm_g, "k_norm_g": k_norm_g,
            "conv_w": conv_w, "w_out": w_out, "final_norm_g": final_norm_g}


def reference(x, c, ctx, c_ctx, w_mod, b_mod, norm_g, w_in, q_norm_g, k_norm_g,
              conv_w, w_out, final_norm_g):
    S = x.shape[1]
    ROWS = S // GRID_W
    row = jnp.repeat(jnp.arange(ROWS, dtype=jnp.int32), GRID_W)
    col = jnp.tile(jnp.arange(GRID_W, dtype=jnp.int32), ROWS)

    for layer in range(DEPTH):
        shift, scale, gate = adaln_params(c, w_mod[layer], b_mod[layer])
        shift_c, scale_c, gate_c = adaln_params(c_ctx, w_mod[layer], b_mod[layer])

        h_ctx = rms_norm(ctx, norm_g[layer]) * (1.0 + scale_c) + shift_c
        qc, kc, vc, ga_c, b_c, cg_c, hh_c, gc_c = split_proj(h_ctx @ w_in[layer])
        qc, kc, vc = qk_heads(qc, kc, vc, q_norm_g[layer], k_norm_g[layer])

        h = rms_norm(x, norm_g[layer]) * (1.0 + scale[:, None, :]) + shift[:, None, :]
        q, k, v, ga, b, cg, hh, gc = split_proj(h @ w_in[layer])
        q, k, v = qk_heads(q, k, v, q_norm_g[layer], k_norm_g[layer])
        q = axial_rope(q, row, col)
        k = axial_rope(k, row, col)
        attn = jax.nn.silu(ga) * latent_attention(q, k, v, kc, vc)
        conv = conv_branch(b, cg, hh, gc, conv_w[layer])
        y = jnp.concatenate([attn, conv], axis=-1) @ w_out[layer]
        x = x + gate[:, None, :] * y

        if layer < DEPTH - 1:
            attn_c = jax.nn.silu(ga_c) * context_attention(qc, kc, vc)
            conv_c = conv_branch(b_c, cg_c, hh_c, gc_c, conv_w[layer])
            y_c = jnp.concatenate([attn_c, conv_c], axis=-1) @ w_out[layer]
            ctx = ctx + gate_c * y_c

    return rms_norm(x, final_norm_g)
```

```python
import math
from contextlib import ExitStack

import numpy as np
import concourse.bass as bass
import concourse.mybir as mybir
from concourse.bass_utils import run_bass_kernel_spmd

F32 = mybir.dt.float32
BF16 = mybir.dt.bfloat16
AF = mybir.ActivationFunctionType
ALU = mybir.AluOpType

D = 2048
KC = 16
SEQ = 4096
CTX = 256
OWN = 1024
NKEY = CTX + SEQ
NCHUNK = NKEY // 128
EPS = 1e-6
ENGS = ["tensor", "scalar", "vector", "gpsimd", "sync"]


class Prog:
    def __init__(self, nc, es):
        self.nc = nc
        self.es = es
        self.streams = {e: [] for e in ENGS}
        self.esem = {e: es.enter_context(nc.semaphore("p_" + e)) for e in ENGS[:4]}
        self.ecount = {e: 0 for e in ENGS[:4]}
        self.dsem = {}
        self.dcount = {}
        self.lastw = {}
        self.readers = {}
        self.waited = {e: {} for e in ENGS}

    def _sem(self, key):
        if key[0] == "e":
            return self.esem[key[1]]
        return self.dsem[key[1]]

    def _waits(self, stream, reads, writes, skip=None):
        need = {}

        def add(tok):
            k, v = tok
            if need.get(k, 0) < v:
                need[k] = v

        for r in reads:
            if r in self.lastw:
                add(self.lastw[r])
        for w in writes:
            if w in self.lastw:
                add(self.lastw[w])
            for k, v in self.readers.get(w, {}).items():
                add((k, v))
        for k, v in need.items():
            if k == skip or k == ("e", "tensor") == ("e", stream):
                continue
            if k[0] == "e":
                assert self.ecount[k[1]] >= v, f"dependency on unsignaled op {k} {v} {self.ecount[k[1]]}"
            if self.waited[stream].get(k, 0) >= v:
                continue
            self.waited[stream][k] = v
            sem = self._sem(k)
            self.streams[stream].append(lambda e, sem=sem, v=v: e.wait_ge(sem, v))

    def _record(self, tok, reads, writes):
        k, v = tok
        for r in reads:
            d = self.readers.setdefault(r, {})
            if d.get(k, 0) < v:
                d[k] = v
        for w in writes:
            self.lastw[w] = tok
            self.readers[w] = {}

    def op(self, eng, fn, reads=(), writes=(), signal=True):
        ps_reads = [r for r in reads if isinstance(r, tuple) and r[0] == "ps"]
        if ps_reads:
            reads = [r for r in reads if r not in ps_reads]
            writes = list(writes) + ps_reads
        self._waits(eng, reads, writes)
        if signal:
            self.ecount[eng] += 1
            tok = (("e", eng), self.ecount[eng])
            sem = self.esem[eng]
            self.streams[eng].append(lambda e, fn=fn, sem=sem: fn(e).then_inc(sem, 1))
        else:
            tok = (("e", eng), self.ecount[eng] + 1)
            self.streams[eng].append(lambda e, fn=fn: fn(e))
        self._record(tok, reads, writes)

    def dma(self, queue, key, out, in_, reads=(), writes=(), cont=False):
        if key not in self.dsem:
            self.dsem[key] = self.es.enter_context(self.nc.semaphore("d_" + key))
            self.dcount[key] = 0
        self._waits(queue, reads, writes, skip=("d", key) if cont else None)
        self.dcount[key] += 16
        tok = (("d", key), self.dcount[key])
        sem = self.dsem[key]
        self.streams[queue].append(
            lambda e, out=out, in_=in_, sem=sem: e.dma_start(out=out, in_=in_).then_inc(sem, 16))
        self._record(tok, reads, writes)

    def inherit(self, news, olds):
        for n in news:
            d = self.readers.setdefault(n, {})
            for o in olds:
                for k, v in self.readers.get(o, {}).items():
                    if d.get(k, 0) < v:
                        d[k] = v
                if o in self.lastw:
                    k, v = self.lastw[o]
                    if d.get(k, 0) < v:
                        d[k] = v

    def final_wait(self, stream, keys):
        for key in keys:
            sem = self.dsem[key]
            v = self.dcount[key]
            self.streams[stream].append(lambda e, sem=sem, v=v: e.wait_ge(sem, v))

    def emit(self, block):
        for eng in ENGS:
            thunks = self.streams[eng]
            if not thunks:
                continue

            def body(e, thunks=thunks):
                for t in thunks:
                    t(e)

            getattr(block, eng)(body)


class _Stop(Exception):
    pass


def build_program(stop=None):
    nc = bass.Bass("TRN2", target_bir_lowering=False)

    def din(name, shape):
        return nc.dram_tensor(name, shape, F32, kind="ExternalInput").ap()

    xT = din("xT", [D, SEQ])
    xTh = din("xTh", [D, 2])
    xown = din("xown", [OWN, D])
    ctxT = din("ctxT", [D, CTX])
    cc = din("cc", [128, 32])
    w_mod = din("w_mod", [D, 3 * D])
    bmod = din("bmod", [128, 48])
    ng = din("ng", [128, 16])
    w_in = din("w_in", [D, 2560])
    w_conv = din("w_conv", [D, 4096])
    qg = din("qg", [128, 128])
    kg = din("kg", [128, 128])
    convw = din("convw", [128, 24])
    w_out = din("w_out", [D, D])
    fg = din("fg", [128, D])
    cosF = din("cosF", [SEQ, 128])
    sinF = din("sinF", [SEQ, 128])
    ident = din("ident", [128, 128])
    hmask = din("hmask", [128, 2])
    out = nc.dram_tensor("out", [OWN, D], F32, kind="ExternalOutput").ap()

    xT_v = xT.rearrange("(k p) n -> p k n", p=128)
    xTh_v = xTh.rearrange("(k p) n -> p k n", p=128)
    ctxT_v = ctxT.rearrange("(k p) n -> p k n", p=128)
    w_mod_v = w_mod.rearrange("(k p) n -> p k n", p=128)
    w_in_v = w_in.rearrange("(k p) n -> p k n", p=128)
    w_conv_v = w_conv.rearrange("(k p) n -> p k n", p=128)
    w_out_v = w_out.rearrange("(k p) n -> p k n", p=128)

    with ExitStack() as es:
        def sb(name, shape, dt):
            return es.enter_context(nc.sbuf_tensor(name, shape, dt))

        big = sb("big", [128, 16384], F32)
        R2 = sb("R2", [128, 16384], F32)
        hHt = sb("hHt", [128, 16, 2], BF16)
        wg = sb("wg", [128, 16, 256], BF16)
        xh = sb("xh", [128, 16, 2], F32)
        KT = sb("KT", [128, 2, NKEY], BF16)
        V = sb("V", [128, NCHUNK, 256], BF16)
        work = sb("work", [128, 5376], F32)
        ident_bf = sb("ident_bf", [128, 128], BF16)
        ident_f = sb("ident_f", [128, 128], F32)
        ones_bf = sb("ones_bf", [128, 128], BF16)
        ones_f = sb("ones_f", [128, 128], F32)
        qg_sb = sb("qg_sb", [128, 128], F32)
        kg_sb = sb("kg_sb", [128, 128], F32)
        ng_sb = sb("ng_sb", [128, 16], F32)
        bmod_sb = sb("bmod_sb", [128, 48], F32)
        convw_sb = sb("convw_sb", [128, 24], F32)
        hmask_sb = sb("hmask_sb", [128, 2], F32)
        cc_sb = sb("cc_sb", [128, 32], F32)
        s_bf = sb("s_bf", [128, 16, 2], BF16)
        m_sb = sb("m_sb", [128, 48, 2], F32)
        gm = sb("gm", [128, 16, 2], F32)
        eps_sb = sb("eps_sb", [128, 1], F32)
        ss = sb("ss", [128, 8], F32)
        rstd_s = sb("rstd_s", [128, 8], F32)
        ssx = sb("ssx", [128, 8], F32)
        rsx = sb("rsx", [128, 8], F32)
        smx = sb("smx", [128, 4], F32)
        cosb = [sb(f"cosb{i}", [128, 4, 128], F32) for i in range(2)]
        sinb = [sb(f"sinb{i}", [128, 4, 128], F32) for i in range(2)]
        PS = [es.enter_context(nc.psum_tensor(f"PS{i}", [128, 1024], F32)) for i in range(4)]

        P = Prog(nc, es)
        block = es.enter_context(nc.Block())
        try:
            _body(locals(), stop)
        except _Stop:
            pass
        P.emit(block)
    return nc


def _body(env, stop):
        globals().update({})
        hHt = env["hHt"]
        ssx = env["ssx"]
        xh = env["xh"]
        rsx = env["rsx"]
        smx = env["smx"]
        wg = env["wg"]
        (nc, es, P, block, big, R2, KT, V, work, ident_bf, ident_f, ones_bf, ones_f, qg_sb, kg_sb, ng_sb, bmod_sb,
         convw_sb, hmask_sb, cc_sb, s_bf, m_sb, gm, eps_sb, ss, rstd_s, cosb, sinb, PS) = [env[k] for k in (
            "nc", "es", "P", "block", "big", "R2", "KT", "V", "work", "ident_bf", "ident_f", "ones_bf", "ones_f",
            "qg_sb", "kg_sb", "ng_sb", "bmod_sb", "convw_sb", "hmask_sb", "cc_sb", "s_bf", "m_sb", "gm", "eps_sb",
            "ss", "rstd_s", "cosb", "sinb", "PS")]
        w_conv_v = env["w_conv_v"]
        (xT_v, xTh_v, ctxT_v, w_mod_v, w_in_v, w_out_v, xown, cc, bmod, ng, qg, kg, convw, fg, cosF, sinF, ident,
         hmask, out) = [env[k] for k in (
            "xT_v", "xTh_v", "ctxT_v", "w_mod_v", "w_in_v", "w_out_v", "xown", "cc", "bmod", "ng", "qg", "kg",
            "convw", "fg", "cosF", "sinF", "ident", "hmask", "out")]

        def chk(name):
            if stop == name:
                raise _Stop()

        def bank(b):
            return PS[b // 2][:, (b % 2) * 512:(b % 2) * 512 + 512]

        def bank_bf(b):
            return bank(b).bitcast(BF16)

        def psr(b):
            return ("ps", b)

        xbuf = [big[:, i * 8192:(i + 1) * 8192].rearrange("p (k n) -> p k n", k=KC) for i in range(2)]
        qT = big[:, 0:4096].bitcast(BF16).rearrange("p (h n) -> p h n", h=8)
        sga = big[:, 4096:8192].bitcast(BF16).rearrange("p (h n) -> p h n", h=8)
        mixT = big[:, 8192:16384].bitcast(BF16).rearrange("p (k n) -> p k n", k=KC)
        hT_own = R2[:, 0:8192].bitcast(BF16).rearrange("p (k n) -> p k n", k=KC)
        hA = hT_own[:, :, 0:512]
        hB = hT_own[:, :, 512:1024]
        hH = hHt[:]
        wbuf = [R2[:, 8192 + i * 4096:8192 + (i + 1) * 4096].bitcast(BF16).rearrange("p (k n) -> p k n", k=KC)
                for i in range(2)]
        woutb = [R2[:, c * 4096:(c + 1) * 4096].bitcast(BF16).rearrange("p (k n) -> p k n", k=KC)
                 for c in range(4)]
        R2_res = [("hT", n, k) for n in "AB" for k in range(KC)] + [("wbuf", 0), ("wbuf", 1)]

        rstd_bc = work[:, 512:1024]
        kn = work[:, 1024:1536].rearrange("p (h n) -> p h n", h=4)
        t1 = work[:, 1536:2048].rearrange("p (h n) -> p h n", h=4)
        t2 = work[:, 2048:2560].rearrange("p (h n) -> p h n", h=4)
        rot_bf = [work[:, o:o + 256].bitcast(BF16).rearrange("p (h n) -> p h n", h=4)
                  for o in (2560, 2816, 4608, 3328)]
        junk = work[:, 3072:3328].bitcast(BF16)
        t1b = work[:, 0:512].rearrange("p (h n) -> p h n", h=4)
        t2b = work[:, 512:1024].rearrange("p (h n) -> p h n", h=4)
        cg_sb = work[:, 0:1026]
        u_sb = work[:, 1026:2052]
        acc_c = work[:, 2052:3076]
        acc2 = work[:, 3076:4100]
        sgc = work[:, 4100:5124]
        PT = [work[:, i * 512:(i + 1) * 512].bitcast(BF16) for i in range(2)]
        tmpb = [work[:, 1024 + i * 256:1024 + (i + 1) * 256].bitcast(BF16) for i in range(2)]
        accs = [work[:, 1536 + i * 512:1536 + (i + 1) * 512] for i in range(4)]
        lns_sb = work[:, 1536:2048]
        o_sb = work[:, 2048:2560]
        rs_sb = work[:, 3584:4096]
        ot_sb = work[:, 4096:4608]
        fg_sb = work[:, 0:2048]
        gate_bc = work[:, 2048:4096]
        diagb = [work[:, 4096 + i * 512:4096 + (i + 1) * 512].rearrange("p (a n) -> p a n", a=4) for i in range(2)]
        xo = [big[:, i * 2048:(i + 1) * 2048] for i in range(2)]
        resb = [big[:, 4096 + i * 2048:4096 + (i + 1) * 2048] for i in range(2)]

        for name, dst, src in [("cc", cc_sb, cc), ("bmod", bmod_sb, bmod), ("ng", ng_sb, ng), ("qg", qg_sb, qg),
                               ("kg", kg_sb, kg), ("convw", convw_sb, convw), ("hmask", hmask_sb, hmask),
                               ("identf", ident_f, ident)]:
            P.dma("sync", "const", dst[:], src, writes=[name])
        for name in ["cc", "bmod", "ng", "qg", "kg", "convw", "hmask", "identf"]:
            P.lastw[name] = (("d", "const"), P.dcount["const"])
        P.dma("gpsimd", "constg", ident_bf[:], ident, writes=["identbf"])
        P.op("vector", lambda e: e.memset(ones_bf[:], 1.0), writes=["ones_bf"])
        P.op("vector", lambda e: e.memset(ones_f[:], 1.0), writes=["ones_f"])
        P.op("vector", lambda e: e.memset(eps_sb[:], EPS), writes=["eps"])
        P.op("scalar", lambda e: e.activation(out=s_bf[:].rearrange("p k r -> p (k r)"), in_=cc_sb[:], func=AF.Silu),
             reads=["cc"], writes=["s_bf"])
        P.op("vector", lambda e: e.reduce_max(out=smx[:, 0:1], in_=qg_sb[:], axis=mybir.AxisListType.X,
                                              apply_absolute_value=True), reads=["qg"], writes=["smx0"])
        P.op("vector", lambda e: e.reduce_max(out=smx[:, 1:2], in_=kg_sb[:], axis=mybir.AxisListType.X,
                                              apply_absolute_value=True), reads=["kg"], writes=["smx1"])
        P.op("vector", lambda e: e.scalar_tensor_tensor(
            out=smx[:, 2:3], in0=smx[:, 0:1], scalar=-math.sqrt(128.0), in1=smx[:, 1:2], op0=ALU.mult, op1=ALU.mult),
            reads=["smx0", "smx1"], writes=["smx2"])

        seq = [("ctx", CTX, 1), ("halo", 2, 0), (6, 512, 0), (7, 512, 0)]

        def tile_src(tile, N):
            if tile == "ctx":
                return ctxT_v
            if tile == "halo":
                return xTh_v
            return xT_v[:, :, tile * 512:(tile + 1) * 512]

        def xsrc_buf(i):
            tile, N, r = seq[i]
            if tile == "halo":
                return xh[:], "xh"
            return xbuf[0][:, :, 0:N], ("xbuf", 0)

        def issue_x_load(i):
            tile, N, r = seq[i]
            xb, xres = xsrc_buf(i)
            P.dma("sync", "xh" if tile == "halo" else "x0", xb, tile_src(tile, N),
                  writes=[xres, (xres, "h", 0), (xres, "h", 1)])
            if isinstance(tile, int):
                tp = tile % 2
                P.dma("sync", f"c{tp}", cosb[tp][:], cosF[tile * 512:(tile + 1) * 512, :].rearrange("(b p) d -> p b d", p=128),
                      writes=[("cos", tp)])
                P.dma("sync", f"s{tp}", sinb[tp][:], sinF[tile * 512:(tile + 1) * 512, :].rearrange("(b p) d -> p b d", p=128),
                      writes=[("sin", tp)])

        issue_x_load(0)
        issue_x_load(1)

        def adaln_tile(t, wb, wres, key, pb):
            P.dma("gpsimd", key, wb[:], w_mod_v[:, :, t * 512:(t + 1) * 512], writes=wres)
            pv = bank(pb)[:, 0:96].rearrange("p (c r) -> p c r", r=2)
            for c4 in range(4):
                cidx = t * 4 + c4
                for kc in range(KC):
                    last = (c4 == 3 and kc == KC - 1)
                    P.op("tensor",
                         lambda e, pv=pv, wb=wb, c4=c4, cidx=cidx, kc=kc: e.matmul(
                             pv[:, cidx, :], wb[:, kc, c4 * 128:(c4 + 1) * 128], s_bf[:, kc, :],
                             start=(kc == 0), stop=(kc == KC - 1)),
                         reads=wres + ["s_bf"], writes=[psr(pb)], signal=last)
            return pv

        def adaln_evac(pv, pb, c0, c1, tag):
            P.op("vector",
                 lambda e: e.tensor_tensor(
                     out=m_sb[:, c0:c1, :], in0=pv[:, c0:c1, :],
                     in1=bmod_sb[:, c0:c1].unsqueeze(2).broadcast_to([128, c1 - c0, 2]), op=ALU.add),
                 reads=[psr(pb), "bmod"], writes=[("m", tag)])

        for t in range(8):
            pv = adaln_tile(t, wbuf[t % 2], [("wbuf", t % 2)], f"w{t % 2}", 0)
        adaln_evac(pv, 0, 0, 32, 7)
        P.op("vector",
             lambda e: e.scalar_tensor_tensor(
                 out=gm[:], in0=m_sb[:, 16:32, :], scalar=1.0,
                 in1=ng_sb[:].unsqueeze(2).broadcast_to([128, 16, 2]), op0=ALU.add, op1=ALU.mult),
             reads=[("m", 7), "ng"], writes=["gm"])

        chk("A")
        P.dma("gpsimd", "w0", wbuf[0][:], w_in_v[:, :, 1024:1536], writes=[("wbuf", 0)])
        hdst = {"ctx": ("B", hB[:, :, 0:CTX]), "halo": ("H", hH), 6: ("A", hA), 7: ("B", hB)}
        kn2 = [kn, work[:, 4096:4608].rearrange("p (h n) -> p h n", h=4)]
        pending = []
        gblk = [0]

        def flush_pending(keep=0):
            while len(pending) > keep:
                pending.pop(0)()

        def front_a(i):
            tile, N, r = seq[i]
            xb, xres = xsrc_buf(i)
            hname, hap = hdst[tile]
            hres = [("hT", hname, kc) for kc in range(KC)]
            ssb = 3

            def f1(hf=None):
                k0, k1 = (0, KC) if hf is None else (hf * 8, hf * 8 + 8)
                P.op("scalar", lambda e: e.activation(out=hap[:, k0:k1, :], in_=xb[:, k0:k1, :], func=AF.Square),
                     reads=[xres], writes=hres[k0:k1])

            def f2():
                for kc in range(KC):
                    P.op("tensor", lambda e, kc=kc: e.matmul(
                        bank(ssb)[:, 0:N], ones_bf[:], hap[:, kc, :], start=(kc == 0), stop=(kc == KC - 1)),
                        reads=[("hT", hname, kc), "ones_bf"], writes=[psr(ssb)], signal=(kc == KC - 1))
                P.op("scalar", lambda e: e.activation(
                    out=rstd_bc[:, 0:N], in_=bank(ssb)[:, 0:N], func=AF.Ln, bias=eps_sb[:, 0:1], scale=1.0 / D),
                    reads=[psr(ssb), "eps"], writes=["rstd_bc"])
                P.op("scalar", lambda e: e.activation(out=rstd_bc[:, 0:N], in_=rstd_bc[:, 0:N], func=AF.Exp, scale=-0.5),
                     reads=["rstd_bc"], writes=["rstd_bc"])

            def f3(hf=None):
                k0, k1 = (0, KC) if hf is None else (hf * 8, hf * 8 + 8)
                P.op("vector", lambda e: e.tensor_tensor(
                    out=xb[:, k0:k1, :], in0=xb[:, k0:k1, :],
                    in1=rstd_bc[:, 0:N].unsqueeze(1).broadcast_to([128, k1 - k0, N]), op=ALU.mult),
                    reads=[xres, "rstd_bc"], writes=[(xres, "h", hf)] if hf is not None else [xres])

            return [f1, f2, f3]

        def front_b(i, hf=None):
            tile, N, r = seq[i]
            xb, xres = xsrc_buf(i)
            hname, hap = hdst[tile]
            for kc in (range(KC) if hf is None else range(hf * 8, hf * 8 + 8)):
                if kc % 2 == 0:
                    P.op("scalar", lambda e, kc=kc: e.activation(
                        out=hap[:, kc, :], in_=xb[:, kc, :], func=AF.Identity,
                        bias=m_sb[:, kc, r:r + 1], scale=gm[:, kc, r:r + 1]),
                        reads=[xres, (xres, "h", kc // 8), "gm", ("m", 7)], writes=[("hT", hname, kc)])
                else:
                    P.op("vector", lambda e, kc=kc: e.tensor_scalar(
                        out=hap[:, kc, :], in0=xb[:, kc, :], scalar1=gm[:, kc, r:r + 1],
                        scalar2=m_sb[:, kc, r:r + 1], op0=ALU.mult, op1=ALU.add),
                        reads=[xres, (xres, "h", kc // 8), "gm", ("m", 7)], writes=[("hT", hname, kc)])
            if hf == 0:
                return
            if tile == "ctx":
                issue_x_load(2)
            elif tile == 6:
                issue_x_load(3)

        def kv(i):
            tile, N, r = seq[i]
            if tile == "halo":
                return []
            hname, hap = hdst[tile]
            blocks = []
            nblk = N // 128
            latent = isinstance(tile, int)
            key0 = 0 if tile == "ctx" else CTX + tile * 512
            tp = tile % 2 if latent else 0
            for blk in range(nblk):
                blocks.append(lambda blk=blk: kv_block(blk, hname, hap, latent, key0, tp))
            return blocks

        KVB = [4, 5, 0, 1]

        def kv_block(blk, hname, hap, latent, key0, tp, newpath=None, cs=None):
                if cs is None:
                    cs = (cosb[tp], sinb[tp], ("cos", tp), ("sin", tp))
                g = gblk[0]
                gblk[0] += 1
                kvb = KVB[g % 4]
                tb = 6 + (g % 2)
                knb = kn2[g % 2]
                rb = rot_bf[g % 3]
                rres = ("rot", g % 3)
                kres = ("kn", g % 2)
                if newpath is None:
                    for kc in range(KC):
                        P.op("tensor", lambda e, kc=kc, blk=blk, kvb=kvb: e.matmul(
                            bank(kvb), hap[:, kc, blk * 128:(blk + 1) * 128], wbuf[0][:, kc, :],
                            start=(kc == 0), stop=(kc == KC - 1)),
                            reads=[("hT", hname, kc), ("wbuf", 0)], writes=[psr(kvb)], signal=(kc == KC - 1))
                    srcap, sres = bank(kvb), psr(kvb)
                else:
                    xq, qres, jcol = newpath
                    for kc in range(KC):
                        P.op("tensor", lambda e, kc=kc, blk=blk, kvb=kvb: e.matmul(
                            bank(kvb), xq[:, kc, blk * 128:(blk + 1) * 128], wbuf[1][:, kc, :],
                            start=(kc == 0), stop=(kc == KC - 1)),
                            reads=[qres, ("wkv2", kc)], writes=[psr(kvb)], signal=(kc == KC - 1))
                    kr = kraw[g % 2]
                    sres = ("kraw", g % 2)
                    P.op("vector", lambda e, kvb=kvb, kr=kr, jcol=jcol: e.scalar_tensor_tensor(
                        out=kr, in0=bank(kvb), scalar=rsx[:, jcol:jcol + 1], in1=bias_sb, op0=ALU.mult, op1=ALU.add),
                        reads=[psr(kvb), ("rsx", jcol), "bias_sb"], writes=[sres])
                    srcap = kr
                chunk = key0 // 128 + blk
                for h in range(2):
                    P.op("scalar", lambda e, h=h: e.activation(
                        out=junk[:, h * 128:(h + 1) * 128], in_=srcap[:, h * 128:(h + 1) * 128], func=AF.Square,
                        accum_out=ss[:, h:h + 1]),
                        reads=[sres], writes=["ss", "junk"])
                P.op("scalar", lambda e: e.activation(out=rstd_s[:, 0:2], in_=ss[:, 0:2], func=AF.Ln,
                                                      bias=eps_sb[:, 0:1], scale=1.0 / 128),
                     reads=["ss", "eps"], writes=["rstd_s"])
                P.op("scalar", lambda e: e.activation(out=rstd_s[:, 0:2], in_=rstd_s[:, 0:2], func=AF.Exp, scale=-0.5),
                     reads=["rstd_s"], writes=["rstd_s"])
                if newpath is None:
                    P.op("scalar", lambda e, chunk=chunk: e.activation(
                        out=V[:, chunk, :], in_=srcap[:, 256:512], func=AF.Copy),
                        reads=[sres], writes=["V"])
                else:
                    P.op("scalar", lambda e, chunk=chunk: e.activation(
                        out=V[:, chunk, :], in_=srcap[:, 256:512], func=AF.Copy),
                        reads=[sres], writes=["V"])
                P.op("vector", lambda e, knb=knb: e.tensor_tensor(
                    out=knb[:, 0:2, :], in0=srcap[:, 0:256].rearrange("p (h n) -> p h n", h=2),
                    in1=rstd_s[:, 0:2].unsqueeze(2).broadcast_to([128, 2, 128]), op=ALU.mult),
                    reads=[sres, "rstd_s"], writes=[kres])
                if latent:
                    P.op("vector", lambda e, knb=knb: e.tensor_tensor(
                        out=knb[:, 0:2, :], in0=knb[:, 0:2, :], in1=kg_sb[:].unsqueeze(1).broadcast_to([128, 2, 128]),
                        op=ALU.mult), reads=[kres, "kg"], writes=[kres])
                    if newpath is not None and g % 2 == 1:
                        emit_rope(P, knb, t1[:, 2:4, :], t2[:, 2:4, :], rb, cs[0][:, blk, :], cs[1][:, blk, :], 2,
                                  [cs[2], cs[3]], rres, kres, eng="vector", tres=("t1u", "t2u"))
                    else:
                        emit_rope(P, knb, t1, t2, rb, cs[0][:, blk, :], cs[1][:, blk, :], 2,
                                  [cs[2], cs[3]], rres, kres)
                else:
                    P.op("vector", lambda e, rb=rb, knb=knb: e.tensor_tensor(
                        out=rb[:, 0:2, :], in0=knb[:, 0:2, :], in1=kg_sb[:].unsqueeze(1).broadcast_to([128, 2, 128]),
                        op=ALU.mult), reads=[kres, "kg"], writes=[rres])

                def trans(rb=rb, rres=rres, tb=tb, kpos=key0 + blk * 128):
                    for h in range(2):
                        P.op("tensor", lambda e, h=h: e.transpose(
                            bank_bf(tb)[:, h * 128:(h + 1) * 128], rb[:, h, :], ident_bf[:]),
                            reads=[rres, "identbf"], writes=[psr(tb)], signal=(h == 1))
                    P.op("vector", lambda e: e.tensor_copy(
                        out=KT[:, :, kpos:kpos + 128],
                        in_=bank_bf(tb)[:, 0:256].rearrange("p (h n) -> p h n", h=2)),
                        reads=[psr(tb)], writes=["KT"])

                flush_pending(1)
                pending.append(trans)

        xbf = [big[:, q * 4096:(q + 1) * 4096].bitcast(BF16).rearrange("p (k n) -> p k n", k=KC) for q in range(4)]
        kraw = [work[:, 0:512], work[:, 3456:3968]]
        bias_sb = work[:, 4864:5376]
        junkf = work[:, 3328:3456]
        shiftb = env["wg"][:].rearrange("p k (a n) -> p (k a) n", a=2)[:, 0:16, :]
        NT = 6
        wg_f = env["wg"][:].rearrange("p k n -> p (k n)").bitcast(F32)
        cosb2 = [wg_f[:, i * 1024:i * 1024 + 512].rearrange("p (b d) -> p b d", b=4) for i in range(2)]
        sinb2 = [wg_f[:, i * 1024 + 512:(i + 1) * 1024].rearrange("p (b d) -> p b d", b=4) for i in range(2)]
        CS2 = [("cos2", 0), ("sin2", 0), ("cos2", 1), ("sin2", 1)]

        def xq_res(q):
            return ("xq", q)

        def newpath_load(t):
            if t >= NT:
                return
            q = 2 + t % 2
            P.dma("gpsimd", f"xq{q}", xbf[q][:], xT_v[:, :, t * 512:(t + 1) * 512], writes=[xq_res(q)])

        def cs_load(t):
            if t >= NT:
                return
            tp = t % 2
            P.dma("sync", f"c2{tp}", cosb2[tp][:], cosF[t * 512:(t + 1) * 512, :].rearrange("(b p) d -> p b d", p=128),
                  writes=[("cos2", tp)])
            P.dma("sync", f"s2{tp}", sinb2[tp][:], sinF[t * 512:(t + 1) * 512, :].rearrange("(b p) d -> p b d", p=128),
                  writes=[("sin2", tp)])

        def _unused_cs_load(t):
            tp = t % 2
            P.dma("sync", f"c{tp}", cosb[tp][:], cosF[t * 512:(t + 1) * 512, :].rearrange("(b p) d -> p b d", p=128),
                  writes=[("cos", tp)])
            P.dma("sync", f"s{tp}", sinb[tp][:], sinF[t * 512:(t + 1) * 512, :].rearrange("(b p) d -> p b d", p=128),
                  writes=[("sin", tp)])

        def prep_newpath():
            for t in range(2):
                newpath_load(t)
            P.op("vector", lambda e: e.tensor_copy(
                out=shiftb, in_=m_sb[:, 0:16, 0:1].broadcast_to([128, 16, 128])),
                reads=[("m", 7)], writes=["wg"])
            for kc in range(KC):
                P.op("tensor", lambda e, kc=kc: e.matmul(bank(3), shiftb[:, kc, :], wbuf[0][:, kc, :],
                                                         start=(kc == 0), stop=(kc == KC - 1)),
                     reads=["wg", ("wbuf", 0)], writes=[psr(3)], signal=(kc == KC - 1))
            P.op("scalar", lambda e: e.activation(out=bias_sb, in_=bank(3), func=AF.Copy),
                 reads=[psr(3)], writes=["bias_sb"])
            P.inherit(CS2, ["wg"])
            cs_load(0)
            cs_load(1)
            P.inherit([("wkv2", kc) for kc in range(KC)], [("wbuf", 1)])
            for kc in range(KC):
                if kc % 2 == 0:
                    P.op("scalar", lambda e, kc=kc: e.activation(
                        out=wbuf[1][:, kc, :], in_=wbuf[0][:, kc, :], func=AF.Copy, scale=gm[:, kc, 0:1]),
                        reads=[("wbuf", 0), "gm"], writes=[("wkv2", kc)])
                else:
                    P.op("vector", lambda e, kc=kc: e.tensor_scalar(
                        out=wbuf[1][:, kc, :], in0=wbuf[0][:, kc, :], scalar1=gm[:, kc, 0:1], scalar2=None,
                        op0=ALU.mult), reads=[("wbuf", 0), "gm"], writes=[("wkv2", kc)])

        def gram(j):
            t, blk = divmod(j, 4)
            q = 2 + t % 2
            gb = 2
            jc = j % 8
            for kc in range(KC):
                P.op("tensor", lambda e, kc=kc: e.matmul(
                    bank(gb)[:, 0:128], xbf[q][:, kc, blk * 128:(blk + 1) * 128],
                    xbf[q][:, kc, blk * 128:(blk + 1) * 128], start=(kc == 0), stop=(kc == KC - 1)),
                    reads=[xq_res(q)], writes=[psr(gb)], signal=(kc == KC - 1))
            P.op("vector", lambda e: e.scalar_tensor_tensor(
                out=junkf, in0=bank(gb)[:, 0:128], scalar=1.0, in1=ident_f[:], op0=ALU.mult, op1=ALU.mult,
                accum_out=ssx[:, jc:jc + 1]),
                reads=[psr(gb), "identf"], writes=[("ssx", jc), "junkf"])
            P.op("scalar", lambda e: e.activation(out=rsx[:, jc:jc + 1], in_=ssx[:, jc:jc + 1], func=AF.Ln,
                                                  bias=eps_sb[:, 0:1], scale=1.0 / D),
                 reads=[("ssx", jc), "eps"], writes=[("rsx", jc)])
            P.op("scalar", lambda e: e.activation(out=rsx[:, jc:jc + 1], in_=rsx[:, jc:jc + 1], func=AF.Exp, scale=-0.5),
                 reads=[("rsx", jc)], writes=[("rsx", jc)])

        def wload_q(n):
            P.dma("gpsimd", f"xq{2 + n}", xbf[2 + n][:], w_in_v[:, :, n * 512:(n + 1) * 512], writes=[xq_res(2 + n)])

        def run_newpath(old_steps):
            nb = NT * 4
            gram(0)
            for j in range(nb):
                if j + 1 < nb:
                    gram(j + 1)
                t, blk = divmod(j, 4)
                q = 2 + t % 2
                kv_block(blk, None, None, True, CTX + t * 512, t % 2, newpath=(xbf[q], xq_res(q), j % 8),
                         cs=(cosb2[t % 2], sinb2[t % 2], ("cos2", t % 2), ("sin2", t % 2)))
                if blk == 3:
                    newpath_load(t + 2)
                    cs_load(t + 2)
                    if t >= NT - 2:
                        wload_q(t - (NT - 2))
                if old_steps:
                    old_steps.pop(0)()
            while old_steps:
                old_steps.pop(0)()

        prep_newpath()
        fa = [front_a(i) for i in range(4)]
        kb = [kv(i) for i in range(4)]
        F1, F2, F3 = 0, 1, 2
        CTXI, HALO, OWN0, OWN1 = 0, 1, 2, 3
        for st in (fa[CTXI][F1], fa[HALO][F1], fa[CTXI][F2], fa[CTXI][F3], fa[HALO][F2], fa[HALO][F3],
                   lambda: front_b(CTXI), lambda: front_b(HALO), kb[CTXI][0], kb[CTXI][1]):
            st()
        sched = {
            1: [lambda: fa[OWN0][F1](0)],
            2: [lambda: fa[OWN0][F1](1)],
            4: [fa[OWN0][F2]],
            5: [lambda: fa[OWN0][F3](0)],
            6: [lambda: fa[OWN0][F3](1)],
            7: [lambda: front_b(OWN0, 0)],
            8: [lambda: front_b(OWN0, 1)],
            9: [kb[OWN0][0]],
            10: [kb[OWN0][1]],
            11: [kb[OWN0][2], lambda: fa[OWN1][F1](0)],
            12: [kb[OWN0][3], lambda: fa[OWN1][F1](1)],
            14: [fa[OWN1][F2]],
            15: [lambda: fa[OWN1][F3](0)],
            16: [lambda: fa[OWN1][F3](1)],
            17: [lambda: front_b(OWN1, 0)],
            18: [lambda: front_b(OWN1, 1)],
            19: [kb[OWN1][0]],
            20: [kb[OWN1][1]],
            21: [kb[OWN1][2]],
            22: [kb[OWN1][3]],
        }
        old_steps = []
        for slot in range(24):
            steps = sched.get(slot, [])
            old_steps.append(lambda steps=steps: [s() for s in steps])
        run_newpath(old_steps)
        P.inherit(["wg"], CS2)

        chk("BC")
        P.inherit(["t1b", "t2b"], ["rstd_bc", ("kraw", 0), ("kraw", 1)])
        P.inherit([("rot", 3)], ["junkf", ("kraw", 1)])
        P.inherit(["t1", "t2"], ["t1u", "t2u"])
        P.inherit([("wbuf", 1)], [("wkv2", kc) for kc in range(KC)])
        P.inherit(["qT", "sga"], [("xbuf", 0), ("xq", 0), ("xq", 1)])
        P.inherit(["mixT"], [("xbuf", 1), ("xq", 2), ("xq", 3)])
        wsrc = [w_in_v[:, :, 0:512], w_in_v[:, :, 512:1024], w_in_v[:, :, 1536:2048], w_in_v[:, :, 2048:2560]] + \
               [w_conv_v[:, :, i * 512:(i + 1) * 512] for i in range(8)]

        def wtile(n):
            if n < 2:
                return xbf[2 + n], xq_res(2 + n), f"xq{2 + n}"
            return wbuf[n % 2], ("wbuf", n % 2), f"w{n % 2}"

        def wload(n):
            if n < len(wsrc):
                wb_, wres_, key_ = wtile(n)
                P.dma("gpsimd", key_, wb_[:], wsrc[n], writes=[wres_])

        wload(2)
        wload(3)
        for half in range(2):
            for blk in range(8):
                g = gblk[0]
                gblk[0] += 1
                qb = [2, 3, 4, 5, 0, 1][g % 6]
                tb = 6 + (g % 2)
                knb = kn2[g % 2]
                rb = rot_bf[g % 4]
                rres = ("rot", g % 4)
                kres = ("kn", g % 2)
                for kc in range(KC):
                    P.op("tensor", lambda e, kc=kc, blk=blk, qb=qb, half=half: e.matmul(
                        bank(qb), hT_own[:, kc, blk * 128:(blk + 1) * 128], xbf[2 + half][:, kc, :],
                        start=(kc == 0), stop=(kc == KC - 1)),
                        reads=[("hT", "A" if blk < 4 else "B", kc), xq_res(2 + half)], writes=[psr(qb)],
                        signal=(kc == KC - 1))
                if half == 0 and blk == 0:
                    flush_pending(0)
                for h in range(4):
                    P.op("scalar", lambda e, qb=qb, h=h: e.activation(
                        out=junk[:, h * 128:(h + 1) * 128], in_=bank(qb)[:, h * 128:(h + 1) * 128], func=AF.Square,
                        accum_out=ss[:, h:h + 1]),
                        reads=[psr(qb)], writes=["ss", "junk"])
                P.op("scalar", lambda e: e.activation(out=rstd_s[:, 0:4], in_=ss[:, 0:4], func=AF.Ln,
                                                      bias=eps_sb[:, 0:1], scale=1.0 / 128),
                     reads=["ss", "eps"], writes=["rstd_s"])
                P.op("scalar", lambda e: e.activation(out=rstd_s[:, 0:4], in_=rstd_s[:, 0:4], func=AF.Exp, scale=-0.5),
                     reads=["rstd_s"], writes=["rstd_s"])
                P.op("vector", lambda e, qb=qb, knb=knb: e.tensor_tensor(
                    out=knb[:], in0=bank(qb).rearrange("p (h n) -> p h n", h=4),
                    in1=rstd_s[:, 0:4].unsqueeze(2).broadcast_to([128, 4, 128]), op=ALU.mult),
                    reads=[psr(qb), "rstd_s"], writes=[kres])
                P.op("vector", lambda e, knb=knb: e.tensor_tensor(
                    out=knb[:], in0=knb[:], in1=qg_sb[:].unsqueeze(1).broadcast_to([128, 4, 128]), op=ALU.mult),
                    reads=[kres, "qg"], writes=[kres])
                tp = (6 + blk // 4) % 2
                if g % 2 == 0:
                    emit_rope(P, knb, t1, t2, rb, cosb[tp][:, blk % 4, :], sinb[tp][:, blk % 4, :], 4,
                              [("cos", tp), ("sin", tp)], rres, kres)
                else:
                    emit_rope(P, knb, t1b, t2b, rb, cosb[tp][:, blk % 4, :], sinb[tp][:, blk % 4, :], 4,
                              [("cos", tp), ("sin", tp)], rres, kres, eng="vector", tres=("t1b", "t2b"))

                def transq(rb=rb, rres=rres, tb=tb, half=half, blk=blk):
                    for h in range(4):
                        P.op("tensor", lambda e, h=h: e.transpose(
                            bank_bf(tb)[:, h * 128:(h + 1) * 128], rb[:, h, :], ident_bf[:]),
                            reads=[rres, "identbf"], writes=[psr(tb)], signal=(h == 3))
                    P.op("vector", lambda e: e.tensor_copy(
                        out=qT[:, half * 4:(half + 1) * 4, blk * 128:(blk + 1) * 128],
                        in_=bank_bf(tb)[:, 0:512].rearrange("p (h n) -> p h n", h=4)),
                        reads=[psr(tb)], writes=["qT"])

                flush_pending(2)
                pending.append(transq)

        chk("D1")
        bctr = [0]

        def next_bank():
            b = bctr[0] % 7
            bctr[0] += 1
            return b

        pv7 = bank(7)[:, 0:96].rearrange("p (c r) -> p c r", r=2)

        def gate_dma(n):
            if n < 8:
                P.dma("gpsimd", "wg", wg[:], w_mod_v[:, :, 4096 + n * 256:4096 + (n + 1) * 256], writes=["wg"])

        def gate_mm(n):
            for c2 in range(2):
                cidx = 32 + n * 2 + c2
                for kc in range(KC):
                    P.op("tensor", lambda e, c2=c2, cidx=cidx, kc=kc: e.matmul(
                        pv7[:, cidx, :], wg[:, kc, c2 * 128:(c2 + 1) * 128], s_bf[:, kc, :],
                        start=(kc == 0), stop=(kc == KC - 1)),
                        reads=["wg", "s_bf"], writes=[psr(7)], signal=(c2 == 1 and kc == KC - 1))
            if n == 7:
                adaln_evac(pv7, 7, 32, 48, 11)

        for grp in range(2):
            wb = wbuf[grp % 2]
            if grp == 0:
                gate_dma(0)
            else:
                gate_mm(0)
                gate_dma(1)
            for c4 in range(4):
                head = grp * 4 + c4
                for tt in range(2):
                    b = next_bank()
                    for kc in range(KC):
                        P.op("tensor", lambda e, wb=wb, kc=kc, c4=c4, tt=tt, b=b: e.matmul(
                            bank(b), wb[:, kc, c4 * 128:(c4 + 1) * 128], hT_own[:, kc, tt * 512:(tt + 1) * 512],
                            start=(kc == 0), stop=(kc == KC - 1)),
                            reads=[("hT", "AB"[tt], kc), ("wbuf", grp % 2)], writes=[psr(b)], signal=(kc == KC - 1))
                    P.op("scalar", lambda e, b=b, head=head, tt=tt: e.activation(
                        out=sga[:, head, tt * 512:(tt + 1) * 512], in_=bank(b), func=AF.Silu),
                        reads=[psr(b)], writes=["sga"])
                if grp == 0 and c4 == 0:
                    flush_pending(0)
            wload(grp + 4)

        chk("D2")
        W_BC = ["rstd_bc", ("kn", 0), ("kn", 1), "t1", "t2", ("rot", 0), ("rot", 1), ("rot", 2), ("rot", 3), "junk", "t1u", "t2u",
                "t1b", "t2b", ("kraw", 0), ("kraw", 1), "bias_sb", "junkf"]
        W_D3 = ["cg", "u", "acc_c", "acc2", "sgc"]
        P.inherit(W_D3, W_BC)
        for i in range(8):
            wpar = i % 2
            wb4 = wbuf[wpar].rearrange("p k (g n) -> p k g n", g=4)
            if i < 7:
                gate_mm(i + 1)
                gate_dma(i + 2)
            banks = {}
            bh = next_bank()
            for g in (1, 2, 0, 3):
                for tt in range(2):
                    b = next_bank()
                    banks[(g, tt)] = b
                    for kc in range(KC):
                        P.op("tensor", lambda e, wb4=wb4, kc=kc, g=g, tt=tt, b=b: e.matmul(
                            bank(b), wb4[:, kc, g, :], hT_own[:, kc, tt * 512:(tt + 1) * 512],
                            start=(kc == 0), stop=(kc == KC - 1)),
                            reads=[("hT", "AB"[tt], kc), ("wbuf", wpar)], writes=[psr(b)], signal=(kc == KC - 1))
                        if g in (1, 2) and tt == 1:
                            off = 0 if g == 1 else 2
                            P.op("tensor", lambda e, wb4=wb4, kc=kc, g=g, off=off, bh=bh: e.matmul(
                                bank(bh)[:, off:off + 2], wb4[:, kc, g, :], hH[:, kc, :],
                                start=(kc == 0), stop=(kc == KC - 1)),
                                reads=[("hT", "H", kc), ("wbuf", wpar)], writes=[psr(bh)], signal=(kc == KC - 1))
                    if g == 1:
                        P.op("scalar", lambda e, b=b, tt=tt: e.activation(
                            out=cg_sb[:, 1 + tt * 512:1 + (tt + 1) * 512], in_=bank(b), func=AF.Copy),
                            reads=[psr(b)], writes=["cg"])
                    elif g == 2:
                        P.op("vector", lambda e, b=b, tt=tt: e.tensor_tensor(
                            out=u_sb[:, 1 + tt * 512:1 + (tt + 1) * 512], in0=bank(b),
                            in1=cg_sb[:, 1 + tt * 512:1 + (tt + 1) * 512], op=ALU.mult),
                            reads=[psr(b), "cg"], writes=["u"])
                    elif g == 3:
                        P.op("scalar", lambda e, b=b, tt=tt: e.activation(
                            out=sgc[:, tt * 512:(tt + 1) * 512], in_=bank(b), func=AF.Silu),
                            reads=[psr(b)], writes=["sgc"])
                if g == 2:
                    P.op("scalar", lambda e, bh=bh: e.activation(
                        out=cg_sb[:, 0:1026:1025], in_=bank(bh)[:, 0:2], func=AF.Copy),
                        reads=[psr(bh)], writes=["cg"])
                    P.op("vector", lambda e, bh=bh: e.tensor_tensor(
                        out=u_sb[:, 0:1026:1025], in0=bank(bh)[:, 2:4], in1=cg_sb[:, 0:1026:1025], op=ALU.mult),
                        reads=[psr(bh), "cg"], writes=["u"])
                    P.op("vector", lambda e: e.tensor_tensor(
                        out=u_sb[:, 0:1026:1025], in0=u_sb[:, 0:1026:1025], in1=hmask_sb[:], op=ALU.mult),
                        reads=["u", "hmask"], writes=["u"])
                    cw = convw_sb[:, i * 3:(i + 1) * 3]
                    P.op("vector", lambda e, cw=cw: e.tensor_scalar(
                        out=acc_c, in0=u_sb[:, 1:1025], scalar1=cw[:, 1:2], scalar2=None, op0=ALU.mult),
                        reads=["u", "convw"], writes=["acc_c"])
                    P.op("vector", lambda e, cw=cw: e.scalar_tensor_tensor(
                        out=acc_c, in0=u_sb[:, 0:1024], scalar=cw[:, 0:1], in1=acc_c, op0=ALU.mult, op1=ALU.add),
                        reads=["u", "convw", "acc_c"], writes=["acc_c"])
                    P.op("vector", lambda e, cw=cw: e.scalar_tensor_tensor(
                        out=acc_c, in0=u_sb[:, 2:1026], scalar=cw[:, 2:3], in1=acc_c, op0=ALU.mult, op1=ALU.add),
                        reads=["u", "convw", "acc_c"], writes=["acc_c"])
                if g == 0:
                    for tt in range(2):
                        b = banks[(0, tt)]
                        P.op("vector", lambda e, b=b, tt=tt: e.tensor_tensor(
                            out=acc2[:, tt * 512:(tt + 1) * 512], in0=bank(b), in1=acc_c[:, tt * 512:(tt + 1) * 512],
                            op=ALU.mult), reads=[psr(b), "acc_c"], writes=["acc2"])
            wload(i + 6)
            P.op("gpsimd", lambda e, i=i: e.tensor_tensor(
                out=mixT[:, 8 + i, :], in0=acc2, in1=sgc, op=ALU.mult),
                reads=["acc2", "sgc"], writes=["mixT"])

        chk("D3")
        W_E = [("PT", 0), ("PT", 1), ("tmpb", 0), ("tmpb", 1), "rs", "ot", "lns", "o_sb"]
        P.inherit(W_E, W_D3)
        for c in range(4):
            P.dma("gpsimd", f"wout{c}", woutb[c][:], w_out_v[:, :, c * 512:(c + 1) * 512],
                  writes=R2_res + [("wout", c)])

        sm_scale = 1.0 / math.sqrt(128.0)
        iters = [(g, tt, hq) for g in range(2) for tt in range(2) for hq in range(4)]
        items = [(it, p) for it in range(len(iters)) for p in range(NCHUNK // 2)]

        def emit_S(idx):
            it, p = items[idx]
            g, tt, hq = iters[it]
            head = g * 4 + hq
            sp = idx % 3
            for cl in range(2):
                chunk = p * 2 + cl
                b = sp * 2 + cl
                P.op("tensor", lambda e, b=b, g=g, chunk=chunk, head=head, tt=tt: e.matmul(
                    bank(b), KT[:, g, chunk * 128:(chunk + 1) * 128], qT[:, head, tt * 512:(tt + 1) * 512],
                    start=True, stop=True),
                    reads=["KT", "qT"], writes=[psr(b)], signal=(cl == 1))

        def emit_norm(it):
            g, tt, hq = iters[it]
            head = g * 4 + hq
            ob = 6
            sb_ = 7
            P.op("scalar", lambda e: e.activation(out=lns_sb, in_=bank(sb_), func=AF.Ln), reads=[psr(sb_)], writes=["lns"])
            P.op("vector", lambda e: e.tensor_copy(out=o_sb, in_=bank(ob)), reads=[psr(ob)], writes=["o_sb"])
            P.op("scalar", lambda e: e.activation(out=rs_sb, in_=lns_sb, func=AF.Exp, scale=-1.0),
                 reads=["lns"], writes=["rs"])
            P.op("vector", lambda e: e.tensor_tensor(out=ot_sb, in0=o_sb, in1=rs_sb, op=ALU.mult),
                 reads=["o_sb", "rs"], writes=["ot"])
            P.op("gpsimd", lambda e: e.tensor_tensor(
                out=mixT[:, head, tt * 512:(tt + 1) * 512], in0=ot_sb, in1=sga[:, head, tt * 512:(tt + 1) * 512],
                op=ALU.mult), reads=["ot", "sga"], writes=["mixT"])

        def emit_summm(idx):
            it, p = items[idx]
            tq = idx % 2
            sb_ = 7
            P.op("tensor", lambda e: e.matmul(bank(sb_), ones_bf[:], tmpb[tq], start=(p == 0), stop=(p == NP - 1)),
                 reads=[("tmpb", tq), "ones_bf"], writes=[psr(sb_)])

        NP = NCHUNK // 2
        emit_S(0)
        emit_S(1)
        for idx, (it, p) in enumerate(items):
            g, tt, hq = iters[it]
            sp = idx % 3
            pp = idx % 2
            if idx + 2 < len(items):
                emit_S(idx + 2)
            P.op("scalar", lambda e, sp=sp, pp=pp: e.activation(out=PT[pp], in_=PS[sp][:], func=AF.Exp, scale=sm_scale,
                                                                bias=smx[:, 2:3]),
                 reads=[psr(sp * 2), psr(sp * 2 + 1), "smx2"], writes=[("PT", pp)])
            ob = 6
            if idx > 0:
                emit_summm(idx - 1)
                if items[idx - 1][1] == NP - 1:
                    emit_norm(items[idx - 1][0])
            for cl in range(2):
                chunk = p * 2 + cl
                P.op("tensor", lambda e, ob=ob, chunk=chunk, g=g, pp=pp, cl=cl: e.matmul(
                    bank(ob), V[:, chunk, g * 128:(g + 1) * 128], PT[pp][:, cl * 512:(cl + 1) * 512],
                    start=(chunk == 0), stop=(chunk == NCHUNK - 1)),
                    reads=["V", ("PT", pp)], writes=[psr(ob)], signal=(cl == 1))
            tq = idx % 2
            P.op("vector", lambda e, pp=pp, tq=tq: e.tensor_tensor(
                out=tmpb[tq], in0=PT[pp][:, 0:512], in1=PT[pp][:, 512:1024], op=ALU.add),
                reads=[("PT", pp)], writes=[("tmpb", tq)])
        emit_summm(len(items) - 1)
        emit_norm(len(iters) - 1)

        chk("E")
        W_F = ["fg", "gate_bc", ("diag", 0), ("diag", 1)]
        P.inherit(W_F, W_E)
        P.inherit([("xo", 0), ("xo", 1)], ["qT"])
        P.inherit([("res", 0), ("res", 1)], ["sga"])
        P.dma("sync", "fgk", fg_sb, fg, writes=["fg"])
        for blk0 in range(2):
            P.dma("sync", f"xo{blk0}", xo[blk0], xown[blk0 * 128:(blk0 + 1) * 128, :], writes=[("xo", blk0)])
        def build_gate_bc():
            for q4 in range(4):
                db = diagb[q4 % 2]
                for a in range(4):
                    kc = q4 * 4 + a
                    P.op("vector", lambda e, db=db, a=a, kc=kc: e.tensor_scalar(
                        out=db[:, a, :], in0=ident_f[:], scalar1=m_sb[:, 32 + kc, 0:1], scalar2=None, op0=ALU.mult),
                        reads=["identf", ("m", 11)], writes=[("diag", q4 % 2)])
                for a in range(4):
                    P.op("tensor", lambda e, db=db, a=a: e.matmul(
                        bank(7)[:, a * 128:(a + 1) * 128], ones_f[:], db[:, a, :], start=True, stop=True),
                        reads=[("diag", q4 % 2), "ones_f"], writes=[psr(7)], signal=(a == 3))
                P.op("scalar", lambda e, q4=q4: e.activation(out=gate_bc[:, q4 * 512:(q4 + 1) * 512], in_=bank(7), func=AF.Copy),
                     reads=[psr(7)], writes=["gate_bc"])


        for blk in range(8):
            par = blk % 2
            for c in range(4):
                b = par * 4 + c
                for kc in range(KC):
                    P.op("tensor", lambda e, b=b, kc=kc, blk=blk, c=c: e.matmul(
                        bank(b), mixT[:, kc, blk * 128:(blk + 1) * 128], woutb[c][:, kc, :],
                        start=(kc == 0), stop=(kc == KC - 1)),
                        reads=["mixT", ("wout", c)], writes=[psr(b)], signal=(kc == KC - 1))
            rsb = resb[par]
            if blk == 0:
                build_gate_bc()
            for hh in range(2):
                P.op("vector", lambda e, rsb=rsb, par=par, hh=hh: e.tensor_tensor(
                    out=rsb[:, hh * 1024:(hh + 1) * 1024], in0=PS[par * 2 + hh][:],
                    in1=gate_bc[:, hh * 1024:(hh + 1) * 1024], op=ALU.mult),
                    reads=[psr(par * 4 + hh * 2), psr(par * 4 + hh * 2 + 1), "gate_bc"], writes=[("res", par)])
            P.op("vector", lambda e, rsb=rsb, par=par: e.tensor_tensor(out=rsb, in0=rsb, in1=xo[par], op=ALU.add),
                 reads=[("res", par), ("xo", par)], writes=[("res", par)])
            P.op("scalar", lambda e, rsb=rsb, par=par: e.activation(out=xo[par], in_=rsb, func=AF.Square,
                                                                   accum_out=ss[:, 0:1]),
                 reads=[("res", par)], writes=["ss", ("xo", par)])
            if blk + 2 < 8:
                P.dma("sync", f"xo{par}", xo[par], xown[(blk + 2) * 128:(blk + 3) * 128, :], writes=[("xo", par)])
            P.op("scalar", lambda e: e.activation(out=rstd_s[:, 0:1], in_=ss[:, 0:1], func=AF.Ln,
                                                  bias=eps_sb[:, 0:1], scale=1.0 / D),
                 reads=["ss", "eps"], writes=["rstd_s"])
            P.op("scalar", lambda e: e.activation(out=rstd_s[:, 0:1], in_=rstd_s[:, 0:1], func=AF.Exp, scale=-0.5),
                 reads=["rstd_s"], writes=["rstd_s"])
            P.op("vector", lambda e, rsb=rsb: e.scalar_tensor_tensor(
                out=rsb, in0=rsb, scalar=rstd_s[:, 0:1], in1=fg_sb, op0=ALU.mult, op1=ALU.mult),
                reads=[("res", par), "rstd_s", "fg"], writes=[("res", par)])
            P.dma("sync", f"o{par}", out[blk * 128:(blk + 1) * 128, :], rsb, reads=[("res", par)], writes=[])
        P.final_wait("sync", ["o0", "o1"])


def emit_rope(P, kn, t1, t2, rb, cos_blk, sin_blk, H, tab_res, rot_res, kn_res="kn", eng="gpsimd", tres=("t1", "t2")):
    knv = kn[:, 0:H, :].rearrange("p h (a b j) -> p h a b j", a=2, b=2)
    t2v = t2[:, 0:H, :].rearrange("p h (a b j) -> p h a b j", a=2, b=2)
    sv = sin_blk.rearrange("p (a b j) -> p a b j", a=2, b=2)
    P.op(eng, lambda e: e.tensor_tensor(
        out=t1[:, 0:H, :], in0=kn[:, 0:H, :], in1=cos_blk.unsqueeze(1).broadcast_to([128, H, 128]), op=ALU.mult),
        reads=[kn_res] + tab_res, writes=[tres[0]])
    for bsel in range(2):
        P.op(eng, lambda e, bsel=bsel: e.tensor_tensor(
            out=t2v[:, :, :, bsel, :], in0=knv[:, :, :, 1 - bsel, :],
            in1=sv[:, :, bsel, :].unsqueeze(1).broadcast_to([128, H, 2, 32]), op=ALU.mult),
            reads=[kn_res] + tab_res, writes=[tres[1]])
    P.op(eng, lambda e: e.tensor_tensor(out=rb[:, 0:H, :], in0=t1[:, 0:H, :], in1=t2[:, 0:H, :], op=ALU.add),
         reads=list(tres), writes=[rot_res])


_NC_CACHE = {}


def _rope_tables():
    quarter = 32
    inv = (10000.0 ** (-np.arange(quarter, dtype=np.float32) / quarter)).astype(np.float32)
    t = np.arange(SEQ)
    row = (t // 64).astype(np.float32)
    col = (t % 64).astype(np.float32)
    ar = row[:, None] * inv[None, :]
    ac = col[:, None] * inv[None, :]
    cr, sr, c_c, s_c = np.cos(ar), np.sin(ar), np.cos(ac), np.sin(ac)
    cosF = np.concatenate([cr, cr, c_c, c_c], axis=1).astype(np.float32)
    sinF = np.concatenate([-sr, sr, -s_c, s_c], axis=1).astype(np.float32)
    return cosF, sinF


def kernel(x, c, ctx, c_ctx, w_mod, b_mod, norm_g, w_in, q_norm_g, k_norm_g, conv_w, w_out, final_norm_g):
    f = lambda a: np.ascontiguousarray(np.asarray(a, dtype=np.float32))
    x, c, ctx, c_ctx = f(x), f(c), f(ctx), f(c_ctx)
    w_mod0, w_in0, w_out0 = f(w_mod[0]), f(w_in[0]), f(w_out[0])
    b_mod0, ng0, qg0, kg0, cw0, fg0 = f(b_mod[0]), f(norm_g[0]), f(q_norm_g[0]), f(k_norm_g[0]), f(conv_w[0]), f(final_norm_g)
    if "nc" not in _NC_CACHE:
        _NC_CACHE["nc"] = build_program()
    nc = _NC_CACHE["nc"]
    cosF, sinF = _rope_tables()
    shared = {
        "w_mod": w_mod0, "w_in": np.ascontiguousarray(w_in0[:, :2560]), "w_out": w_out0,
        "w_conv": np.ascontiguousarray(
            w_in0[:, 2560:].reshape(D, 4, 8, 128).transpose(0, 2, 1, 3).reshape(D, 4096)),
        "bmod": np.ascontiguousarray(b_mod0.reshape(48, 128).T),
        "ng": np.ascontiguousarray(ng0.reshape(16, 128).T),
        "qg": np.ascontiguousarray(np.broadcast_to(qg0[None, :], (128, 128))),
        "kg": np.ascontiguousarray(np.broadcast_to(kg0[None, :], (128, 128))),
        "convw": np.ascontiguousarray(cw0.reshape(3, 8, 128).transpose(2, 1, 0).reshape(128, 24)),
        "fg": np.ascontiguousarray(np.broadcast_to(fg0[None, :], (128, D))),
        "ident": np.eye(128, dtype=np.float32),
    }
    in_maps = []
    for core in range(8):
        b, j = core // 4, core % 4
        t0 = j * OWN
        order = np.concatenate([np.arange(0, t0), np.arange(t0 + OWN, SEQ), np.arange(t0, t0 + OWN)])
        xb = x[b]
        xTc = np.ascontiguousarray(xb[order].T)
        halo = np.zeros((2, D), np.float32)
        hm = np.zeros((128, 2), np.float32)
        if t0 > 0:
            halo[0] = xb[t0 - 1]
            hm[:, 0] = 1.0
        if t0 + OWN < SEQ:
            halo[1] = xb[t0 + OWN]
            hm[:, 1] = 1.0
        ccm = np.stack([c[b].reshape(16, 128).T, c_ctx.reshape(16, 128).T], axis=2).reshape(128, 32)
        m = dict(shared)
        m.update({
            "xT": xTc, "xTh": np.ascontiguousarray(halo.T), "xown": np.ascontiguousarray(xb[t0:t0 + OWN]),
            "ctxT": np.ascontiguousarray(ctx[b].T), "cc": np.ascontiguousarray(ccm),
            "cosF": np.ascontiguousarray(cosF[order]), "sinF": np.ascontiguousarray(sinF[order]), "hmask": hm,
        })
        in_maps.append(m)
    if _NC_CACHE.get("prep_only"):
        return nc, in_maps
    res = run_bass_kernel_spmd(nc, in_maps, core_ids=list(range(8)))
    outp = np.empty((2, SEQ, D), np.float32)
    for core in range(8):
        b, j = core // 4, core % 4
        outp[b, j * OWN:(j + 1) * OWN] = res.results[core]["out"]
    return outp
```

```python
import math
from contextlib import ExitStack

import numpy as np
import concourse.bass as bass
import concourse.mybir as mybir
from concourse.bass_utils import run_bass_kernel_spmd

F32 = mybir.dt.float32
BF16 = mybir.dt.bfloat16
AF = mybir.ActivationFunctionType
ALU = mybir.AluOpType

D = 2048
KC = 16
SEQ = 4096
CTX = 256
OWN = 1024
NKEY = CTX + SEQ
NCHUNK = NKEY // 128
EPS = 1e-6
ENGS = ["tensor", "scalar", "vector", "gpsimd", "sync"]


class Prog:
    def __init__(self, nc, es):
        self.nc = nc
        self.es = es
        self.streams = {e: [] for e in ENGS}
        self.esem = {e: es.enter_context(nc.semaphore("p_" + e)) for e in ENGS[:4]}
        self.ecount = {e: 0 for e in ENGS[:4]}
        self.dsem = {}
        self.dcount = {}
        self.lastw = {}
        self.readers = {}
        self.waited = {e: {} for e in ENGS}

    def _sem(self, key):
        if key[0] == "e":
            return self.esem[key[1]]
        return self.dsem[key[1]]

    def _waits(self, stream, reads, writes, skip=None):
        need = {}

        def add(tok):
            k, v = tok
            if need.get(k, 0) < v:
                need[k] = v

        for r in reads:
            if r in self.lastw:
                add(self.lastw[r])
        for w in writes:
            if w in self.lastw:
                add(self.lastw[w])
            for k, v in self.readers.get(w, {}).items():
                add((k, v))
        for k, v in need.items():
            if k == skip or k == ("e", "tensor") == ("e", stream):
                continue
            if k[0] == "e":
                assert self.ecount[k[1]] >= v, f"dependency on unsignaled op {k} {v} {self.ecount[k[1]]}"
            if self.waited[stream].get(k, 0) >= v:
                continue
            self.waited[stream][k] = v
            sem = self._sem(k)
            self.streams[stream].append(lambda e, sem=sem, v=v: e.wait_ge(sem, v))

    def _record(self, tok, reads, writes):
        k, v = tok
        for r in reads:
            d = self.readers.setdefault(r, {})
            if d.get(k, 0) < v:
                d[k] = v
        for w in writes:
            self.lastw[w] = tok
            self.readers[w] = {}

    def op(self, eng, fn, reads=(), writes=(), signal=True):
        ps_reads = [r for r in reads if isinstance(r, tuple) and r[0] == "ps"]
        if ps_reads:
            reads = [r for r in reads if r not in ps_reads]
            writes = list(writes) + ps_reads
        self._waits(eng, reads, writes)
        if signal:
            self.ecount[eng] += 1
            tok = (("e", eng), self.ecount[eng])
            sem = self.esem[eng]
            self.streams[eng].append(lambda e, fn=fn, sem=sem: fn(e).then_inc(sem, 1))
        else:
            tok = (("e", eng), self.ecount[eng] + 1)
            self.streams[eng].append(lambda e, fn=fn: fn(e))
        self._record(tok, reads, writes)

    def dma(self, queue, key, out, in_, reads=(), writes=(), cont=False):
        if key not in self.dsem:
            self.dsem[key] = self.es.enter_context(self.nc.semaphore("d_" + key))
            self.dcount[key] = 0
        self._waits(queue, reads, writes, skip=("d", key) if cont else None)
        self.dcount[key] += 16
        tok = (("d", key), self.dcount[key])
        sem = self.dsem[key]
        self.streams[queue].append(
            lambda e, out=out, in_=in_, sem=sem: e.dma_start(out=out, in_=in_).then_inc(sem, 16))
        self._record(tok, reads, writes)

    def inherit(self, news, olds):
        for n in news:
            d = self.readers.setdefault(n, {})
            for o in olds:
                for k, v in self.readers.get(o, {}).items():
                    if d.get(k, 0) < v:
                        d[k] = v
                if o in self.lastw:
                    k, v = self.lastw[o]
                    if d.get(k, 0) < v:
                        d[k] = v

    def final_wait(self, stream, keys):
        for key in keys:
            sem = self.dsem[key]
            v = self.dcount[key]
            self.streams[stream].append(lambda e, sem=sem, v=v: e.wait_ge(sem, v))

    def emit(self, block):
        for eng in ENGS:
            thunks = self.streams[eng]
            if not thunks:
                continue

            def body(e, thunks=thunks):
                for t in thunks:
                    t(e)

            getattr(block, eng)(body)


class _Stop(Exception):
    pass


def build_program(stop=None):
    nc = bass.Bass("TRN2", target_bir_lowering=False)

    def din(name, shape):
        return nc.dram_tensor(name, shape, F32, kind="ExternalInput").ap()

    xT = din("xT", [D, SEQ])
    xTh = din("xTh", [D, 2])
    xown = din("xown", [OWN, D])
    ctxT = din("ctxT", [D, CTX])
    cc = din("cc", [128, 32])
    w_mod = din("w_mod", [D, 3 * D])
    bmod = din("bmod", [128, 48])
    ng = din("ng", [128, 16])
    w_in = din("w_in", [D, 2560])
    w_conv = din("w_conv", [D, 4096])
    qg = din("qg", [128, 128])
    kg = din("kg", [128, 128])
    convw = din("convw", [128, 24])
    w_out = din("w_out", [D, D])
    fg = din("fg", [128, D])
    cosF = din("cosF", [SEQ, 128])
    sinF = din("sinF", [SEQ, 128])
    ident = din("ident", [128, 128])
    hmask = din("hmask", [128, 2])
    out = nc.dram_tensor("out", [OWN, D], F32, kind="ExternalOutput").ap()

    xT_v = xT.rearrange("(k p) n -> p k n", p=128)
    xTh_v = xTh.rearrange("(k p) n -> p k n", p=128)
    ctxT_v = ctxT.rearrange("(k p) n -> p k n", p=128)
    w_mod_v = w_mod.rearrange("(k p) n -> p k n", p=128)
    w_in_v = w_in.rearrange("(k p) n -> p k n", p=128)
    w_conv_v = w_conv.rearrange("(k p) n -> p k n", p=128)
    w_out_v = w_out.rearrange("(k p) n -> p k n", p=128)

    with ExitStack() as es:
        def sb(name, shape, dt):
            return es.enter_context(nc.sbuf_tensor(name, shape, dt))

        big = sb("big", [128, 16384], F32)
        R2 = sb("R2", [128, 16384], F32)
        hHt = sb("hHt", [128, 16, 2], BF16)
        wg = sb("wg", [128, 16, 256], BF16)
        xh = sb("xh", [128, 16, 2], F32)
        KT = sb("KT", [128, 2, NKEY], BF16)
        V = sb("V", [128, NCHUNK, 256], BF16)
        work = sb("work", [128, 5376], F32)
        ident_bf = sb("ident_bf", [128, 128], BF16)
        ident_f = sb("ident_f", [128, 128], F32)
        ones_bf = sb("ones_bf", [128, 128], BF16)
        ones_f = sb("ones_f", [128, 128], F32)
        qg_sb = sb("qg_sb", [128, 128], F32)
        kg_sb = sb("kg_sb", [128, 128], F32)
        ng_sb = sb("ng_sb", [128, 16], F32)
        bmod_sb = sb("bmod_sb", [128, 48], F32)
        convw_sb = sb("convw_sb", [128, 24], F32)
        hmask_sb = sb("hmask_sb", [128, 2], F32)
        cc_sb = sb("cc_sb", [128, 32], F32)
        s_bf = sb("s_bf", [128, 16, 2], BF16)
        m_sb = sb("m_sb", [128, 48, 2], F32)
        gm = sb("gm", [128, 16, 2], F32)
        eps_sb = sb("eps_sb", [128, 1], F32)
        ss = sb("ss", [128, 8], F32)
        rstd_s = sb("rstd_s", [128, 8], F32)
        ssx = sb("ssx", [128, 8], F32)
        rsx = sb("rsx", [128, 8], F32)
        smx = sb("smx", [128, 4], F32)
        cosb = [sb(f"cosb{i}", [128, 4, 128], F32) for i in range(2)]
        sinb = [sb(f"sinb{i}", [128, 4, 128], F32) for i in range(2)]
        PS = [es.enter_context(nc.psum_tensor(f"PS{i}", [128, 1024], F32)) for i in range(4)]

        P = Prog(nc, es)
        block = es.enter_context(nc.Block())
        try:
            _body(locals(), stop)
        except _Stop:
            pass
        P.emit(block)
    return nc


def _body(env, stop):
        globals().update({})
        hHt = env["hHt"]
        ssx = env["ssx"]
        xh = env["xh"]
        rsx = env["rsx"]
        smx = env["smx"]
        wg = env["wg"]
        (nc, es, P, block, big, R2, KT, V, work, ident_bf, ident_f, ones_bf, ones_f, qg_sb, kg_sb, ng_sb, bmod_sb,
         convw_sb, hmask_sb, cc_sb, s_bf, m_sb, gm, eps_sb, ss, rstd_s, cosb, sinb, PS) = [env[k] for k in (
            "nc", "es", "P", "block", "big", "R2", "KT", "V", "work", "ident_bf", "ident_f", "ones_bf", "ones_f",
            "qg_sb", "kg_sb", "ng_sb", "bmod_sb", "convw_sb", "hmask_sb", "cc_sb", "s_bf", "m_sb", "gm", "eps_sb",
            "ss", "rstd_s", "cosb", "sinb", "PS")]
        w_conv_v = env["w_conv_v"]
        (xT_v, xTh_v, ctxT_v, w_mod_v, w_in_v, w_out_v, xown, cc, bmod, ng, qg, kg, convw, fg, cosF, sinF, ident,
         hmask, out) = [env[k] for k in (
            "xT_v", "xTh_v", "ctxT_v", "w_mod_v", "w_in_v", "w_out_v", "xown", "cc", "bmod", "ng", "qg", "kg",
            "convw", "fg", "cosF", "sinF", "ident", "hmask", "out")]

        def chk(name):
            if stop == name:
                raise _Stop()

        def bank(b):
            return PS[b // 2][:, (b % 2) * 512:(b % 2) * 512 + 512]

        def bank_bf(b):
            return bank(b).bitcast(BF16)

        def psr(b):
            return ("ps", b)

        xbuf = [big[:, i * 8192:(i + 1) * 8192].rearrange("p (k n) -> p k n", k=KC) for i in range(2)]
        qT = big[:, 0:4096].bitcast(BF16).rearrange("p (h n) -> p h n", h=8)
        sga = big[:, 4096:8192].bitcast(BF16).rearrange("p (h n) -> p h n", h=8)
        mixT = big[:, 8192:16384].bitcast(BF16).rearrange("p (k n) -> p k n", k=KC)
        hT_own = R2[:, 0:8192].bitcast(BF16).rearrange("p (k n) -> p k n", k=KC)
        hA = hT_own[:, :, 0:512]
        hB = hT_own[:, :, 512:1024]
        hH = hHt[:]
        wbuf = [R2[:, 8192 + i * 4096:8192 + (i + 1) * 4096].bitcast(BF16).rearrange("p (k n) -> p k n", k=KC)
                for i in range(2)]
        woutb = [R2[:, c * 4096:(c + 1) * 4096].bitcast(BF16).rearrange("p (k n) -> p k n", k=KC)
                 for c in range(4)]
        R2_res = [("hT", n, k) for n in "AB" for k in range(KC)] + [("wbuf", 0), ("wbuf", 1)]

        rstd_bc = work[:, 512:1024]
        kn = work[:, 1024:1536].rearrange("p (h n) -> p h n", h=4)
        t1 = work[:, 1536:2048].rearrange("p (h n) -> p h n", h=4)
        t2 = work[:, 2048:2560].rearrange("p (h n) -> p h n", h=4)
        rot_bf = [work[:, o:o + 256].bitcast(BF16).rearrange("p (h n) -> p h n", h=4)
                  for o in (2560, 2816, 4608, 3328)]
        junk = work[:, 3072:3328].bitcast(BF16)
        t1b = work[:, 0:512].rearrange("p (h n) -> p h n", h=4)
        t2b = work[:, 512:1024].rearrange("p (h n) -> p h n", h=4)
        cg_sb = work[:, 0:1026]
        u_sb = work[:, 1026:2052]
        acc_c = work[:, 2052:3076]
        acc2 = work[:, 3076:4100]
        sgc = work[:, 4100:5124]
        PT = [work[:, i * 512:(i + 1) * 512].bitcast(BF16) for i in range(2)]
        tmpb = [work[:, 1024 + i * 256:1024 + (i + 1) * 256].bitcast(BF16) for i in range(2)]
        accs = [work[:, 1536 + i * 512:1536 + (i + 1) * 512] for i in range(4)]
        lns_sb = work[:, 1536:2048]
        o_sb = work[:, 2048:2560]
        rs_sb = work[:, 3584:4096]
        ot_sb = work[:, 4096:4608]
        fg_sb = work[:, 0:2048]
        gate_bc = work[:, 2048:4096]
        diagb = [work[:, 4096 + i * 512:4096 + (i + 1) * 512].rearrange("p (a n) -> p a n", a=4) for i in range(2)]
        xo = [big[:, i * 2048:(i + 1) * 2048] for i in range(2)]
        resb = [big[:, 4096 + i * 2048:4096 + (i + 1) * 2048] for i in range(2)]

        for name, dst, src in [("cc", cc_sb, cc), ("bmod", bmod_sb, bmod), ("ng", ng_sb, ng), ("qg", qg_sb, qg),
                               ("kg", kg_sb, kg), ("convw", convw_sb, convw), ("hmask", hmask_sb, hmask),
                               ("identf", ident_f, ident)]:
            P.dma("sync", "const", dst[:], src, writes=[name])
        for name in ["cc", "bmod", "ng", "qg", "kg", "convw", "hmask", "identf"]:
            P.lastw[name] = (("d", "const"), P.dcount["const"])
        P.dma("gpsimd", "constg", ident_bf[:], ident, writes=["identbf"])
        P.op("vector", lambda e: e.memset(ones_bf[:], 1.0), writes=["ones_bf"])
        P.op("vector", lambda e: e.memset(ones_f[:], 1.0), writes=["ones_f"])
        P.op("vector", lambda e: e.memset(eps_sb[:], EPS), writes=["eps"])
        P.op("scalar", lambda e: e.activation(out=s_bf[:].rearrange("p k r -> p (k r)"), in_=cc_sb[:], func=AF.Silu),
             reads=["cc"], writes=["s_bf"])
        P.op("vector", lambda e: e.reduce_max(out=smx[:, 0:1], in_=qg_sb[:], axis=mybir.AxisListType.X,
                                              apply_absolute_value=True), reads=["qg"], writes=["smx0"])
        P.op("vector", lambda e: e.reduce_max(out=smx[:, 1:2], in_=kg_sb[:], axis=mybir.AxisListType.X,
                                              apply_absolute_value=True), reads=["kg"], writes=["smx1"])
        P.op("vector", lambda e: e.scalar_tensor_tensor(
            out=smx[:, 2:3], in0=smx[:, 0:1], scalar=-math.sqrt(128.0), in1=smx[:, 1:2], op0=ALU.mult, op1=ALU.mult),
            reads=["smx0", "smx1"], writes=["smx2"])

        seq = [("ctx", CTX, 1), ("halo", 2, 0), (6, 512, 0), (7, 512, 0)]

        def tile_src(tile, N):
            if tile == "ctx":
                return ctxT_v
            if tile == "halo":
                return xTh_v
            return xT_v[:, :, tile * 512:(tile + 1) * 512]

        def xsrc_buf(i):
            tile, N, r = seq[i]
            if tile == "halo":
                return xh[:], "xh"
            return xbuf[0][:, :, 0:N], ("xbuf", 0)

        def issue_x_load(i):
            tile, N, r = seq[i]
            xb, xres = xsrc_buf(i)
            P.dma("sync", "xh" if tile == "halo" else "x0", xb, tile_src(tile, N),
                  writes=[xres, (xres, "h", 0), (xres, "h", 1)])
            if isinstance(tile, int):
                tp = tile % 2
                P.dma("sync", f"c{tp}", cosb[tp][:], cosF[tile * 512:(tile + 1) * 512, :].rearrange("(b p) d -> p b d", p=128),
                      writes=[("cos", tp)])
                P.dma("sync", f"s{tp}", sinb[tp][:], sinF[tile * 512:(tile + 1) * 512, :].rearrange("(b p) d -> p b d", p=128),
                      writes=[("sin", tp)])

        issue_x_load(0)
        issue_x_load(1)

        def adaln_tile(t, wb, wres, key, pb):
            P.dma("gpsimd", key, wb[:], w_mod_v[:, :, t * 512:(t + 1) * 512], writes=wres)
            pv = bank(pb)[:, 0:96].rearrange("p (c r) -> p c r", r=2)
            for c4 in range(4):
                cidx = t * 4 + c4
                for kc in range(KC):
                    last = (c4 == 3 and kc == KC - 1)
                    P.op("tensor",
                         lambda e, pv=pv, wb=wb, c4=c4, cidx=cidx, kc=kc: e.matmul(
                             pv[:, cidx, :], wb[:, kc, c4 * 128:(c4 + 1) * 128], s_bf[:, kc, :],
                             start=(kc == 0), stop=(kc == KC - 1)),
                         reads=wres + ["s_bf"], writes=[psr(pb)], signal=last)
            return pv

        def adaln_evac(pv, pb, c0, c1, tag):
            P.op("vector",
                 lambda e: e.tensor_tensor(
                     out=m_sb[:, c0:c1, :], in0=pv[:, c0:c1, :],
                     in1=bmod_sb[:, c0:c1].unsqueeze(2).broadcast_to([128, c1 - c0, 2]), op=ALU.add),
                 reads=[psr(pb), "bmod"], writes=[("m", tag)])

        for t in range(8):
            pv = adaln_tile(t, wbuf[t % 2], [("wbuf", t % 2)], f"w{t % 2}", 0)
        adaln_evac(pv, 0, 0, 32, 7)
        P.op("vector",
             lambda e: e.scalar_tensor_tensor(
                 out=gm[:], in0=m_sb[:, 16:32, :], scalar=1.0,
                 in1=ng_sb[:].unsqueeze(2).broadcast_to([128, 16, 2]), op0=ALU.add, op1=ALU.mult),
             reads=[("m", 7), "ng"], writes=["gm"])

        chk("A")
        P.dma("gpsimd", "w0", wbuf[0][:], w_in_v[:, :, 1024:1536], writes=[("wbuf", 0)])
        hdst = {"ctx": ("B", hB[:, :, 0:CTX]), "halo": ("H", hH), 6: ("A", hA), 7: ("B", hB)}
        kn2 = [kn, work[:, 4096:4608].rearrange("p (h n) -> p h n", h=4)]
        pending = []
        gblk = [0]

        def flush_pending(keep=0):
            while len(pending) > keep:
                pending.pop(0)()

        def front_a(i):
            tile, N, r = seq[i]
            xb, xres = xsrc_buf(i)
            hname, hap = hdst[tile]
            hres = [("hT", hname, kc) for kc in range(KC)]
            ssb = 3

            def f1(hf=None):
                k0, k1 = (0, KC) if hf is None else (hf * 8, hf * 8 + 8)
                P.op("scalar", lambda e: e.activation(out=hap[:, k0:k1, :], in_=xb[:, k0:k1, :], func=AF.Square),
                     reads=[xres], writes=hres[k0:k1])

            def f2():
                for kc in range(KC):
                    P.op("tensor", lambda e, kc=kc: e.matmul(
                        bank(ssb)[:, 0:N], ones_bf[:], hap[:, kc, :], start=(kc == 0), stop=(kc == KC - 1)),
                        reads=[("hT", hname, kc), "ones_bf"], writes=[psr(ssb)], signal=(kc == KC - 1))
                P.op("scalar", lambda e: e.activation(
                    out=rstd_bc[:, 0:N], in_=bank(ssb)[:, 0:N], func=AF.Ln, bias=eps_sb[:, 0:1], scale=1.0 / D),
                    reads=[psr(ssb), "eps"], writes=["rstd_bc"])
                P.op("scalar", lambda e: e.activation(out=rstd_bc[:, 0:N], in_=rstd_bc[:, 0:N], func=AF.Exp, scale=-0.5),
                     reads=["rstd_bc"], writes=["rstd_bc"])

            def f3(hf=None):
                k0, k1 = (0, KC) if hf is None else (hf * 8, hf * 8 + 8)
                P.op("vector", lambda e: e.tensor_tensor(
                    out=xb[:, k0:k1, :], in0=xb[:, k0:k1, :],
                    in1=rstd_bc[:, 0:N].unsqueeze(1).broadcast_to([128, k1 - k0, N]), op=ALU.mult),
                    reads=[xres, "rstd_bc"], writes=[(xres, "h", hf)] if hf is not None else [xres])

            return [f1, f2, f3]

        def front_b(i):
            tile, N, r = seq[i]
            xb, xres = xsrc_buf(i)
            hname, hap = hdst[tile]
            for kc in range(KC):
                if kc % 2 == 0:
                    P.op("scalar", lambda e, kc=kc: e.activation(
                        out=hap[:, kc, :], in_=xb[:, kc, :], func=AF.Identity,
                        bias=m_sb[:, kc, r:r + 1], scale=gm[:, kc, r:r + 1]),
                        reads=[xres, (xres, "h", kc // 8), "gm", ("m", 7)], writes=[("hT", hname, kc)])
                else:
                    P.op("vector", lambda e, kc=kc: e.tensor_scalar(
                        out=hap[:, kc, :], in0=xb[:, kc, :], scalar1=gm[:, kc, r:r + 1],
                        scalar2=m_sb[:, kc, r:r + 1], op0=ALU.mult, op1=ALU.add),
                        reads=[xres, (xres, "h", kc // 8), "gm", ("m", 7)], writes=[("hT", hname, kc)])
            if tile == "ctx":
                issue_x_load(2)
            elif tile == 6:
                issue_x_load(3)

        def kv(i):
            tile, N, r = seq[i]
            if tile == "halo":
                return []
            hname, hap = hdst[tile]
            blocks = []
            nblk = N // 128
            latent = isinstance(tile, int)
            key0 = 0 if tile == "ctx" else CTX + tile * 512
            tp = tile % 2 if latent else 0
            for blk in range(nblk):
                blocks.append(lambda blk=blk: kv_block(blk, hname, hap, latent, key0, tp))
            return blocks

        KVB = [4, 5, 0, 1]

        def kv_block(blk, hname, hap, latent, key0, tp, newpath=None, cs=None):
                if cs is None:
                    cs = (cosb[tp], sinb[tp], ("cos", tp), ("sin", tp))
                g = gblk[0]
                gblk[0] += 1
                kvb = KVB[g % 4]
                tb = 6 + (g % 2)
                knb = kn2[g % 2]
                rb = rot_bf[g % 3]
                rres = ("rot", g % 3)
                kres = ("kn", g % 2)
                if newpath is None:
                    for kc in range(KC):
                        P.op("tensor", lambda e, kc=kc, blk=blk, kvb=kvb: e.matmul(
                            bank(kvb), hap[:, kc, blk * 128:(blk + 1) * 128], wbuf[0][:, kc, :],
                            start=(kc == 0), stop=(kc == KC - 1)),
                            reads=[("hT", hname, kc), ("wbuf", 0)], writes=[psr(kvb)], signal=(kc == KC - 1))
                    srcap, sres = bank(kvb), psr(kvb)
                else:
                    xq, qres, jcol = newpath
                    for kc in range(KC):
                        P.op("tensor", lambda e, kc=kc, blk=blk, kvb=kvb: e.matmul(
                            bank(kvb), xq[:, kc, blk * 128:(blk + 1) * 128], wbuf[1][:, kc, :],
                            start=(kc == 0), stop=(kc == KC - 1)),
                            reads=[qres, ("wkv2", kc)], writes=[psr(kvb)], signal=(kc == KC - 1))
                    kr = kraw[g % 2]
                    sres = ("kraw", g % 2)
                    P.op("vector", lambda e, kvb=kvb, kr=kr, jcol=jcol: e.scalar_tensor_tensor(
                        out=kr, in0=bank(kvb), scalar=rsx[:, jcol:jcol + 1], in1=bias_sb, op0=ALU.mult, op1=ALU.add),
                        reads=[psr(kvb), ("rsx", jcol), "bias_sb"], writes=[sres])
                    srcap = kr
                chunk = key0 // 128 + blk
                for h in range(2):
                    P.op("scalar", lambda e, h=h: e.activation(
                        out=junk[:, h * 128:(h + 1) * 128], in_=srcap[:, h * 128:(h + 1) * 128], func=AF.Square,
                        accum_out=ss[:, h:h + 1]),
                        reads=[sres], writes=["ss", "junk"])
                P.op("scalar", lambda e: e.activation(out=rstd_s[:, 0:2], in_=ss[:, 0:2], func=AF.Ln,
                                                      bias=eps_sb[:, 0:1], scale=1.0 / 128),
                     reads=["ss", "eps"], writes=["rstd_s"])
                P.op("scalar", lambda e: e.activation(out=rstd_s[:, 0:2], in_=rstd_s[:, 0:2], func=AF.Exp, scale=-0.5),
                     reads=["rstd_s"], writes=["rstd_s"])
                if newpath is None:
                    P.op("scalar", lambda e, chunk=chunk: e.activation(
                        out=V[:, chunk, :], in_=srcap[:, 256:512], func=AF.Copy),
                        reads=[sres], writes=["V"])
                else:
                    P.op("scalar", lambda e, chunk=chunk: e.activation(
                        out=V[:, chunk, :], in_=srcap[:, 256:512], func=AF.Copy),
                        reads=[sres], writes=["V"])
                P.op("vector", lambda e, knb=knb: e.tensor_tensor(
                    out=knb[:, 0:2, :], in0=srcap[:, 0:256].rearrange("p (h n) -> p h n", h=2),
                    in1=rstd_s[:, 0:2].unsqueeze(2).broadcast_to([128, 2, 128]), op=ALU.mult),
                    reads=[sres, "rstd_s"], writes=[kres])
                if latent:
                    P.op("vector", lambda e, knb=knb: e.tensor_tensor(
                        out=knb[:, 0:2, :], in0=knb[:, 0:2, :], in1=kg_sb[:].unsqueeze(1).broadcast_to([128, 2, 128]),
                        op=ALU.mult), reads=[kres, "kg"], writes=[kres])
                    if newpath is not None and g % 2 == 1:
                        emit_rope(P, knb, t1[:, 2:4, :], t2[:, 2:4, :], rb, cs[0][:, blk, :], cs[1][:, blk, :], 2,
                                  [cs[2], cs[3]], rres, kres, eng="vector", tres=("t1u", "t2u"))
                    else:
                        emit_rope(P, knb, t1, t2, rb, cs[0][:, blk, :], cs[1][:, blk, :], 2,
                                  [cs[2], cs[3]], rres, kres)
                else:
                    P.op("vector", lambda e, rb=rb, knb=knb: e.tensor_tensor(
                        out=rb[:, 0:2, :], in0=knb[:, 0:2, :], in1=kg_sb[:].unsqueeze(1).broadcast_to([128, 2, 128]),
                        op=ALU.mult), reads=[kres, "kg"], writes=[rres])

                def trans(rb=rb, rres=rres, tb=tb, kpos=key0 + blk * 128):
                    for h in range(2):
                        P.op("tensor", lambda e, h=h: e.transpose(
                            bank_bf(tb)[:, h * 128:(h + 1) * 128], rb[:, h, :], ident_bf[:]),
                            reads=[rres, "identbf"], writes=[psr(tb)], signal=(h == 1))
                    P.op("vector", lambda e: e.tensor_copy(
                        out=KT[:, :, kpos:kpos + 128],
                        in_=bank_bf(tb)[:, 0:256].rearrange("p (h n) -> p h n", h=2)),
                        reads=[psr(tb)], writes=["KT"])

                flush_pending(1)
                pending.append(trans)

        xbf = [big[:, q * 4096:(q + 1) * 4096].bitcast(BF16).rearrange("p (k n) -> p k n", k=KC) for q in range(4)]
        kraw = [work[:, 0:512], work[:, 3456:3968]]
        bias_sb = work[:, 4864:5376]
        junkf = work[:, 3328:3456]
        shiftb = env["wg"][:].rearrange("p k (a n) -> p (k a) n", a=2)[:, 0:16, :]
        NT = 6
        wg_f = env["wg"][:].rearrange("p k n -> p (k n)").bitcast(F32)
        cosb2 = [wg_f[:, i * 1024:i * 1024 + 512].rearrange("p (b d) -> p b d", b=4) for i in range(2)]
        sinb2 = [wg_f[:, i * 1024 + 512:(i + 1) * 1024].rearrange("p (b d) -> p b d", b=4) for i in range(2)]
        CS2 = [("cos2", 0), ("sin2", 0), ("cos2", 1), ("sin2", 1)]

        def xq_res(q):
            return ("xq", q)

        def newpath_load(t):
            if t >= NT:
                return
            q = 2 + t % 2
            P.dma("gpsimd", f"xq{q}", xbf[q][:], xT_v[:, :, t * 512:(t + 1) * 512], writes=[xq_res(q)])

        def cs_load(t):
            if t >= NT:
                return
            tp = t % 2
            P.dma("sync", f"c2{tp}", cosb2[tp][:], cosF[t * 512:(t + 1) * 512, :].rearrange("(b p) d -> p b d", p=128),
                  writes=[("cos2", tp)])
            P.dma("sync", f"s2{tp}", sinb2[tp][:], sinF[t * 512:(t + 1) * 512, :].rearrange("(b p) d -> p b d", p=128),
                  writes=[("sin2", tp)])

        def _unused_cs_load(t):
            tp = t % 2
            P.dma("sync", f"c{tp}", cosb[tp][:], cosF[t * 512:(t + 1) * 512, :].rearrange("(b p) d -> p b d", p=128),
                  writes=[("cos", tp)])
            P.dma("sync", f"s{tp}", sinb[tp][:], sinF[t * 512:(t + 1) * 512, :].rearrange("(b p) d -> p b d", p=128),
                  writes=[("sin", tp)])

        def prep_newpath():
            for t in range(2):
                newpath_load(t)
            P.op("vector", lambda e: e.tensor_copy(
                out=shiftb, in_=m_sb[:, 0:16, 0:1].broadcast_to([128, 16, 128])),
                reads=[("m", 7)], writes=["wg"])
            for kc in range(KC):
                P.op("tensor", lambda e, kc=kc: e.matmul(bank(3), shiftb[:, kc, :], wbuf[0][:, kc, :],
                                                         start=(kc == 0), stop=(kc == KC - 1)),
                     reads=["wg", ("wbuf", 0)], writes=[psr(3)], signal=(kc == KC - 1))
            P.op("scalar", lambda e: e.activation(out=bias_sb, in_=bank(3), func=AF.Copy),
                 reads=[psr(3)], writes=["bias_sb"])
            P.inherit(CS2, ["wg"])
            cs_load(0)
            cs_load(1)
            P.inherit([("wkv2", kc) for kc in range(KC)], [("wbuf", 1)])
            for kc in range(KC):
                if kc % 2 == 0:
                    P.op("scalar", lambda e, kc=kc: e.activation(
                        out=wbuf[1][:, kc, :], in_=wbuf[0][:, kc, :], func=AF.Copy, scale=gm[:, kc, 0:1]),
                        reads=[("wbuf", 0), "gm"], writes=[("wkv2", kc)])
                else:
                    P.op("vector", lambda e, kc=kc: e.tensor_scalar(
                        out=wbuf[1][:, kc, :], in0=wbuf[0][:, kc, :], scalar1=gm[:, kc, 0:1], scalar2=None,
                        op0=ALU.mult), reads=[("wbuf", 0), "gm"], writes=[("wkv2", kc)])

        def gram(j):
            t, blk = divmod(j, 4)
            q = 2 + t % 2
            gb = 2
            jc = j % 8
            for kc in range(KC):
                P.op("tensor", lambda e, kc=kc: e.matmul(
                    bank(gb)[:, 0:128], xbf[q][:, kc, blk * 128:(blk + 1) * 128],
                    xbf[q][:, kc, blk * 128:(blk + 1) * 128], start=(kc == 0), stop=(kc == KC - 1)),
                    reads=[xq_res(q)], writes=[psr(gb)], signal=(kc == KC - 1))
            P.op("vector", lambda e: e.scalar_tensor_tensor(
                out=junkf, in0=bank(gb)[:, 0:128], scalar=1.0, in1=ident_f[:], op0=ALU.mult, op1=ALU.mult,
                accum_out=ssx[:, jc:jc + 1]),
                reads=[psr(gb), "identf"], writes=[("ssx", jc), "junkf"])
            P.op("scalar", lambda e: e.activation(out=rsx[:, jc:jc + 1], in_=ssx[:, jc:jc + 1], func=AF.Ln,
                                                  bias=eps_sb[:, 0:1], scale=1.0 / D),
                 reads=[("ssx", jc), "eps"], writes=[("rsx", jc)])
            P.op("scalar", lambda e: e.activation(out=rsx[:, jc:jc + 1], in_=rsx[:, jc:jc + 1], func=AF.Exp, scale=-0.5),
                 reads=[("rsx", jc)], writes=[("rsx", jc)])

        def wload_q(n):
            P.dma("gpsimd", f"xq{2 + n}", xbf[2 + n][:], w_in_v[:, :, n * 512:(n + 1) * 512], writes=[xq_res(2 + n)])

        def run_newpath(old_steps):
            nb = NT * 4
            gram(0)
            for j in range(nb):
                if j + 1 < nb:
                    gram(j + 1)
                t, blk = divmod(j, 4)
                q = 2 + t % 2
                kv_block(blk, None, None, True, CTX + t * 512, t % 2, newpath=(xbf[q], xq_res(q), j % 8),
                         cs=(cosb2[t % 2], sinb2[t % 2], ("cos2", t % 2), ("sin2", t % 2)))
                if blk == 3:
                    newpath_load(t + 2)
                    cs_load(t + 2)
                    if t >= NT - 2:
                        wload_q(t - (NT - 2))
                if old_steps:
                    old_steps.pop(0)()
            while old_steps:
                old_steps.pop(0)()

        prep_newpath()
        fa = [front_a(i) for i in range(4)]
        kb = [kv(i) for i in range(4)]
        F1, F2, F3 = 0, 1, 2
        CTXI, HALO, OWN0, OWN1 = 0, 1, 2, 3
        for st in (fa[CTXI][F1], fa[HALO][F1], fa[CTXI][F2], fa[CTXI][F3], fa[HALO][F2], fa[HALO][F3],
                   lambda: front_b(CTXI), lambda: front_b(HALO), kb[CTXI][0], kb[CTXI][1]):
            st()
        sched = {
            1: [lambda: fa[OWN0][F1](0)],
            2: [lambda: fa[OWN0][F1](1)],
            4: [fa[OWN0][F2]],
            5: [lambda: fa[OWN0][F3](0)],
            6: [lambda: fa[OWN0][F3](1)],
            7: [lambda: front_b(OWN0)],
            8: [kb[OWN0][0]],
            9: [kb[OWN0][1]],
            10: [kb[OWN0][2], lambda: fa[OWN1][F1](0)],
            11: [kb[OWN0][3], lambda: fa[OWN1][F1](1)],
            13: [fa[OWN1][F2]],
            14: [lambda: fa[OWN1][F3](0)],
            15: [lambda: fa[OWN1][F3](1)],
            16: [lambda: front_b(OWN1)],
            17: [kb[OWN1][0]],
            18: [kb[OWN1][1]],
            19: [kb[OWN1][2]],
            20: [kb[OWN1][3]],
        }
        old_steps = []
        for slot in range(24):
            steps = sched.get(slot, [])
            old_steps.append(lambda steps=steps: [s() for s in steps])
        run_newpath(old_steps)
        P.inherit(["wg"], CS2)

        chk("BC")
        P.inherit(["t1b", "t2b"], ["rstd_bc", ("kraw", 0), ("kraw", 1)])
        P.inherit([("rot", 3)], ["junkf", ("kraw", 1)])
        P.inherit(["t1", "t2"], ["t1u", "t2u"])
        P.inherit([("wbuf", 1)], [("wkv2", kc) for kc in range(KC)])
        P.inherit(["qT", "sga"], [("xbuf", 0), ("xq", 0), ("xq", 1)])
        P.inherit(["mixT"], [("xbuf", 1), ("xq", 2), ("xq", 3)])
        wsrc = [w_in_v[:, :, 0:512], w_in_v[:, :, 512:1024], w_in_v[:, :, 1536:2048], w_in_v[:, :, 2048:2560]] + \
               [w_conv_v[:, :, i * 512:(i + 1) * 512] for i in range(8)]

        def wtile(n):
            if n < 2:
                return xbf[2 + n], xq_res(2 + n), f"xq{2 + n}"
            return wbuf[n % 2], ("wbuf", n % 2), f"w{n % 2}"

        def wload(n):
            if n < len(wsrc):
                wb_, wres_, key_ = wtile(n)
                P.dma("gpsimd", key_, wb_[:], wsrc[n], writes=[wres_])

        wload(2)
        wload(3)
        for half in range(2):
            for blk in range(8):
                g = gblk[0]
                gblk[0] += 1
                qb = [2, 3, 4, 5, 0, 1][g % 6]
                tb = 6 + (g % 2)
                knb = kn2[g % 2]
                rb = rot_bf[g % 4]
                rres = ("rot", g % 4)
                kres = ("kn", g % 2)
                for kc in range(KC):
                    P.op("tensor", lambda e, kc=kc, blk=blk, qb=qb, half=half: e.matmul(
                        bank(qb), hT_own[:, kc, blk * 128:(blk + 1) * 128], xbf[2 + half][:, kc, :],
                        start=(kc == 0), stop=(kc == KC - 1)),
                        reads=[("hT", "A" if blk < 4 else "B", kc), xq_res(2 + half)], writes=[psr(qb)],
                        signal=(kc == KC - 1))
                if half == 0 and blk == 0:
                    flush_pending(0)
                for h in range(4):
                    P.op("scalar", lambda e, qb=qb, h=h: e.activation(
                        out=junk[:, h * 128:(h + 1) * 128], in_=bank(qb)[:, h * 128:(h + 1) * 128], func=AF.Square,
                        accum_out=ss[:, h:h + 1]),
                        reads=[psr(qb)], writes=["ss", "junk"])
                P.op("scalar", lambda e: e.activation(out=rstd_s[:, 0:4], in_=ss[:, 0:4], func=AF.Ln,
                                                      bias=eps_sb[:, 0:1], scale=1.0 / 128),
                     reads=["ss", "eps"], writes=["rstd_s"])
                P.op("scalar", lambda e: e.activation(out=rstd_s[:, 0:4], in_=rstd_s[:, 0:4], func=AF.Exp, scale=-0.5),
                     reads=["rstd_s"], writes=["rstd_s"])
                P.op("vector", lambda e, qb=qb, knb=knb: e.tensor_tensor(
                    out=knb[:], in0=bank(qb).rearrange("p (h n) -> p h n", h=4),
                    in1=rstd_s[:, 0:4].unsqueeze(2).broadcast_to([128, 4, 128]), op=ALU.mult),
                    reads=[psr(qb), "rstd_s"], writes=[kres])
                P.op("vector", lambda e, knb=knb: e.tensor_tensor(
                    out=knb[:], in0=knb[:], in1=qg_sb[:].unsqueeze(1).broadcast_to([128, 4, 128]), op=ALU.mult),
                    reads=[kres, "qg"], writes=[kres])
                tp = (6 + blk // 4) % 2
                if g % 2 == 0:
                    emit_rope(P, knb, t1, t2, rb, cosb[tp][:, blk % 4, :], sinb[tp][:, blk % 4, :], 4,
                              [("cos", tp), ("sin", tp)], rres, kres)
                else:
                    emit_rope(P, knb, t1b, t2b, rb, cosb[tp][:, blk % 4, :], sinb[tp][:, blk % 4, :], 4,
                              [("cos", tp), ("sin", tp)], rres, kres, eng="vector", tres=("t1b", "t2b"))

                def transq(rb=rb, rres=rres, tb=tb, half=half, blk=blk):
                    for h in range(4):
                        P.op("tensor", lambda e, h=h: e.transpose(
                            bank_bf(tb)[:, h * 128:(h + 1) * 128], rb[:, h, :], ident_bf[:]),
                            reads=[rres, "identbf"], writes=[psr(tb)], signal=(h == 3))
                    P.op("vector", lambda e: e.tensor_copy(
                        out=qT[:, half * 4:(half + 1) * 4, blk * 128:(blk + 1) * 128],
                        in_=bank_bf(tb)[:, 0:512].rearrange("p (h n) -> p h n", h=4)),
                        reads=[psr(tb)], writes=["qT"])

                flush_pending(2)
                pending.append(transq)

        chk("D1")
        bctr = [0]

        def next_bank():
            b = bctr[0] % 7
            bctr[0] += 1
            return b

        pv7 = bank(7)[:, 0:96].rearrange("p (c r) -> p c r", r=2)

        def gate_dma(n):
            if n < 8:
                P.dma("gpsimd", "wg", wg[:], w_mod_v[:, :, 4096 + n * 256:4096 + (n + 1) * 256], writes=["wg"])

        def gate_mm(n):
            for c2 in range(2):
                cidx = 32 + n * 2 + c2
                for kc in range(KC):
                    P.op("tensor", lambda e, c2=c2, cidx=cidx, kc=kc: e.matmul(
                        pv7[:, cidx, :], wg[:, kc, c2 * 128:(c2 + 1) * 128], s_bf[:, kc, :],
                        start=(kc == 0), stop=(kc == KC - 1)),
                        reads=["wg", "s_bf"], writes=[psr(7)], signal=(c2 == 1 and kc == KC - 1))
            if n == 7:
                adaln_evac(pv7, 7, 32, 48, 11)

        for grp in range(2):
            wb = wbuf[grp % 2]
            if grp == 0:
                gate_dma(0)
            else:
                gate_mm(0)
                gate_dma(1)
            for c4 in range(4):
                head = grp * 4 + c4
                for tt in range(2):
                    b = next_bank()
                    for kc in range(KC):
                        P.op("tensor", lambda e, wb=wb, kc=kc, c4=c4, tt=tt, b=b: e.matmul(
                            bank(b), wb[:, kc, c4 * 128:(c4 + 1) * 128], hT_own[:, kc, tt * 512:(tt + 1) * 512],
                            start=(kc == 0), stop=(kc == KC - 1)),
                            reads=[("hT", "AB"[tt], kc), ("wbuf", grp % 2)], writes=[psr(b)], signal=(kc == KC - 1))
                    P.op("scalar", lambda e, b=b, head=head, tt=tt: e.activation(
                        out=sga[:, head, tt * 512:(tt + 1) * 512], in_=bank(b), func=AF.Silu),
                        reads=[psr(b)], writes=["sga"])
                if grp == 0 and c4 == 0:
                    flush_pending(0)
            wload(grp + 4)

        chk("D2")
        W_BC = ["rstd_bc", ("kn", 0), ("kn", 1), "t1", "t2", ("rot", 0), ("rot", 1), ("rot", 2), ("rot", 3), "junk", "t1u", "t2u",
                "t1b", "t2b", ("kraw", 0), ("kraw", 1), "bias_sb", "junkf"]
        W_D3 = ["cg", "u", "acc_c", "acc2", "sgc"]
        P.inherit(W_D3, W_BC)
        for i in range(8):
            wpar = i % 2
            wb4 = wbuf[wpar].rearrange("p k (g n) -> p k g n", g=4)
            if i < 7:
                gate_mm(i + 1)
                gate_dma(i + 2)
            banks = {}
            bh = next_bank()
            for g in (1, 2, 0, 3):
                for tt in range(2):
                    b = next_bank()
                    banks[(g, tt)] = b
                    for kc in range(KC):
                        P.op("tensor", lambda e, wb4=wb4, kc=kc, g=g, tt=tt, b=b: e.matmul(
                            bank(b), wb4[:, kc, g, :], hT_own[:, kc, tt * 512:(tt + 1) * 512],
                            start=(kc == 0), stop=(kc == KC - 1)),
                            reads=[("hT", "AB"[tt], kc), ("wbuf", wpar)], writes=[psr(b)], signal=(kc == KC - 1))
                        if g in (1, 2) and tt == 1:
                            off = 0 if g == 1 else 2
                            P.op("tensor", lambda e, wb4=wb4, kc=kc, g=g, off=off, bh=bh: e.matmul(
                                bank(bh)[:, off:off + 2], wb4[:, kc, g, :], hH[:, kc, :],
                                start=(kc == 0), stop=(kc == KC - 1)),
                                reads=[("hT", "H", kc), ("wbuf", wpar)], writes=[psr(bh)], signal=(kc == KC - 1))
                    if g == 1:
                        P.op("scalar", lambda e, b=b, tt=tt: e.activation(
                            out=cg_sb[:, 1 + tt * 512:1 + (tt + 1) * 512], in_=bank(b), func=AF.Copy),
                            reads=[psr(b)], writes=["cg"])
                    elif g == 2:
                        P.op("vector", lambda e, b=b, tt=tt: e.tensor_tensor(
                            out=u_sb[:, 1 + tt * 512:1 + (tt + 1) * 512], in0=bank(b),
                            in1=cg_sb[:, 1 + tt * 512:1 + (tt + 1) * 512], op=ALU.mult),
                            reads=[psr(b), "cg"], writes=["u"])
                    elif g == 3:
                        P.op("scalar", lambda e, b=b, tt=tt: e.activation(
                            out=sgc[:, tt * 512:(tt + 1) * 512], in_=bank(b), func=AF.Silu),
                            reads=[psr(b)], writes=["sgc"])
                if g == 2:
                    P.op("scalar", lambda e, bh=bh: e.activation(
                        out=cg_sb[:, 0:1026:1025], in_=bank(bh)[:, 0:2], func=AF.Copy),
                        reads=[psr(bh)], writes=["cg"])
                    P.op("vector", lambda e, bh=bh: e.tensor_tensor(
                        out=u_sb[:, 0:1026:1025], in0=bank(bh)[:, 2:4], in1=cg_sb[:, 0:1026:1025], op=ALU.mult),
                        reads=[psr(bh), "cg"], writes=["u"])
                    P.op("vector", lambda e: e.tensor_tensor(
                        out=u_sb[:, 0:1026:1025], in0=u_sb[:, 0:1026:1025], in1=hmask_sb[:], op=ALU.mult),
                        reads=["u", "hmask"], writes=["u"])
                    cw = convw_sb[:, i * 3:(i + 1) * 3]
                    P.op("vector", lambda e, cw=cw: e.tensor_scalar(
                        out=acc_c, in0=u_sb[:, 1:1025], scalar1=cw[:, 1:2], scalar2=None, op0=ALU.mult),
                        reads=["u", "convw"], writes=["acc_c"])
                    P.op("vector", lambda e, cw=cw: e.scalar_tensor_tensor(
                        out=acc_c, in0=u_sb[:, 0:1024], scalar=cw[:, 0:1], in1=acc_c, op0=ALU.mult, op1=ALU.add),
                        reads=["u", "convw", "acc_c"], writes=["acc_c"])
                    P.op("vector", lambda e, cw=cw: e.scalar_tensor_tensor(
                        out=acc_c, in0=u_sb[:, 2:1026], scalar=cw[:, 2:3], in1=acc_c, op0=ALU.mult, op1=ALU.add),
                        reads=["u", "convw", "acc_c"], writes=["acc_c"])
                if g == 0:
                    for tt in range(2):
                        b = banks[(0, tt)]
                        P.op("vector", lambda e, b=b, tt=tt: e.tensor_tensor(
                            out=acc2[:, tt * 512:(tt + 1) * 512], in0=bank(b), in1=acc_c[:, tt * 512:(tt + 1) * 512],
                            op=ALU.mult), reads=[psr(b), "acc_c"], writes=["acc2"])
            wload(i + 6)
            P.op("vector", lambda e, i=i: e.tensor_tensor(
                out=mixT[:, 8 + i, :], in0=acc2, in1=sgc, op=ALU.mult),
                reads=["acc2", "sgc"], writes=["mixT"])

        chk("D3")
        W_E = [("PT", 0), ("PT", 1), ("tmpb", 0), ("tmpb", 1), "rs", "ot", "lns", "o_sb"]
        P.inherit(W_E, W_D3)
        for c in range(4):
            P.dma("gpsimd", f"wout{c}", woutb[c][:], w_out_v[:, :, c * 512:(c + 1) * 512],
                  writes=R2_res + [("wout", c)])

        sm_scale = 1.0 / math.sqrt(128.0)
        iters = [(g, tt, hq) for g in range(2) for tt in range(2) for hq in range(4)]
        items = [(it, p) for it in range(len(iters)) for p in range(NCHUNK // 2)]

        def emit_S(idx):
            it, p = items[idx]
            g, tt, hq = iters[it]
            head = g * 4 + hq
            sp = idx % 3
            for cl in range(2):
                chunk = p * 2 + cl
                b = sp * 2 + cl
                P.op("tensor", lambda e, b=b, g=g, chunk=chunk, head=head, tt=tt: e.matmul(
                    bank(b), KT[:, g, chunk * 128:(chunk + 1) * 128], qT[:, head, tt * 512:(tt + 1) * 512],
                    start=True, stop=True),
                    reads=["KT", "qT"], writes=[psr(b)], signal=(cl == 1))

        def emit_norm(it):
            g, tt, hq = iters[it]
            head = g * 4 + hq
            ob = 6
            sb_ = 7
            P.op("scalar", lambda e: e.activation(out=lns_sb, in_=bank(sb_), func=AF.Ln), reads=[psr(sb_)], writes=["lns"])
            P.op("vector", lambda e: e.tensor_copy(out=o_sb, in_=bank(ob)), reads=[psr(ob)], writes=["o_sb"])
            P.op("scalar", lambda e: e.activation(out=rs_sb, in_=lns_sb, func=AF.Exp, scale=-1.0),
                 reads=["lns"], writes=["rs"])
            P.op("vector", lambda e: e.tensor_tensor(out=ot_sb, in0=o_sb, in1=rs_sb, op=ALU.mult),
                 reads=["o_sb", "rs"], writes=["ot"])
            P.op("vector", lambda e: e.tensor_tensor(
                out=mixT[:, head, tt * 512:(tt + 1) * 512], in0=ot_sb, in1=sga[:, head, tt * 512:(tt + 1) * 512],
                op=ALU.mult), reads=["ot", "sga"], writes=["mixT"])

        def emit_summm(idx):
            it, p = items[idx]
            tq = idx % 2
            sb_ = 7
            P.op("tensor", lambda e: e.matmul(bank(sb_), ones_bf[:], tmpb[tq], start=(p == 0), stop=(p == NP - 1)),
                 reads=[("tmpb", tq), "ones_bf"], writes=[psr(sb_)])

        NP = NCHUNK // 2
        emit_S(0)
        emit_S(1)
        for idx, (it, p) in enumerate(items):
            g, tt, hq = iters[it]
            sp = idx % 3
            pp = idx % 2
            if idx + 2 < len(items):
                emit_S(idx + 2)
            P.op("scalar", lambda e, sp=sp, pp=pp: e.activation(out=PT[pp], in_=PS[sp][:], func=AF.Exp, scale=sm_scale,
                                                                bias=smx[:, 2:3]),
                 reads=[psr(sp * 2), psr(sp * 2 + 1), "smx2"], writes=[("PT", pp)])
            ob = 6
            if idx > 0:
                emit_summm(idx - 1)
                if items[idx - 1][1] == NP - 1:
                    emit_norm(items[idx - 1][0])
            for cl in range(2):
                chunk = p * 2 + cl
                P.op("tensor", lambda e, ob=ob, chunk=chunk, g=g, pp=pp, cl=cl: e.matmul(
                    bank(ob), V[:, chunk, g * 128:(g + 1) * 128], PT[pp][:, cl * 512:(cl + 1) * 512],
                    start=(chunk == 0), stop=(chunk == NCHUNK - 1)),
                    reads=["V", ("PT", pp)], writes=[psr(ob)], signal=(cl == 1))
            tq = idx % 2
            P.op("vector", lambda e, pp=pp, tq=tq: e.tensor_tensor(
                out=tmpb[tq], in0=PT[pp][:, 0:512], in1=PT[pp][:, 512:1024], op=ALU.add),
                reads=[("PT", pp)], writes=[("tmpb", tq)])
        emit_summm(len(items) - 1)
        emit_norm(len(iters) - 1)

        chk("E")
        W_F = ["fg", "gate_bc", ("diag", 0), ("diag", 1)]
        P.inherit(W_F, W_E)
        P.inherit([("xo", 0), ("xo", 1)], ["qT"])
        P.inherit([("res", 0), ("res", 1)], ["sga"])
        P.dma("sync", "fgk", fg_sb, fg, writes=["fg"])
        for blk0 in range(2):
            P.dma("sync", f"xo{blk0}", xo[blk0], xown[blk0 * 128:(blk0 + 1) * 128, :], writes=[("xo", blk0)])
        def build_gate_bc():
            for q4 in range(4):
                db = diagb[q4 % 2]
                for a in range(4):
                    kc = q4 * 4 + a
                    P.op("vector", lambda e, db=db, a=a, kc=kc: e.tensor_scalar(
                        out=db[:, a, :], in0=ident_f[:], scalar1=m_sb[:, 32 + kc, 0:1], scalar2=None, op0=ALU.mult),
                        reads=["identf", ("m", 11)], writes=[("diag", q4 % 2)])
                for a in range(4):
                    P.op("tensor", lambda e, db=db, a=a: e.matmul(
                        bank(7)[:, a * 128:(a + 1) * 128], ones_f[:], db[:, a, :], start=True, stop=True),
                        reads=[("diag", q4 % 2), "ones_f"], writes=[psr(7)], signal=(a == 3))
                P.op("scalar", lambda e, q4=q4: e.activation(out=gate_bc[:, q4 * 512:(q4 + 1) * 512], in_=bank(7), func=AF.Copy),
                     reads=[psr(7)], writes=["gate_bc"])


        for blk in range(8):
            par = blk % 2
            for c in range(4):
                b = par * 4 + c
                for kc in range(KC):
                    P.op("tensor", lambda e, b=b, kc=kc, blk=blk, c=c: e.matmul(
                        bank(b), mixT[:, kc, blk * 128:(blk + 1) * 128], woutb[c][:, kc, :],
                        start=(kc == 0), stop=(kc == KC - 1)),
                        reads=["mixT", ("wout", c)], writes=[psr(b)], signal=(kc == KC - 1))
            rsb = resb[par]
            if blk == 0:
                build_gate_bc()
            for hh in range(2):
                P.op("vector", lambda e, rsb=rsb, par=par, hh=hh: e.tensor_tensor(
                    out=rsb[:, hh * 1024:(hh + 1) * 1024], in0=PS[par * 2 + hh][:],
                    in1=gate_bc[:, hh * 1024:(hh + 1) * 1024], op=ALU.mult),
                    reads=[psr(par * 4 + hh * 2), psr(par * 4 + hh * 2 + 1), "gate_bc"], writes=[("res", par)])
            P.op("vector", lambda e, rsb=rsb, par=par: e.tensor_tensor(out=rsb, in0=rsb, in1=xo[par], op=ALU.add),
                 reads=[("res", par), ("xo", par)], writes=[("res", par)])
            P.op("scalar", lambda e, rsb=rsb, par=par: e.activation(out=xo[par], in_=rsb, func=AF.Square,
                                                                   accum_out=ss[:, 0:1]),
                 reads=[("res", par)], writes=["ss", ("xo", par)])
            if blk + 2 < 8:
                P.dma("sync", f"xo{par}", xo[par], xown[(blk + 2) * 128:(blk + 3) * 128, :], writes=[("xo", par)])
            P.op("scalar", lambda e: e.activation(out=rstd_s[:, 0:1], in_=ss[:, 0:1], func=AF.Ln,
                                                  bias=eps_sb[:, 0:1], scale=1.0 / D),
                 reads=["ss", "eps"], writes=["rstd_s"])
            P.op("scalar", lambda e: e.activation(out=rstd_s[:, 0:1], in_=rstd_s[:, 0:1], func=AF.Exp, scale=-0.5),
                 reads=["rstd_s"], writes=["rstd_s"])
            P.op("vector", lambda e, rsb=rsb: e.scalar_tensor_tensor(
                out=rsb, in0=rsb, scalar=rstd_s[:, 0:1], in1=fg_sb, op0=ALU.mult, op1=ALU.mult),
                reads=[("res", par), "rstd_s", "fg"], writes=[("res", par)])
            P.dma("sync", f"o{par}", out[blk * 128:(blk + 1) * 128, :], rsb, reads=[("res", par)], writes=[])
        P.final_wait("sync", ["o0", "o1"])


def emit_rope(P, kn, t1, t2, rb, cos_blk, sin_blk, H, tab_res, rot_res, kn_res="kn", eng="gpsimd", tres=("t1", "t2")):
    knv = kn[:, 0:H, :].rearrange("p h (a b j) -> p h a b j", a=2, b=2)
    t2v = t2[:, 0:H, :].rearrange("p h (a b j) -> p h a b j", a=2, b=2)
    sv = sin_blk.rearrange("p (a b j) -> p a b j", a=2, b=2)
    P.op(eng, lambda e: e.tensor_tensor(
        out=t1[:, 0:H, :], in0=kn[:, 0:H, :], in1=cos_blk.unsqueeze(1).broadcast_to([128, H, 128]), op=ALU.mult),
        reads=[kn_res] + tab_res, writes=[tres[0]])
    for bsel in range(2):
        P.op(eng, lambda e, bsel=bsel: e.tensor_tensor(
            out=t2v[:, :, :, bsel, :], in0=knv[:, :, :, 1 - bsel, :],
            in1=sv[:, :, bsel, :].unsqueeze(1).broadcast_to([128, H, 2, 32]), op=ALU.mult),
            reads=[kn_res] + tab_res, writes=[tres[1]])
    P.op(eng, lambda e: e.tensor_tensor(out=rb[:, 0:H, :], in0=t1[:, 0:H, :], in1=t2[:, 0:H, :], op=ALU.add),
         reads=list(tres), writes=[rot_res])


_NC_CACHE = {}


def _rope_tables():
    quarter = 32
    inv = (10000.0 ** (-np.arange(quarter, dtype=np.float32) / quarter)).astype(np.float32)
    t = np.arange(SEQ)
    row = (t // 64).astype(np.float32)
    col = (t % 64).astype(np.float32)
    ar = row[:, None] * inv[None, :]
    ac = col[:, None] * inv[None, :]
    cr, sr, c_c, s_c = np.cos(ar), np.sin(ar), np.cos(ac), np.sin(ac)
    cosF = np.concatenate([cr, cr, c_c, c_c], axis=1).astype(np.float32)
    sinF = np.concatenate([-sr, sr, -s_c, s_c], axis=1).astype(np.float32)
    return cosF, sinF


def kernel(x, c, ctx, c_ctx, w_mod, b_mod, norm_g, w_in, q_norm_g, k_norm_g, conv_w, w_out, final_norm_g):
    f = lambda a: np.ascontiguousarray(np.asarray(a, dtype=np.float32))
    x, c, ctx, c_ctx = f(x), f(c), f(ctx), f(c_ctx)
    w_mod0, w_in0, w_out0 = f(w_mod[0]), f(w_in[0]), f(w_out[0])
    b_mod0, ng0, qg0, kg0, cw0, fg0 = f(b_mod[0]), f(norm_g[0]), f(q_norm_g[0]), f(k_norm_g[0]), f(conv_w[0]), f(final_norm_g)
    if "nc" not in _NC_CACHE:
        _NC_CACHE["nc"] = build_program()
    nc = _NC_CACHE["nc"]
    cosF, sinF = _rope_tables()
    shared = {
        "w_mod": w_mod0, "w_in": np.ascontiguousarray(w_in0[:, :2560]), "w_out": w_out0,
        "w_conv": np.ascontiguousarray(
            w_in0[:, 2560:].reshape(D, 4, 8, 128).transpose(0, 2, 1, 3).reshape(D, 4096)),
        "bmod": np.ascontiguousarray(b_mod0.reshape(48, 128).T),
        "ng": np.ascontiguousarray(ng0.reshape(16, 128).T),
        "qg": np.ascontiguousarray(np.broadcast_to(qg0[None, :], (128, 128))),
        "kg": np.ascontiguousarray(np.broadcast_to(kg0[None, :], (128, 128))),
        "convw": np.ascontiguousarray(cw0.reshape(3, 8, 128).transpose(2, 1, 0).reshape(128, 24)),
        "fg": np.ascontiguousarray(np.broadcast_to(fg0[None, :], (128, D))),
        "ident": np.eye(128, dtype=np.float32),
    }
    in_maps = []
    for core in range(8):
        b, j = core // 4, core % 4
        t0 = j * OWN
        order = np.concatenate([np.arange(0, t0), np.arange(t0 + OWN, SEQ), np.arange(t0, t0 + OWN)])
        xb = x[b]
        xTc = np.ascontiguousarray(xb[order].T)
        halo = np.zeros((2, D), np.float32)
        hm = np.zeros((128, 2), np.float32)
        if t0 > 0:
            halo[0] = xb[t0 - 1]
            hm[:, 0] = 1.0
        if t0 + OWN < SEQ:
            halo[1] = xb[t0 + OWN]
            hm[:, 1] = 1.0
        ccm = np.stack([c[b].reshape(16, 128).T, c_ctx.reshape(16, 128).T], axis=2).reshape(128, 32)
        m = dict(shared)
        m.update({
            "xT": xTc, "xTh": np.ascontiguousarray(halo.T), "xown": np.ascontiguousarray(xb[t0:t0 + OWN]),
            "ctxT": np.ascontiguousarray(ctx[b].T), "cc": np.ascontiguousarray(ccm),
            "cosF": np.ascontiguousarray(cosF[order]), "sinF": np.ascontiguousarray(sinF[order]), "hmask": hm,
        })
        in_maps.append(m)
    if _NC_CACHE.get("prep_only"):
        return nc, in_maps
    res = run_bass_kernel_spmd(nc, in_maps, core_ids=list(range(8)))
    outp = np.empty((2, SEQ, D), np.float32)
    for core in range(8):
        b, j = core // 4, core % 4
        outp[b, j * OWN:(j + 1) * OWN] = res.results[core]["out"]
    return outp
```

```python
import math
from contextlib import ExitStack

import numpy as np
import concourse.bass as bass
import concourse.mybir as mybir
from concourse.bass_utils import run_bass_kernel_spmd

F32 = mybir.dt.float32
BF16 = mybir.dt.bfloat16
AF = mybir.ActivationFunctionType
ALU = mybir.AluOpType

D = 2048
KC = 16
SEQ = 4096
CTX = 256
OWN = 1024
NKEY = CTX + SEQ
NCHUNK = NKEY // 128
EPS = 1e-6
ENGS = ["tensor", "scalar", "vector", "gpsimd", "sync"]


class Prog:
    def __init__(self, nc, es):
        self.nc = nc
        self.es = es
        self.streams = {e: [] for e in ENGS}
        self.esem = {e: es.enter_context(nc.semaphore("p_" + e)) for e in ENGS[:4]}
        self.ecount = {e: 0 for e in ENGS[:4]}
        self.dsem = {}
        self.dcount = {}
        self.lastw = {}
        self.readers = {}
        self.waited = {e: {} for e in ENGS}

    def _sem(self, key):
        if key[0] == "e":
            return self.esem[key[1]]
        return self.dsem[key[1]]

    def _waits(self, stream, reads, writes, skip=None):
        need = {}

        def add(tok):
            k, v = tok
            if need.get(k, 0) < v:
                need[k] = v

        for r in reads:
            if r in self.lastw:
                add(self.lastw[r])
        for w in writes:
            if w in self.lastw:
                add(self.lastw[w])
            for k, v in self.readers.get(w, {}).items():
                add((k, v))
        for k, v in need.items():
            if k == skip or k == ("e", "tensor") == ("e", stream):
                continue
            if k[0] == "e":
                assert self.ecount[k[1]] >= v, f"dependency on unsignaled op {k} {v} {self.ecount[k[1]]}"
            if self.waited[stream].get(k, 0) >= v:
                continue
            self.waited[stream][k] = v
            sem = self._sem(k)
            self.streams[stream].append(lambda e, sem=sem, v=v: e.wait_ge(sem, v))

    def _record(self, tok, reads, writes):
        k, v = tok
        for r in reads:
            d = self.readers.setdefault(r, {})
            if d.get(k, 0) < v:
                d[k] = v
        for w in writes:
            self.lastw[w] = tok
            self.readers[w] = {}

    def op(self, eng, fn, reads=(), writes=(), signal=True):
        ps_reads = [r for r in reads if isinstance(r, tuple) and r[0] == "ps"]
        if ps_reads:
            reads = [r for r in reads if r not in ps_reads]
            writes = list(writes) + ps_reads
        self._waits(eng, reads, writes)
        if signal:
            self.ecount[eng] += 1
            tok = (("e", eng), self.ecount[eng])
            sem = self.esem[eng]
            self.streams[eng].append(lambda e, fn=fn, sem=sem: fn(e).then_inc(sem, 1))
        else:
            tok = (("e", eng), self.ecount[eng] + 1)
            self.streams[eng].append(lambda e, fn=fn: fn(e))
        self._record(tok, reads, writes)

    def dma(self, queue, key, out, in_, reads=(), writes=(), cont=False):
        if key not in self.dsem:
            self.dsem[key] = self.es.enter_context(self.nc.semaphore("d_" + key))
            self.dcount[key] = 0
        self._waits(queue, reads, writes, skip=("d", key) if cont else None)
        self.dcount[key] += 16
        tok = (("d", key), self.dcount[key])
        sem = self.dsem[key]
        self.streams[queue].append(
            lambda e, out=out, in_=in_, sem=sem: e.dma_start(out=out, in_=in_).then_inc(sem, 16))
        self._record(tok, reads, writes)

    def inherit(self, news, olds):
        for n in news:
            d = self.readers.setdefault(n, {})
            for o in olds:
                for k, v in self.readers.get(o, {}).items():
                    if d.get(k, 0) < v:
                        d[k] = v
                if o in self.lastw:
                    k, v = self.lastw[o]
                    if d.get(k, 0) < v:
                        d[k] = v

    def final_wait(self, stream, keys):
        for key in keys:
            sem = self.dsem[key]
            v = self.dcount[key]
            self.streams[stream].append(lambda e, sem=sem, v=v: e.wait_ge(sem, v))

    def emit(self, block):
        for eng in ENGS:
            thunks = self.streams[eng]
            if not thunks:
                continue

            def body(e, thunks=thunks):
                for t in thunks:
                    t(e)

            getattr(block, eng)(body)


class _Stop(Exception):
    pass


def build_program(stop=None):
    nc = bass.Bass("TRN2", target_bir_lowering=False)

    def din(name, shape):
        return nc.dram_tensor(name, shape, F32, kind="ExternalInput").ap()

    xT = din("xT", [D, SEQ])
    xTh = din("xTh", [D, 2])
    xown = din("xown", [OWN, D])
    ctxT = din("ctxT", [D, CTX])
    cc = din("cc", [128, 32])
    w_mod = din("w_mod", [D, 3 * D])
    bmod = din("bmod", [128, 48])
    ng = din("ng", [128, 16])
    w_in = din("w_in", [D, 2560])
    w_conv = din("w_conv", [D, 4096])
    qg = din("qg", [128, 128])
    kg = din("kg", [128, 128])
    convw = din("convw", [128, 24])
    w_out = din("w_out", [D, D])
    fg = din("fg", [128, D])
    cosF = din("cosF", [SEQ, 128])
    sinF = din("sinF", [SEQ, 128])
    ident = din("ident", [128, 128])
    hmask = din("hmask", [128, 2])
    out = nc.dram_tensor("out", [OWN, D], F32, kind="ExternalOutput").ap()

    xT_v = xT.rearrange("(k p) n -> p k n", p=128)
    xTh_v = xTh.rearrange("(k p) n -> p k n", p=128)
    ctxT_v = ctxT.rearrange("(k p) n -> p k n", p=128)
    w_mod_v = w_mod.rearrange("(k p) n -> p k n", p=128)
    w_in_v = w_in.rearrange("(k p) n -> p k n", p=128)
    w_conv_v = w_conv.rearrange("(k p) n -> p k n", p=128)
    w_out_v = w_out.rearrange("(k p) n -> p k n", p=128)

    with ExitStack() as es:
        def sb(name, shape, dt):
            return es.enter_context(nc.sbuf_tensor(name, shape, dt))

        big = sb("big", [128, 16384], F32)
        R2 = sb("R2", [128, 16384], F32)
        hHt = sb("hHt", [128, 16, 2], BF16)
        wg = sb("wg", [128, 16, 256], BF16)
        xh = sb("xh", [128, 16, 2], F32)
        KT = sb("KT", [128, 2, NKEY], BF16)
        V = sb("V", [128, NCHUNK, 256], BF16)
        work = sb("work", [128, 5376], F32)
        ident_bf = sb("ident_bf", [128, 128], BF16)
        ident_f = sb("ident_f", [128, 128], F32)
        ones_bf = sb("ones_bf", [128, 128], BF16)
        ones_f = sb("ones_f", [128, 128], F32)
        qg_sb = sb("qg_sb", [128, 128], F32)
        kg_sb = sb("kg_sb", [128, 128], F32)
        ng_sb = sb("ng_sb", [128, 16], F32)
        bmod_sb = sb("bmod_sb", [128, 48], F32)
        convw_sb = sb("convw_sb", [128, 24], F32)
        hmask_sb = sb("hmask_sb", [128, 2], F32)
        cc_sb = sb("cc_sb", [128, 32], F32)
        s_bf = sb("s_bf", [128, 16, 2], BF16)
        m_sb = sb("m_sb", [128, 48, 2], F32)
        gm = sb("gm", [128, 16, 2], F32)
        eps_sb = sb("eps_sb", [128, 1], F32)
        ss = sb("ss", [128, 8], F32)
        rstd_s = sb("rstd_s", [128, 8], F32)
        ssx = sb("ssx", [128, 8], F32)
        rsx = sb("rsx", [128, 8], F32)
        smx = sb("smx", [128, 4], F32)
        cosb = [sb(f"cosb{i}", [128, 4, 128], F32) for i in range(2)]
        sinb = [sb(f"sinb{i}", [128, 4, 128], F32) for i in range(2)]
        PS = [es.enter_context(nc.psum_tensor(f"PS{i}", [128, 1024], F32)) for i in range(4)]

        P = Prog(nc, es)
        block = es.enter_context(nc.Block())
        try:
            _body(locals(), stop)
        except _Stop:
            pass
        P.emit(block)
    return nc


def _body(env, stop):
        globals().update({})
        hHt = env["hHt"]
        ssx = env["ssx"]
        xh = env["xh"]
        rsx = env["rsx"]
        smx = env["smx"]
        wg = env["wg"]
        (nc, es, P, block, big, R2, KT, V, work, ident_bf, ident_f, ones_bf, ones_f, qg_sb, kg_sb, ng_sb, bmod_sb,
         convw_sb, hmask_sb, cc_sb, s_bf, m_sb, gm, eps_sb, ss, rstd_s, cosb, sinb, PS) = [env[k] for k in (
            "nc", "es", "P", "block", "big", "R2", "KT", "V", "work", "ident_bf", "ident_f", "ones_bf", "ones_f",
            "qg_sb", "kg_sb", "ng_sb", "bmod_sb", "convw_sb", "hmask_sb", "cc_sb", "s_bf", "m_sb", "gm", "eps_sb",
            "ss", "rstd_s", "cosb", "sinb", "PS")]
        w_conv_v = env["w_conv_v"]
        (xT_v, xTh_v, ctxT_v, w_mod_v, w_in_v, w_out_v, xown, cc, bmod, ng, qg, kg, convw, fg, cosF, sinF, ident,
         hmask, out) = [env[k] for k in (
            "xT_v", "xTh_v", "ctxT_v", "w_mod_v", "w_in_v", "w_out_v", "xown", "cc", "bmod", "ng", "qg", "kg",
            "convw", "fg", "cosF", "sinF", "ident", "hmask", "out")]

        def chk(name):
            if stop == name:
                raise _Stop()

        def bank(b):
            return PS[b // 2][:, (b % 2) * 512:(b % 2) * 512 + 512]

        def bank_bf(b):
            return bank(b).bitcast(BF16)

        def psr(b):
            return ("ps", b)

        xbuf = [big[:, i * 8192:(i + 1) * 8192].rearrange("p (k n) -> p k n", k=KC) for i in range(2)]
        qT = big[:, 0:4096].bitcast(BF16).rearrange("p (h n) -> p h n", h=8)
        sga = big[:, 4096:8192].bitcast(BF16).rearrange("p (h n) -> p h n", h=8)
        mixT = big[:, 8192:16384].bitcast(BF16).rearrange("p (k n) -> p k n", k=KC)
        hT_own = R2[:, 0:8192].bitcast(BF16).rearrange("p (k n) -> p k n", k=KC)
        hA = hT_own[:, :, 0:512]
        hB = hT_own[:, :, 512:1024]
        hH = hHt[:]
        wbuf = [R2[:, 8192 + i * 4096:8192 + (i + 1) * 4096].bitcast(BF16).rearrange("p (k n) -> p k n", k=KC)
                for i in range(2)]
        woutb = [R2[:, c * 4096:(c + 1) * 4096].bitcast(BF16).rearrange("p (k n) -> p k n", k=KC)
                 for c in range(4)]
        R2_res = [("hT", n, k) for n in "AB" for k in range(KC)] + [("wbuf", 0), ("wbuf", 1)]

        rstd_bc = work[:, 512:1024]
        kn = work[:, 1024:1536].rearrange("p (h n) -> p h n", h=4)
        t1 = work[:, 1536:2048].rearrange("p (h n) -> p h n", h=4)
        t2 = work[:, 2048:2560].rearrange("p (h n) -> p h n", h=4)
        rot_bf = [work[:, o:o + 256].bitcast(BF16).rearrange("p (h n) -> p h n", h=4)
                  for o in (2560, 2816, 4608, 3328)]
        junk = work[:, 3072:3328].bitcast(BF16)
        t1b = work[:, 0:512].rearrange("p (h n) -> p h n", h=4)
        t2b = work[:, 512:1024].rearrange("p (h n) -> p h n", h=4)
        cg_sb = work[:, 0:1026]
        u_sb = work[:, 1026:2052]
        acc_c = work[:, 2052:3076]
        acc2 = work[:, 3076:4100]
        sgc = work[:, 4100:5124]
        PT = [work[:, i * 512:(i + 1) * 512].bitcast(BF16) for i in range(2)]
        tmpb = [work[:, 1024 + i * 256:1024 + (i + 1) * 256].bitcast(BF16) for i in range(2)]
        accs = [work[:, 1536 + i * 512:1536 + (i + 1) * 512] for i in range(4)]
        lns_sb = work[:, 1536:2048]
        o_sb = work[:, 2048:2560]
        rs_sb = work[:, 3584:4096]
        ot_sb = work[:, 4096:4608]
        fg_sb = work[:, 0:2048]
        gate_bc = work[:, 2048:4096]
        diagb = [work[:, 4096 + i * 512:4096 + (i + 1) * 512].rearrange("p (a n) -> p a n", a=4) for i in range(2)]
        xo = [big[:, i * 2048:(i + 1) * 2048] for i in range(2)]
        resb = [big[:, 4096 + i * 2048:4096 + (i + 1) * 2048] for i in range(2)]

        for name, dst, src in [("cc", cc_sb, cc), ("bmod", bmod_sb, bmod), ("ng", ng_sb, ng), ("qg", qg_sb, qg),
                               ("kg", kg_sb, kg), ("convw", convw_sb, convw), ("hmask", hmask_sb, hmask),
                               ("identf", ident_f, ident)]:
            P.dma("sync", "const", dst[:], src, writes=[name])
        for name in ["cc", "bmod", "ng", "qg", "kg", "convw", "hmask", "identf"]:
            P.lastw[name] = (("d", "const"), P.dcount["const"])
        P.dma("gpsimd", "constg", ident_bf[:], ident, writes=["identbf"])
        P.op("vector", lambda e: e.memset(ones_bf[:], 1.0), writes=["ones_bf"])
        P.op("vector", lambda e: e.memset(ones_f[:], 1.0), writes=["ones_f"])
        P.op("vector", lambda e: e.memset(eps_sb[:], EPS), writes=["eps"])
        P.op("scalar", lambda e: e.activation(out=s_bf[:].rearrange("p k r -> p (k r)"), in_=cc_sb[:], func=AF.Silu),
             reads=["cc"], writes=["s_bf"])
        P.op("vector", lambda e: e.reduce_max(out=smx[:, 0:1], in_=qg_sb[:], axis=mybir.AxisListType.X,
                                              apply_absolute_value=True), reads=["qg"], writes=["smx0"])
        P.op("vector", lambda e: e.reduce_max(out=smx[:, 1:2], in_=kg_sb[:], axis=mybir.AxisListType.X,
                                              apply_absolute_value=True), reads=["kg"], writes=["smx1"])
        P.op("vector", lambda e: e.scalar_tensor_tensor(
            out=smx[:, 2:3], in0=smx[:, 0:1], scalar=-math.sqrt(128.0), in1=smx[:, 1:2], op0=ALU.mult, op1=ALU.mult),
            reads=["smx0", "smx1"], writes=["smx2"])

        seq = [("ctx", CTX, 1), ("halo", 2, 0), (6, 512, 0), (7, 512, 0)]

        def tile_src(tile, N):
            if tile == "ctx":
                return ctxT_v
            if tile == "halo":
                return xTh_v
            return xT_v[:, :, tile * 512:(tile + 1) * 512]

        def xsrc_buf(i):
            tile, N, r = seq[i]
            if tile == "halo":
                return xh[:], "xh"
            return xbuf[0][:, :, 0:N], ("xbuf", 0)

        def issue_x_load(i):
            tile, N, r = seq[i]
            xb, xres = xsrc_buf(i)
            P.dma("sync", "xh" if tile == "halo" else "x0", xb, tile_src(tile, N),
                  writes=[xres, (xres, "h", 0), (xres, "h", 1)])
            if isinstance(tile, int):
                tp = tile % 2
                P.dma("sync", f"c{tp}", cosb[tp][:], cosF[tile * 512:(tile + 1) * 512, :].rearrange("(b p) d -> p b d", p=128),
                      writes=[("cos", tp)])
                P.dma("sync", f"s{tp}", sinb[tp][:], sinF[tile * 512:(tile + 1) * 512, :].rearrange("(b p) d -> p b d", p=128),
                      writes=[("sin", tp)])

        issue_x_load(0)
        issue_x_load(1)

        def adaln_tile(t, wb, wres, key, pb):
            P.dma("gpsimd", key, wb[:], w_mod_v[:, :, t * 512:(t + 1) * 512], writes=wres)
            pv = bank(pb)[:, 0:96].rearrange("p (c r) -> p c r", r=2)
            for c4 in range(4):
                cidx = t * 4 + c4
                for kc in range(KC):
                    last = (c4 == 3 and kc == KC - 1)
                    P.op("tensor",
                         lambda e, pv=pv, wb=wb, c4=c4, cidx=cidx, kc=kc: e.matmul(
                             pv[:, cidx, :], wb[:, kc, c4 * 128:(c4 + 1) * 128], s_bf[:, kc, :],
                             start=(kc == 0), stop=(kc == KC - 1)),
                         reads=wres + ["s_bf"], writes=[psr(pb)], signal=last)
            return pv

        def adaln_evac(pv, pb, c0, c1, tag):
            P.op("vector",
                 lambda e: e.tensor_tensor(
                     out=m_sb[:, c0:c1, :], in0=pv[:, c0:c1, :],
                     in1=bmod_sb[:, c0:c1].unsqueeze(2).broadcast_to([128, c1 - c0, 2]), op=ALU.add),
                 reads=[psr(pb), "bmod"], writes=[("m", tag)])

        for t in range(8):
            pv = adaln_tile(t, wbuf[t % 2], [("wbuf", t % 2)], f"w{t % 2}", 0)
        adaln_evac(pv, 0, 0, 32, 7)
        P.op("vector",
             lambda e: e.scalar_tensor_tensor(
                 out=gm[:], in0=m_sb[:, 16:32, :], scalar=1.0,
                 in1=ng_sb[:].unsqueeze(2).broadcast_to([128, 16, 2]), op0=ALU.add, op1=ALU.mult),
             reads=[("m", 7), "ng"], writes=["gm"])

        chk("A")
        P.dma("gpsimd", "w0", wbuf[0][:], w_in_v[:, :, 1024:1536], writes=[("wbuf", 0)])
        hdst = {"ctx": ("B", hB[:, :, 0:CTX]), "halo": ("H", hH), 6: ("A", hA), 7: ("B", hB)}
        kn2 = [kn, work[:, 4096:4608].rearrange("p (h n) -> p h n", h=4)]
        pending = []
        gblk = [0]

        def flush_pending(keep=0):
            while len(pending) > keep:
                pending.pop(0)()

        def front_a(i):
            tile, N, r = seq[i]
            xb, xres = xsrc_buf(i)
            hname, hap = hdst[tile]
            hres = [("hT", hname, kc) for kc in range(KC)]
            ssb = 3

            def f1(hf=None):
                k0, k1 = (0, KC) if hf is None else (hf * 8, hf * 8 + 8)
                P.op("scalar", lambda e: e.activation(out=hap[:, k0:k1, :], in_=xb[:, k0:k1, :], func=AF.Square),
                     reads=[xres], writes=hres[k0:k1])

            def f2():
                for kc in range(KC):
                    P.op("tensor", lambda e, kc=kc: e.matmul(
                        bank(ssb)[:, 0:N], ones_bf[:], hap[:, kc, :], start=(kc == 0), stop=(kc == KC - 1)),
                        reads=[("hT", hname, kc), "ones_bf"], writes=[psr(ssb)], signal=(kc == KC - 1))
                P.op("scalar", lambda e: e.activation(
                    out=rstd_bc[:, 0:N], in_=bank(ssb)[:, 0:N], func=AF.Ln, bias=eps_sb[:, 0:1], scale=1.0 / D),
                    reads=[psr(ssb), "eps"], writes=["rstd_bc"])
                P.op("scalar", lambda e: e.activation(out=rstd_bc[:, 0:N], in_=rstd_bc[:, 0:N], func=AF.Exp, scale=-0.5),
                     reads=["rstd_bc"], writes=["rstd_bc"])

            def f3(hf=None):
                k0, k1 = (0, KC) if hf is None else (hf * 8, hf * 8 + 8)
                P.op("vector", lambda e: e.tensor_tensor(
                    out=xb[:, k0:k1, :], in0=xb[:, k0:k1, :],
                    in1=rstd_bc[:, 0:N].unsqueeze(1).broadcast_to([128, k1 - k0, N]), op=ALU.mult),
                    reads=[xres, "rstd_bc"], writes=[(xres, "h", hf)] if hf is not None else [xres])

            return [f1, f2, f3]

        def front_b(i):
            tile, N, r = seq[i]
            xb, xres = xsrc_buf(i)
            hname, hap = hdst[tile]
            for kc in range(KC):
                if kc % 2 == 0:
                    P.op("scalar", lambda e, kc=kc: e.activation(
                        out=hap[:, kc, :], in_=xb[:, kc, :], func=AF.Identity,
                        bias=m_sb[:, kc, r:r + 1], scale=gm[:, kc, r:r + 1]),
                        reads=[xres, (xres, "h", kc // 8), "gm", ("m", 7)], writes=[("hT", hname, kc)])
                else:
                    P.op("vector", lambda e, kc=kc: e.tensor_scalar(
                        out=hap[:, kc, :], in0=xb[:, kc, :], scalar1=gm[:, kc, r:r + 1],
                        scalar2=m_sb[:, kc, r:r + 1], op0=ALU.mult, op1=ALU.add),
                        reads=[xres, (xres, "h", kc // 8), "gm", ("m", 7)], writes=[("hT", hname, kc)])
            if tile == "ctx":
                issue_x_load(2)
            elif tile == 6:
                issue_x_load(3)

        def kv(i):
            tile, N, r = seq[i]
            if tile == "halo":
                return []
            hname, hap = hdst[tile]
            blocks = []
            nblk = N // 128
            latent = isinstance(tile, int)
            key0 = 0 if tile == "ctx" else CTX + tile * 512
            tp = tile % 2 if latent else 0
            for blk in range(nblk):
                blocks.append(lambda blk=blk: kv_block(blk, hname, hap, latent, key0, tp))
            return blocks

        KVB = [4, 5, 0, 1]

        def kv_block(blk, hname, hap, latent, key0, tp, newpath=None, cs=None):
                if cs is None:
                    cs = (cosb[tp], sinb[tp], ("cos", tp), ("sin", tp))
                g = gblk[0]
                gblk[0] += 1
                kvb = KVB[g % 4]
                tb = 6 + (g % 2)
                knb = kn2[g % 2]
                rb = rot_bf[g % 3]
                rres = ("rot", g % 3)
                kres = ("kn", g % 2)
                if newpath is None:
                    for kc in range(KC):
                        P.op("tensor", lambda e, kc=kc, blk=blk, kvb=kvb: e.matmul(
                            bank(kvb), hap[:, kc, blk * 128:(blk + 1) * 128], wbuf[0][:, kc, :],
                            start=(kc == 0), stop=(kc == KC - 1)),
                            reads=[("hT", hname, kc), ("wbuf", 0)], writes=[psr(kvb)], signal=(kc == KC - 1))
                    srcap, sres = bank(kvb), psr(kvb)
                else:
                    xq, qres, jcol = newpath
                    for kc in range(KC):
                        P.op("tensor", lambda e, kc=kc, blk=blk, kvb=kvb: e.matmul(
                            bank(kvb), xq[:, kc, blk * 128:(blk + 1) * 128], wbuf[1][:, kc, :],
                            start=(kc == 0), stop=(kc == KC - 1)),
                            reads=[qres, ("wkv2", kc)], writes=[psr(kvb)], signal=(kc == KC - 1))
                    kr = kraw[g % 2]
                    sres = ("kraw", g % 2)
                    P.op("vector", lambda e, kvb=kvb, kr=kr, jcol=jcol: e.scalar_tensor_tensor(
                        out=kr, in0=bank(kvb), scalar=rsx[:, jcol:jcol + 1], in1=bias_sb, op0=ALU.mult, op1=ALU.add),
                        reads=[psr(kvb), ("rsx", jcol), "bias_sb"], writes=[sres])
                    srcap = kr
                chunk = key0 // 128 + blk
                for h in range(2):
                    P.op("scalar", lambda e, h=h: e.activation(
                        out=junk[:, h * 128:(h + 1) * 128], in_=srcap[:, h * 128:(h + 1) * 128], func=AF.Square,
                        accum_out=ss[:, h:h + 1]),
                        reads=[sres], writes=["ss", "junk"])
                P.op("scalar", lambda e: e.activation(out=rstd_s[:, 0:2], in_=ss[:, 0:2], func=AF.Ln,
                                                      bias=eps_sb[:, 0:1], scale=1.0 / 128),
                     reads=["ss", "eps"], writes=["rstd_s"])
                P.op("scalar", lambda e: e.activation(out=rstd_s[:, 0:2], in_=rstd_s[:, 0:2], func=AF.Exp, scale=-0.5),
                     reads=["rstd_s"], writes=["rstd_s"])
                if newpath is None:
                    P.op("scalar", lambda e, chunk=chunk: e.activation(
                        out=V[:, chunk, :], in_=srcap[:, 256:512], func=AF.Copy),
                        reads=[sres], writes=["V"])
                else:
                    P.op("scalar", lambda e, chunk=chunk: e.activation(
                        out=V[:, chunk, :], in_=srcap[:, 256:512], func=AF.Copy),
                        reads=[sres], writes=["V"])
                P.op("vector", lambda e, knb=knb: e.tensor_tensor(
                    out=knb[:, 0:2, :], in0=srcap[:, 0:256].rearrange("p (h n) -> p h n", h=2),
                    in1=rstd_s[:, 0:2].unsqueeze(2).broadcast_to([128, 2, 128]), op=ALU.mult),
                    reads=[sres, "rstd_s"], writes=[kres])
                if latent:
                    P.op("vector", lambda e, knb=knb: e.tensor_tensor(
                        out=knb[:, 0:2, :], in0=knb[:, 0:2, :], in1=kg_sb[:].unsqueeze(1).broadcast_to([128, 2, 128]),
                        op=ALU.mult), reads=[kres, "kg"], writes=[kres])
                    if newpath is not None and g % 2 == 1:
                        emit_rope(P, knb, t1[:, 2:4, :], t2[:, 2:4, :], rb, cs[0][:, blk, :], cs[1][:, blk, :], 2,
                                  [cs[2], cs[3]], rres, kres, eng="vector", tres=("t1u", "t2u"))
                    else:
                        emit_rope(P, knb, t1, t2, rb, cs[0][:, blk, :], cs[1][:, blk, :], 2,
                                  [cs[2], cs[3]], rres, kres)
                else:
                    P.op("vector", lambda e, rb=rb, knb=knb: e.tensor_tensor(
                        out=rb[:, 0:2, :], in0=knb[:, 0:2, :], in1=kg_sb[:].unsqueeze(1).broadcast_to([128, 2, 128]),
                        op=ALU.mult), reads=[kres, "kg"], writes=[rres])

                def trans(rb=rb, rres=rres, tb=tb, kpos=key0 + blk * 128):
                    for h in range(2):
                        P.op("tensor", lambda e, h=h: e.transpose(
                            bank_bf(tb)[:, h * 128:(h + 1) * 128], rb[:, h, :], ident_bf[:]),
                            reads=[rres, "identbf"], writes=[psr(tb)], signal=(h == 1))
                    P.op("vector", lambda e: e.tensor_copy(
                        out=KT[:, :, kpos:kpos + 128],
                        in_=bank_bf(tb)[:, 0:256].rearrange("p (h n) -> p h n", h=2)),
                        reads=[psr(tb)], writes=["KT"])

                flush_pending(1)
                pending.append(trans)

        xbf = [big[:, q * 4096:(q + 1) * 4096].bitcast(BF16).rearrange("p (k n) -> p k n", k=KC) for q in range(4)]
        kraw = [work[:, 0:512], work[:, 3456:3968]]
        bias_sb = work[:, 4864:5376]
        junkf = work[:, 3328:3456]
        shiftb = env["wg"][:].rearrange("p k (a n) -> p (k a) n", a=2)[:, 0:16, :]
        NT = 6
        wg_f = env["wg"][:].rearrange("p k n -> p (k n)").bitcast(F32)
        cosb2 = [wg_f[:, i * 1024:i * 1024 + 512].rearrange("p (b d) -> p b d", b=4) for i in range(2)]
        sinb2 = [wg_f[:, i * 1024 + 512:(i + 1) * 1024].rearrange("p (b d) -> p b d", b=4) for i in range(2)]
        CS2 = [("cos2", 0), ("sin2", 0), ("cos2", 1), ("sin2", 1)]

        def xq_res(q):
            return ("xq", q)

        def newpath_load(t):
            if t >= NT:
                return
            q = 2 + t % 2
            P.dma("gpsimd", f"xq{q}", xbf[q][:], xT_v[:, :, t * 512:(t + 1) * 512], writes=[xq_res(q)])

        def cs_load(t):
            if t >= NT:
                return
            tp = t % 2
            P.dma("sync", f"c2{tp}", cosb2[tp][:], cosF[t * 512:(t + 1) * 512, :].rearrange("(b p) d -> p b d", p=128),
                  writes=[("cos2", tp)])
            P.dma("sync", f"s2{tp}", sinb2[tp][:], sinF[t * 512:(t + 1) * 512, :].rearrange("(b p) d -> p b d", p=128),
                  writes=[("sin2", tp)])

        def _unused_cs_load(t):
            tp = t % 2
            P.dma("sync", f"c{tp}", cosb[tp][:], cosF[t * 512:(t + 1) * 512, :].rearrange("(b p) d -> p b d", p=128),
                  writes=[("cos", tp)])
            P.dma("sync", f"s{tp}", sinb[tp][:], sinF[t * 512:(t + 1) * 512, :].rearrange("(b p) d -> p b d", p=128),
                  writes=[("sin", tp)])

        def prep_newpath():
            for t in range(2):
                newpath_load(t)
            P.op("vector", lambda e: e.tensor_copy(
                out=shiftb, in_=m_sb[:, 0:16, 0:1].broadcast_to([128, 16, 128])),
                reads=[("m", 7)], writes=["wg"])
            for kc in range(KC):
                P.op("tensor", lambda e, kc=kc: e.matmul(bank(3), shiftb[:, kc, :], wbuf[0][:, kc, :],
                                                         start=(kc == 0), stop=(kc == KC - 1)),
                     reads=["wg", ("wbuf", 0)], writes=[psr(3)], signal=(kc == KC - 1))
            P.op("scalar", lambda e: e.activation(out=bias_sb, in_=bank(3), func=AF.Copy),
                 reads=[psr(3)], writes=["bias_sb"])
            P.inherit(CS2, ["wg"])
            cs_load(0)
            cs_load(1)
            P.inherit([("wkv2", kc) for kc in range(KC)], [("wbuf", 1)])
            for kc in range(KC):
                if kc % 2 == 0:
                    P.op("scalar", lambda e, kc=kc: e.activation(
                        out=wbuf[1][:, kc, :], in_=wbuf[0][:, kc, :], func=AF.Copy, scale=gm[:, kc, 0:1]),
                        reads=[("wbuf", 0), "gm"], writes=[("wkv2", kc)])
                else:
                    P.op("vector", lambda e, kc=kc: e.tensor_scalar(
                        out=wbuf[1][:, kc, :], in0=wbuf[0][:, kc, :], scalar1=gm[:, kc, 0:1], scalar2=None,
                        op0=ALU.mult), reads=[("wbuf", 0), "gm"], writes=[("wkv2", kc)])

        def gram(j):
            t, blk = divmod(j, 4)
            q = 2 + t % 2
            gb = 2
            jc = j % 8
            for kc in range(KC):
                P.op("tensor", lambda e, kc=kc: e.matmul(
                    bank(gb)[:, 0:128], xbf[q][:, kc, blk * 128:(blk + 1) * 128],
                    xbf[q][:, kc, blk * 128:(blk + 1) * 128], start=(kc == 0), stop=(kc == KC - 1)),
                    reads=[xq_res(q)], writes=[psr(gb)], signal=(kc == KC - 1))
            P.op("vector", lambda e: e.scalar_tensor_tensor(
                out=junkf, in0=bank(gb)[:, 0:128], scalar=1.0, in1=ident_f[:], op0=ALU.mult, op1=ALU.mult,
                accum_out=ssx[:, jc:jc + 1]),
                reads=[psr(gb), "identf"], writes=[("ssx", jc), "junkf"])
            P.op("scalar", lambda e: e.activation(out=rsx[:, jc:jc + 1], in_=ssx[:, jc:jc + 1], func=AF.Ln,
                                                  bias=eps_sb[:, 0:1], scale=1.0 / D),
                 reads=[("ssx", jc), "eps"], writes=[("rsx", jc)])
            P.op("scalar", lambda e: e.activation(out=rsx[:, jc:jc + 1], in_=rsx[:, jc:jc + 1], func=AF.Exp, scale=-0.5),
                 reads=[("rsx", jc)], writes=[("rsx", jc)])

        def wload_q(n):
            P.dma("gpsimd", f"xq{2 + n}", xbf[2 + n][:], w_in_v[:, :, n * 512:(n + 1) * 512], writes=[xq_res(2 + n)])

        def run_newpath(old_steps):
            nb = NT * 4
            gram(0)
            for j in range(nb):
                if j + 1 < nb:
                    gram(j + 1)
                t, blk = divmod(j, 4)
                q = 2 + t % 2
                kv_block(blk, None, None, True, CTX + t * 512, t % 2, newpath=(xbf[q], xq_res(q), j % 8),
                         cs=(cosb2[t % 2], sinb2[t % 2], ("cos2", t % 2), ("sin2", t % 2)))
                if blk == 3:
                    newpath_load(t + 2)
                    cs_load(t + 2)
                    if t >= NT - 2:
                        wload_q(t - (NT - 2))
                if old_steps:
                    old_steps.pop(0)()
            while old_steps:
                old_steps.pop(0)()

        prep_newpath()
        fa = [front_a(i) for i in range(4)]
        kb = [kv(i) for i in range(4)]
        F1, F2, F3 = 0, 1, 2
        CTXI, HALO, OWN0, OWN1 = 0, 1, 2, 3
        for st in (fa[CTXI][F1], fa[HALO][F1], fa[CTXI][F2], fa[CTXI][F3], fa[HALO][F2], fa[HALO][F3],
                   lambda: front_b(CTXI), lambda: front_b(HALO), kb[CTXI][0], kb[CTXI][1]):
            st()
        sched = {
            1: [lambda: fa[OWN0][F1](0)],
            2: [lambda: fa[OWN0][F1](1)],
            4: [fa[OWN0][F2]],
            5: [lambda: fa[OWN0][F3](0)],
            6: [lambda: fa[OWN0][F3](1)],
            7: [lambda: front_b(OWN0)],
            8: [kb[OWN0][0]],
            9: [kb[OWN0][1]],
            10: [kb[OWN0][2], lambda: fa[OWN1][F1](0)],
            11: [kb[OWN0][3], lambda: fa[OWN1][F1](1)],
            13: [fa[OWN1][F2]],
            14: [lambda: fa[OWN1][F3](0)],
            15: [lambda: fa[OWN1][F3](1)],
            16: [lambda: front_b(OWN1)],
            17: [kb[OWN1][0]],
            18: [kb[OWN1][1]],
            19: [kb[OWN1][2]],
            20: [kb[OWN1][3]],
        }
        old_steps = []
        for slot in range(24):
            steps = sched.get(slot, [])
            old_steps.append(lambda steps=steps: [s() for s in steps])
        run_newpath(old_steps)
        P.inherit(["wg"], CS2)

        chk("BC")
        P.inherit(["t1b", "t2b"], ["rstd_bc", ("kraw", 0), ("kraw", 1)])
        P.inherit([("rot", 3)], ["junkf", ("kraw", 1)])
        P.inherit(["t1", "t2"], ["t1u", "t2u"])
        P.inherit([("wbuf", 1)], [("wkv2", kc) for kc in range(KC)])
        P.inherit(["qT", "sga"], [("xbuf", 0), ("xq", 0), ("xq", 1)])
        P.inherit(["mixT"], [("xbuf", 1), ("xq", 2), ("xq", 3)])
        wsrc = [w_in_v[:, :, 0:512], w_in_v[:, :, 512:1024], w_in_v[:, :, 1536:2048], w_in_v[:, :, 2048:2560]] + \
               [w_conv_v[:, :, i * 512:(i + 1) * 512] for i in range(8)]

        def wtile(n):
            if n < 2:
                return xbf[2 + n], xq_res(2 + n), f"xq{2 + n}"
            return wbuf[n % 2], ("wbuf", n % 2), f"w{n % 2}"

        def wload(n):
            if n < len(wsrc):
                wb_, wres_, key_ = wtile(n)
                P.dma("gpsimd", key_, wb_[:], wsrc[n], writes=[wres_])

        wload(2)
        wload(3)
        for half in range(2):
            for blk in range(8):
                g = gblk[0]
                gblk[0] += 1
                qb = [2, 3, 4, 5, 0, 1][g % 6]
                tb = 6 + (g % 2)
                knb = kn2[g % 2]
                rb = rot_bf[g % 4]
                rres = ("rot", g % 4)
                kres = ("kn", g % 2)
                for kc in range(KC):
                    P.op("tensor", lambda e, kc=kc, blk=blk, qb=qb, half=half: e.matmul(
                        bank(qb), hT_own[:, kc, blk * 128:(blk + 1) * 128], xbf[2 + half][:, kc, :],
                        start=(kc == 0), stop=(kc == KC - 1)),
                        reads=[("hT", "A" if blk < 4 else "B", kc), xq_res(2 + half)], writes=[psr(qb)],
                        signal=(kc == KC - 1))
                if half == 0 and blk == 0:
                    flush_pending(0)
                for h in range(4):
                    P.op("scalar", lambda e, qb=qb, h=h: e.activation(
                        out=junk[:, h * 128:(h + 1) * 128], in_=bank(qb)[:, h * 128:(h + 1) * 128], func=AF.Square,
                        accum_out=ss[:, h:h + 1]),
                        reads=[psr(qb)], writes=["ss", "junk"])
                P.op("scalar", lambda e: e.activation(out=rstd_s[:, 0:4], in_=ss[:, 0:4], func=AF.Ln,
                                                      bias=eps_sb[:, 0:1], scale=1.0 / 128),
                     reads=["ss", "eps"], writes=["rstd_s"])
                P.op("scalar", lambda e: e.activation(out=rstd_s[:, 0:4], in_=rstd_s[:, 0:4], func=AF.Exp, scale=-0.5),
                     reads=["rstd_s"], writes=["rstd_s"])
                P.op("vector", lambda e, qb=qb, knb=knb: e.tensor_tensor(
                    out=knb[:], in0=bank(qb).rearrange("p (h n) -> p h n", h=4),
                    in1=rstd_s[:, 0:4].unsqueeze(2).broadcast_to([128, 4, 128]), op=ALU.mult),
                    reads=[psr(qb), "rstd_s"], writes=[kres])
                P.op("vector", lambda e, knb=knb: e.tensor_tensor(
                    out=knb[:], in0=knb[:], in1=qg_sb[:].unsqueeze(1).broadcast_to([128, 4, 128]), op=ALU.mult),
                    reads=[kres, "qg"], writes=[kres])
                tp = (6 + blk // 4) % 2
                if g % 2 == 0:
                    emit_rope(P, knb, t1, t2, rb, cosb[tp][:, blk % 4, :], sinb[tp][:, blk % 4, :], 4,
                              [("cos", tp), ("sin", tp)], rres, kres)
                else:
                    emit_rope(P, knb, t1b, t2b, rb, cosb[tp][:, blk % 4, :], sinb[tp][:, blk % 4, :], 4,
                              [("cos", tp), ("sin", tp)], rres, kres, eng="vector", tres=("t1b", "t2b"))

                def transq(rb=rb, rres=rres, tb=tb, half=half, blk=blk):
                    for h in range(4):
                        P.op("tensor", lambda e, h=h: e.transpose(
                            bank_bf(tb)[:, h * 128:(h + 1) * 128], rb[:, h, :], ident_bf[:]),
                            reads=[rres, "identbf"], writes=[psr(tb)], signal=(h == 3))
                    P.op("vector", lambda e: e.tensor_copy(
                        out=qT[:, half * 4:(half + 1) * 4, blk * 128:(blk + 1) * 128],
                        in_=bank_bf(tb)[:, 0:512].rearrange("p (h n) -> p h n", h=4)),
                        reads=[psr(tb)], writes=["qT"])

                flush_pending(2)
                pending.append(transq)

        chk("D1")
        bctr = [0]

        def next_bank():
            b = bctr[0] % 7
            bctr[0] += 1
            return b

        pv7 = bank(7)[:, 0:96].rearrange("p (c r) -> p c r", r=2)

        def gate_dma(n):
            if n < 8:
                P.dma("gpsimd", "wg", wg[:], w_mod_v[:, :, 4096 + n * 256:4096 + (n + 1) * 256], writes=["wg"])

        def gate_mm(n):
            for c2 in range(2):
                cidx = 32 + n * 2 + c2
                for kc in range(KC):
                    P.op("tensor", lambda e, c2=c2, cidx=cidx, kc=kc: e.matmul(
                        pv7[:, cidx, :], wg[:, kc, c2 * 128:(c2 + 1) * 128], s_bf[:, kc, :],
                        start=(kc == 0), stop=(kc == KC - 1)),
                        reads=["wg", "s_bf"], writes=[psr(7)], signal=(c2 == 1 and kc == KC - 1))
            if n == 7:
                adaln_evac(pv7, 7, 32, 48, 11)

        for grp in range(2):
            wb = wbuf[grp % 2]
            if grp == 0:
                gate_dma(0)
            else:
                gate_mm(0)
                gate_dma(1)
            for c4 in range(4):
                head = grp * 4 + c4
                for tt in range(2):
                    b = next_bank()
                    for kc in range(KC):
                        P.op("tensor", lambda e, wb=wb, kc=kc, c4=c4, tt=tt, b=b: e.matmul(
                            bank(b), wb[:, kc, c4 * 128:(c4 + 1) * 128], hT_own[:, kc, tt * 512:(tt + 1) * 512],
                            start=(kc == 0), stop=(kc == KC - 1)),
                            reads=[("hT", "AB"[tt], kc), ("wbuf", grp % 2)], writes=[psr(b)], signal=(kc == KC - 1))
                    P.op("scalar", lambda e, b=b, head=head, tt=tt: e.activation(
                        out=sga[:, head, tt * 512:(tt + 1) * 512], in_=bank(b), func=AF.Silu),
                        reads=[psr(b)], writes=["sga"])
                if grp == 0 and c4 == 0:
                    flush_pending(0)
            wload(grp + 4)

        chk("D2")
        W_BC = ["rstd_bc", ("kn", 0), ("kn", 1), "t1", "t2", ("rot", 0), ("rot", 1), ("rot", 2), ("rot", 3), "junk", "t1u", "t2u",
                "t1b", "t2b", ("kraw", 0), ("kraw", 1), "bias_sb", "junkf"]
        W_D3 = ["cg", "u", "acc_c", "acc2", "sgc"]
        P.inherit(W_D3, W_BC)
        for i in range(8):
            wpar = i % 2
            wb4 = wbuf[wpar].rearrange("p k (g n) -> p k g n", g=4)
            if i < 7:
                gate_mm(i + 1)
                gate_dma(i + 2)
            banks = {}
            bh = next_bank()
            for g in (1, 2, 0, 3):
                for tt in range(2):
                    b = next_bank()
                    banks[(g, tt)] = b
                    for kc in range(KC):
                        P.op("tensor", lambda e, wb4=wb4, kc=kc, g=g, tt=tt, b=b: e.matmul(
                            bank(b), wb4[:, kc, g, :], hT_own[:, kc, tt * 512:(tt + 1) * 512],
                            start=(kc == 0), stop=(kc == KC - 1)),
                            reads=[("hT", "AB"[tt], kc), ("wbuf", wpar)], writes=[psr(b)], signal=(kc == KC - 1))
                        if g in (1, 2) and tt == 1:
                            off = 0 if g == 1 else 2
                            P.op("tensor", lambda e, wb4=wb4, kc=kc, g=g, off=off, bh=bh: e.matmul(
                                bank(bh)[:, off:off + 2], wb4[:, kc, g, :], hH[:, kc, :],
                                start=(kc == 0), stop=(kc == KC - 1)),
                                reads=[("hT", "H", kc), ("wbuf", wpar)], writes=[psr(bh)], signal=(kc == KC - 1))
                    if g == 1:
                        P.op("scalar", lambda e, b=b, tt=tt: e.activation(
                            out=cg_sb[:, 1 + tt * 512:1 + (tt + 1) * 512], in_=bank(b), func=AF.Copy),
                            reads=[psr(b)], writes=["cg"])
                    elif g == 2:
                        P.op("vector", lambda e, b=b, tt=tt: e.tensor_tensor(
                            out=u_sb[:, 1 + tt * 512:1 + (tt + 1) * 512], in0=bank(b),
                            in1=cg_sb[:, 1 + tt * 512:1 + (tt + 1) * 512], op=ALU.mult),
                            reads=[psr(b), "cg"], writes=["u"])
                    elif g == 3:
                        P.op("scalar", lambda e, b=b, tt=tt: e.activation(
                            out=sgc[:, tt * 512:(tt + 1) * 512], in_=bank(b), func=AF.Silu),
                            reads=[psr(b)], writes=["sgc"])
                if g == 2:
                    P.op("scalar", lambda e, bh=bh: e.activation(
                        out=cg_sb[:, 0:1026:1025], in_=bank(bh)[:, 0:2], func=AF.Copy),
                        reads=[psr(bh)], writes=["cg"])
                    P.op("vector", lambda e, bh=bh: e.tensor_tensor(
                        out=u_sb[:, 0:1026:1025], in0=bank(bh)[:, 2:4], in1=cg_sb[:, 0:1026:1025], op=ALU.mult),
                        reads=[psr(bh), "cg"], writes=["u"])
                    P.op("vector", lambda e: e.tensor_tensor(
                        out=u_sb[:, 0:1026:1025], in0=u_sb[:, 0:1026:1025], in1=hmask_sb[:], op=ALU.mult),
                        reads=["u", "hmask"], writes=["u"])
                    cw = convw_sb[:, i * 3:(i + 1) * 3]
                    P.op("vector", lambda e, cw=cw: e.tensor_scalar(
                        out=acc_c, in0=u_sb[:, 1:1025], scalar1=cw[:, 1:2], scalar2=None, op0=ALU.mult),
                        reads=["u", "convw"], writes=["acc_c"])
                    P.op("vector", lambda e, cw=cw: e.scalar_tensor_tensor(
                        out=acc_c, in0=u_sb[:, 0:1024], scalar=cw[:, 0:1], in1=acc_c, op0=ALU.mult, op1=ALU.add),
                        reads=["u", "convw", "acc_c"], writes=["acc_c"])
                    P.op("vector", lambda e, cw=cw: e.scalar_tensor_tensor(
                        out=acc_c, in0=u_sb[:, 2:1026], scalar=cw[:, 2:3], in1=acc_c, op0=ALU.mult, op1=ALU.add),
                        reads=["u", "convw", "acc_c"], writes=["acc_c"])
                if g == 0:
                    for tt in range(2):
                        b = banks[(0, tt)]
                        P.op("vector", lambda e, b=b, tt=tt: e.tensor_tensor(
                            out=acc2[:, tt * 512:(tt + 1) * 512], in0=bank(b), in1=acc_c[:, tt * 512:(tt + 1) * 512],
                            op=ALU.mult), reads=[psr(b), "acc_c"], writes=["acc2"])
            wload(i + 6)
            P.op("gpsimd", lambda e, i=i: e.tensor_tensor(
                out=mixT[:, 8 + i, :], in0=acc2, in1=sgc, op=ALU.mult),
                reads=["acc2", "sgc"], writes=["mixT"])

        chk("D3")
        W_E = [("PT", 0), ("PT", 1), ("tmpb", 0), ("tmpb", 1), "rs", "ot", "lns", "o_sb"]
        P.inherit(W_E, W_D3)
        for c in range(4):
            P.dma("gpsimd", f"wout{c}", woutb[c][:], w_out_v[:, :, c * 512:(c + 1) * 512],
                  writes=R2_res + [("wout", c)])

        sm_scale = 1.0 / math.sqrt(128.0)
        iters = [(g, tt, hq) for g in range(2) for tt in range(2) for hq in range(4)]
        items = [(it, p) for it in range(len(iters)) for p in range(NCHUNK // 2)]

        def emit_S(idx):
            it, p = items[idx]
            g, tt, hq = iters[it]
            head = g * 4 + hq
            sp = idx % 3
            for cl in range(2):
                chunk = p * 2 + cl
                b = sp * 2 + cl
                P.op("tensor", lambda e, b=b, g=g, chunk=chunk, head=head, tt=tt: e.matmul(
                    bank(b), KT[:, g, chunk * 128:(chunk + 1) * 128], qT[:, head, tt * 512:(tt + 1) * 512],
                    start=True, stop=True),
                    reads=["KT", "qT"], writes=[psr(b)], signal=(cl == 1))

        def emit_norm(it):
            g, tt, hq = iters[it]
            head = g * 4 + hq
            ob = 6
            sb_ = 7
            P.op("scalar", lambda e: e.activation(out=lns_sb, in_=bank(sb_), func=AF.Ln), reads=[psr(sb_)], writes=["lns"])
            P.op("vector", lambda e: e.tensor_copy(out=o_sb, in_=bank(ob)), reads=[psr(ob)], writes=["o_sb"])
            P.op("scalar", lambda e: e.activation(out=rs_sb, in_=lns_sb, func=AF.Exp, scale=-1.0),
                 reads=["lns"], writes=["rs"])
            P.op("vector", lambda e: e.tensor_tensor(out=ot_sb, in0=o_sb, in1=rs_sb, op=ALU.mult),
                 reads=["o_sb", "rs"], writes=["ot"])
            P.op("gpsimd", lambda e: e.tensor_tensor(
                out=mixT[:, head, tt * 512:(tt + 1) * 512], in0=ot_sb, in1=sga[:, head, tt * 512:(tt + 1) * 512],
                op=ALU.mult), reads=["ot", "sga"], writes=["mixT"])

        def emit_summm(idx):
            it, p = items[idx]
            tq = idx % 2
            sb_ = 7
            P.op("tensor", lambda e: e.matmul(bank(sb_), ones_bf[:], tmpb[tq], start=(p == 0), stop=(p == NP - 1)),
                 reads=[("tmpb", tq), "ones_bf"], writes=[psr(sb_)])

        NP = NCHUNK // 2
        emit_S(0)
        emit_S(1)
        for idx, (it, p) in enumerate(items):
            g, tt, hq = iters[it]
            sp = idx % 3
            pp = idx % 2
            if idx + 2 < len(items):
                emit_S(idx + 2)
            P.op("scalar", lambda e, sp=sp, pp=pp: e.activation(out=PT[pp], in_=PS[sp][:], func=AF.Exp, scale=sm_scale,
                                                                bias=smx[:, 2:3]),
                 reads=[psr(sp * 2), psr(sp * 2 + 1), "smx2"], writes=[("PT", pp)])
            ob = 6
            if idx > 0:
                emit_summm(idx - 1)
                if items[idx - 1][1] == NP - 1:
                    emit_norm(items[idx - 1][0])
            for cl in range(2):
                chunk = p * 2 + cl
                P.op("tensor", lambda e, ob=ob, chunk=chunk, g=g, pp=pp, cl=cl: e.matmul(
                    bank(ob), V[:, chunk, g * 128:(g + 1) * 128], PT[pp][:, cl * 512:(cl + 1) * 512],
                    start=(chunk == 0), stop=(chunk == NCHUNK - 1)),
                    reads=["V", ("PT", pp)], writes=[psr(ob)], signal=(cl == 1))
            tq = idx % 2
            P.op("vector", lambda e, pp=pp, tq=tq: e.tensor_tensor(
                out=tmpb[tq], in0=PT[pp][:, 0:512], in1=PT[pp][:, 512:1024], op=ALU.add),
                reads=[("PT", pp)], writes=[("tmpb", tq)])
        emit_summm(len(items) - 1)
        emit_norm(len(iters) - 1)

        chk("E")
        W_F = ["fg", "gate_bc", ("diag", 0), ("diag", 1)]
        P.inherit(W_F, W_E)
        P.inherit([("xo", 0), ("xo", 1)], ["qT"])
        P.inherit([("res", 0), ("res", 1)], ["sga"])
        P.dma("sync", "fgk", fg_sb, fg, writes=["fg"])
        for blk0 in range(2):
            P.dma("sync", f"xo{blk0}", xo[blk0], xown[blk0 * 128:(blk0 + 1) * 128, :], writes=[("xo", blk0)])
        def build_gate_bc():
            for q4 in range(4):
                db = diagb[q4 % 2]
                for a in range(4):
                    kc = q4 * 4 + a
                    P.op("vector", lambda e, db=db, a=a, kc=kc: e.tensor_scalar(
                        out=db[:, a, :], in0=ident_f[:], scalar1=m_sb[:, 32 + kc, 0:1], scalar2=None, op0=ALU.mult),
                        reads=["identf", ("m", 11)], writes=[("diag", q4 % 2)])
                for a in range(4):
                    P.op("tensor", lambda e, db=db, a=a: e.matmul(
                        bank(7)[:, a * 128:(a + 1) * 128], ones_f[:], db[:, a, :], start=True, stop=True),
                        reads=[("diag", q4 % 2), "ones_f"], writes=[psr(7)], signal=(a == 3))
                P.op("scalar", lambda e, q4=q4: e.activation(out=gate_bc[:, q4 * 512:(q4 + 1) * 512], in_=bank(7), func=AF.Copy),
                     reads=[psr(7)], writes=["gate_bc"])


        for blk in range(8):
            par = blk % 2
            for c in range(4):
                b = par * 4 + c
                for kc in range(KC):
                    P.op("tensor", lambda e, b=b, kc=kc, blk=blk, c=c: e.matmul(
                        bank(b), mixT[:, kc, blk * 128:(blk + 1) * 128], woutb[c][:, kc, :],
                        start=(kc == 0), stop=(kc == KC - 1)),
                        reads=["mixT", ("wout", c)], writes=[psr(b)], signal=(kc == KC - 1))
            rsb = resb[par]
            if blk == 0:
                build_gate_bc()
            for hh in range(2):
                P.op("vector", lambda e, rsb=rsb, par=par, hh=hh: e.tensor_tensor(
                    out=rsb[:, hh * 1024:(hh + 1) * 1024], in0=PS[par * 2 + hh][:],
                    in1=gate_bc[:, hh * 1024:(hh + 1) * 1024], op=ALU.mult),
                    reads=[psr(par * 4 + hh * 2), psr(par * 4 + hh * 2 + 1), "gate_bc"], writes=[("res", par)])
            P.op("vector", lambda e, rsb=rsb, par=par: e.tensor_tensor(out=rsb, in0=rsb, in1=xo[par], op=ALU.add),
                 reads=[("res", par), ("xo", par)], writes=[("res", par)])
            P.op("scalar", lambda e, rsb=rsb, par=par: e.activation(out=xo[par], in_=rsb, func=AF.Square,
                                                                   accum_out=ss[:, 0:1]),
                 reads=[("res", par)], writes=["ss", ("xo", par)])
            if blk + 2 < 8:
                P.dma("sync", f"xo{par}", xo[par], xown[(blk + 2) * 128:(blk + 3) * 128, :], writes=[("xo", par)])
            P.op("scalar", lambda e: e.activation(out=rstd_s[:, 0:1], in_=ss[:, 0:1], func=AF.Ln,
                                                  bias=eps_sb[:, 0:1], scale=1.0 / D),
                 reads=["ss", "eps"], writes=["rstd_s"])
            P.op("scalar", lambda e: e.activation(out=rstd_s[:, 0:1], in_=rstd_s[:, 0:1], func=AF.Exp, scale=-0.5),
                 reads=["rstd_s"], writes=["rstd_s"])
            P.op("vector", lambda e, rsb=rsb: e.scalar_tensor_tensor(
                out=rsb, in0=rsb, scalar=rstd_s[:, 0:1], in1=fg_sb, op0=ALU.mult, op1=ALU.mult),
                reads=[("res", par), "rstd_s", "fg"], writes=[("res", par)])
            P.dma("sync", f"o{par}", out[blk * 128:(blk + 1) * 128, :], rsb, reads=[("res", par)], writes=[])
        P.final_wait("sync", ["o0", "o1"])


def emit_rope(P, kn, t1, t2, rb, cos_blk, sin_blk, H, tab_res, rot_res, kn_res="kn", eng="gpsimd", tres=("t1", "t2")):
    knv = kn[:, 0:H, :].rearrange("p h (a b j) -> p h a b j", a=2, b=2)
    t2v = t2[:, 0:H, :].rearrange("p h (a b j) -> p h a b j", a=2, b=2)
    sv = sin_blk.rearrange("p (a b j) -> p a b j", a=2, b=2)
    P.op(eng, lambda e: e.tensor_tensor(
        out=t1[:, 0:H, :], in0=kn[:, 0:H, :], in1=cos_blk.unsqueeze(1).broadcast_to([128, H, 128]), op=ALU.mult),
        reads=[kn_res] + tab_res, writes=[tres[0]])
    for bsel in range(2):
        P.op(eng, lambda e, bsel=bsel: e.tensor_tensor(
            out=t2v[:, :, :, bsel, :], in0=knv[:, :, :, 1 - bsel, :],
            in1=sv[:, :, bsel, :].unsqueeze(1).broadcast_to([128, H, 2, 32]), op=ALU.mult),
            reads=[kn_res] + tab_res, writes=[(tres[1], bsel)])
    P.op(eng, lambda e: e.tensor_tensor(out=rb[:, 0:H, :], in0=t1[:, 0:H, :], in1=t2[:, 0:H, :], op=ALU.add),
         reads=[tres[0], (tres[1], 0), (tres[1], 1)], writes=[rot_res, tres[1]])


_NC_CACHE = {}


def _rope_tables():
    quarter = 32
    inv = (10000.0 ** (-np.arange(quarter, dtype=np.float32) / quarter)).astype(np.float32)
    t = np.arange(SEQ)
    row = (t // 64).astype(np.float32)
    col = (t % 64).astype(np.float32)
    ar = row[:, None] * inv[None, :]
    ac = col[:, None] * inv[None, :]
    cr, sr, c_c, s_c = np.cos(ar), np.sin(ar), np.cos(ac), np.sin(ac)
    cosF = np.concatenate([cr, cr, c_c, c_c], axis=1).astype(np.float32)
    sinF = np.concatenate([-sr, sr, -s_c, s_c], axis=1).astype(np.float32)
    return cosF, sinF


def kernel(x, c, ctx, c_ctx, w_mod, b_mod, norm_g, w_in, q_norm_g, k_norm_g, conv_w, w_out, final_norm_g):
    f = lambda a: np.ascontiguousarray(np.asarray(a, dtype=np.float32))
    x, c, ctx, c_ctx = f(x), f(c), f(ctx), f(c_ctx)
    w_mod0, w_in0, w_out0 = f(w_mod[0]), f(w_in[0]), f(w_out[0])
    b_mod0, ng0, qg0, kg0, cw0, fg0 = f(b_mod[0]), f(norm_g[0]), f(q_norm_g[0]), f(k_norm_g[0]), f(conv_w[0]), f(final_norm_g)
    if "nc" not in _NC_CACHE:
        _NC_CACHE["nc"] = build_program()
    nc = _NC_CACHE["nc"]
    cosF, sinF = _rope_tables()
    shared = {
        "w_mod": w_mod0, "w_in": np.ascontiguousarray(w_in0[:, :2560]), "w_out": w_out0,
        "w_conv": np.ascontiguousarray(
            w_in0[:, 2560:].reshape(D, 4, 8, 128).transpose(0, 2, 1, 3).reshape(D, 4096)),
        "bmod": np.ascontiguousarray(b_mod0.reshape(48, 128).T),
        "ng": np.ascontiguousarray(ng0.reshape(16, 128).T),
        "qg": np.ascontiguousarray(np.broadcast_to(qg0[None, :], (128, 128))),
        "kg": np.ascontiguousarray(np.broadcast_to(kg0[None, :], (128, 128))),
        "convw": np.ascontiguousarray(cw0.reshape(3, 8, 128).transpose(2, 1, 0).reshape(128, 24)),
        "fg": np.ascontiguousarray(np.broadcast_to(fg0[None, :], (128, D))),
        "ident": np.eye(128, dtype=np.float32),
    }
    in_maps = []
    for core in range(8):
        b, j = core // 4, core % 4
        t0 = j * OWN
        order = np.concatenate([np.arange(0, t0), np.arange(t0 + OWN, SEQ), np.arange(t0, t0 + OWN)])
        xb = x[b]
        xTc = np.ascontiguousarray(xb[order].T)
        halo = np.zeros((2, D), np.float32)
        hm = np.zeros((128, 2), np.float32)
        if t0 > 0:
            halo[0] = xb[t0 - 1]
            hm[:, 0] = 1.0
        if t0 + OWN < SEQ:
            halo[1] = xb[t0 + OWN]
            hm[:, 1] = 1.0
        ccm = np.stack([c[b].reshape(16, 128).T, c_ctx.reshape(16, 128).T], axis=2).reshape(128, 32)
        m = dict(shared)
        m.update({
            "xT": xTc, "xTh": np.ascontiguousarray(halo.T), "xown": np.ascontiguousarray(xb[t0:t0 + OWN]),
            "ctxT": np.ascontiguousarray(ctx[b].T), "cc": np.ascontiguousarray(ccm),
            "cosF": np.ascontiguousarray(cosF[order]), "sinF": np.ascontiguousarray(sinF[order]), "hmask": hm,
        })
        in_maps.append(m)
    if _NC_CACHE.get("prep_only"):
        return nc, in_maps
    res = run_bass_kernel_spmd(nc, in_maps, core_ids=list(range(8)))
    outp = np.empty((2, SEQ, D), np.float32)
    for core in range(8):
        b, j = core // 4, core % 4
        outp[b, j * OWN:(j + 1) * OWN] = res.results[core]["out"]
    return outp
```

```python
import math
from contextlib import ExitStack

import numpy as np
import concourse.bass as bass
import concourse.mybir as mybir
from concourse.bass_utils import run_bass_kernel_spmd

F32 = mybir.dt.float32
BF16 = mybir.dt.bfloat16
AF = mybir.ActivationFunctionType
ALU = mybir.AluOpType

D = 2048
KC = 16
SEQ = 4096
CTX = 256
OWN = 1024
NKEY = CTX + SEQ
NCHUNK = NKEY // 128
EPS = 1e-6
ENGS = ["tensor", "scalar", "vector", "gpsimd", "sync"]


class Prog:
    def __init__(self, nc, es):
        self.nc = nc
        self.es = es
        self.streams = {e: [] for e in ENGS}
        self.esem = {e: es.enter_context(nc.semaphore("p_" + e)) for e in ENGS[:4]}
        self.ecount = {e: 0 for e in ENGS[:4]}
        self.dsem = {}
        self.dcount = {}
        self.lastw = {}
        self.readers = {}
        self.waited = {e: {} for e in ENGS}

    def _sem(self, key):
        if key[0] == "e":
            return self.esem[key[1]]
        return self.dsem[key[1]]

    def _waits(self, stream, reads, writes, skip=None):
        need = {}

        def add(tok):
            k, v = tok
            if need.get(k, 0) < v:
                need[k] = v

        for r in reads:
            if r in self.lastw:
                add(self.lastw[r])
        for w in writes:
            if w in self.lastw:
                add(self.lastw[w])
            for k, v in self.readers.get(w, {}).items():
                add((k, v))
        for k, v in need.items():
            if k == skip or k == ("e", "tensor") == ("e", stream):
                continue
            if k[0] == "e":
                assert self.ecount[k[1]] >= v, f"dependency on unsignaled op {k} {v} {self.ecount[k[1]]}"
            if self.waited[stream].get(k, 0) >= v:
                continue
            self.waited[stream][k] = v
            sem = self._sem(k)
            self.streams[stream].append(lambda e, sem=sem, v=v: e.wait_ge(sem, v))

    def _record(self, tok, reads, writes):
        k, v = tok
        for r in reads:
            d = self.readers.setdefault(r, {})
            if d.get(k, 0) < v:
                d[k] = v
        for w in writes:
            self.lastw[w] = tok
            self.readers[w] = {}

    def op(self, eng, fn, reads=(), writes=(), signal=True):
        ps_reads = [r for r in reads if isinstance(r, tuple) and r[0] == "ps"]
        if ps_reads:
            reads = [r for r in reads if r not in ps_reads]
            writes = list(writes) + ps_reads
        self._waits(eng, reads, writes)
        if signal:
            self.ecount[eng] += 1
            tok = (("e", eng), self.ecount[eng])
            sem = self.esem[eng]
            self.streams[eng].append(lambda e, fn=fn, sem=sem: fn(e).then_inc(sem, 1))
        else:
            tok = (("e", eng), self.ecount[eng] + 1)
            self.streams[eng].append(lambda e, fn=fn: fn(e))
        self._record(tok, reads, writes)

    def dma(self, queue, key, out, in_, reads=(), writes=(), cont=False):
        if key not in self.dsem:
            self.dsem[key] = self.es.enter_context(self.nc.semaphore("d_" + key))
            self.dcount[key] = 0
        self._waits(queue, reads, writes, skip=("d", key) if cont else None)
        self.dcount[key] += 16
        tok = (("d", key), self.dcount[key])
        sem = self.dsem[key]
        self.streams[queue].append(
            lambda e, out=out, in_=in_, sem=sem: e.dma_start(out=out, in_=in_).then_inc(sem, 16))
        self._record(tok, reads, writes)

    def inherit(self, news, olds):
        for n in news:
            d = self.readers.setdefault(n, {})
            for o in olds:
                for k, v in self.readers.get(o, {}).items():
                    if d.get(k, 0) < v:
                        d[k] = v
                if o in self.lastw:
                    k, v = self.lastw[o]
                    if d.get(k, 0) < v:
                        d[k] = v

    def final_wait(self, stream, keys):
        for key in keys:
            sem = self.dsem[key]
            v = self.dcount[key]
            self.streams[stream].append(lambda e, sem=sem, v=v: e.wait_ge(sem, v))

    def emit(self, block):
        for eng in ENGS:
            thunks = self.streams[eng]
            if not thunks:
                continue

            def body(e, thunks=thunks):
                for t in thunks:
                    t(e)

            getattr(block, eng)(body)


class _Stop(Exception):
    pass


def build_program(stop=None):
    nc = bass.Bass("TRN2", target_bir_lowering=False)

    def din(name, shape):
        return nc.dram_tensor(name, shape, F32, kind="ExternalInput").ap()

    xT = din("xT", [D, SEQ])
    xTh = din("xTh", [D, 2])
    xown = din("xown", [OWN, D])
    ctxT = din("ctxT", [D, CTX])
    cc = din("cc", [128, 32])
    w_mod = din("w_mod", [D, 3 * D])
    bmod = din("bmod", [128, 48])
    ng = din("ng", [128, 16])
    w_in = din("w_in", [D, 2560])
    w_conv = din("w_conv", [D, 4096])
    qg = din("qg", [128, 128])
    kg = din("kg", [128, 128])
    convw = din("convw", [128, 24])
    w_out = din("w_out", [D, D])
    fg = din("fg", [128, D])
    cosF = din("cosF", [SEQ, 128])
    sinF = din("sinF", [SEQ, 128])
    ident = din("ident", [128, 128])
    hmask = din("hmask", [128, 2])
    out = nc.dram_tensor("out", [OWN, D], F32, kind="ExternalOutput").ap()

    xT_v = xT.rearrange("(k p) n -> p k n", p=128)
    xTh_v = xTh.rearrange("(k p) n -> p k n", p=128)
    ctxT_v = ctxT.rearrange("(k p) n -> p k n", p=128)
    w_mod_v = w_mod.rearrange("(k p) n -> p k n", p=128)
    w_in_v = w_in.rearrange("(k p) n -> p k n", p=128)
    w_conv_v = w_conv.rearrange("(k p) n -> p k n", p=128)
    w_out_v = w_out.rearrange("(k p) n -> p k n", p=128)

    with ExitStack() as es:
        def sb(name, shape, dt):
            return es.enter_context(nc.sbuf_tensor(name, shape, dt))

        big = sb("big", [128, 16384], F32)
        R2 = sb("R2", [128, 16384], F32)
        hHt = sb("hHt", [128, 16, 2], BF16)
        wg = sb("wg", [128, 16, 256], BF16)
        xh = sb("xh", [128, 16, 2], F32)
        KT = sb("KT", [128, 2, NKEY], BF16)
        V = sb("V", [128, NCHUNK, 256], BF16)
        work = sb("work", [128, 5376], F32)
        ident_bf = sb("ident_bf", [128, 128], BF16)
        ident_f = sb("ident_f", [128, 128], F32)
        ones_bf = sb("ones_bf", [128, 128], BF16)
        ones_f = sb("ones_f", [128, 128], F32)
        qg_sb = sb("qg_sb", [128, 128], F32)
        kg_sb = sb("kg_sb", [128, 128], F32)
        ng_sb = sb("ng_sb", [128, 16], F32)
        bmod_sb = sb("bmod_sb", [128, 48], F32)
        convw_sb = sb("convw_sb", [128, 24], F32)
        hmask_sb = sb("hmask_sb", [128, 2], F32)
        cc_sb = sb("cc_sb", [128, 32], F32)
        s_bf = sb("s_bf", [128, 16, 2], BF16)
        m_sb = sb("m_sb", [128, 48, 2], F32)
        gm = sb("gm", [128, 16, 2], F32)
        eps_sb = sb("eps_sb", [128, 1], F32)
        ss = sb("ss", [128, 8], F32)
        rstd_s = sb("rstd_s", [128, 8], F32)
        ssx = sb("ssx", [128, 8], F32)
        rsx = sb("rsx", [128, 8], F32)
        smx = sb("smx", [128, 4], F32)
        cosb = [sb(f"cosb{i}", [128, 4, 128], F32) for i in range(2)]
        sinb = [sb(f"sinb{i}", [128, 4, 128], F32) for i in range(2)]
        PS = [es.enter_context(nc.psum_tensor(f"PS{i}", [128, 1024], F32)) for i in range(4)]

        P = Prog(nc, es)
        block = es.enter_context(nc.Block())
        try:
            _body(locals(), stop)
        except _Stop:
            pass
        P.emit(block)
    return nc


def _body(env, stop):
        globals().update({})
        hHt = env["hHt"]
        ssx = env["ssx"]
        xh = env["xh"]
        rsx = env["rsx"]
        smx = env["smx"]
        wg = env["wg"]
        (nc, es, P, block, big, R2, KT, V, work, ident_bf, ident_f, ones_bf, ones_f, qg_sb, kg_sb, ng_sb, bmod_sb,
         convw_sb, hmask_sb, cc_sb, s_bf, m_sb, gm, eps_sb, ss, rstd_s, cosb, sinb, PS) = [env[k] for k in (
            "nc", "es", "P", "block", "big", "R2", "KT", "V", "work", "ident_bf", "ident_f", "ones_bf", "ones_f",
            "qg_sb", "kg_sb", "ng_sb", "bmod_sb", "convw_sb", "hmask_sb", "cc_sb", "s_bf", "m_sb", "gm", "eps_sb",
            "ss", "rstd_s", "cosb", "sinb", "PS")]
        w_conv_v = env["w_conv_v"]
        (xT_v, xTh_v, ctxT_v, w_mod_v, w_in_v, w_out_v, xown, cc, bmod, ng, qg, kg, convw, fg, cosF, sinF, ident,
         hmask, out) = [env[k] for k in (
            "xT_v", "xTh_v", "ctxT_v", "w_mod_v", "w_in_v", "w_out_v", "xown", "cc", "bmod", "ng", "qg", "kg",
            "convw", "fg", "cosF", "sinF", "ident", "hmask", "out")]

        def chk(name):
            if stop == name:
                raise _Stop()

        def bank(b):
            return PS[b // 2][:, (b % 2) * 512:(b % 2) * 512 + 512]

        def bank_bf(b):
            return bank(b).bitcast(BF16)

        def psr(b):
            return ("ps", b)

        xbuf = [big[:, i * 8192:(i + 1) * 8192].rearrange("p (k n) -> p k n", k=KC) for i in range(2)]
        qT = big[:, 0:4096].bitcast(BF16).rearrange("p (h n) -> p h n", h=8)
        sga = big[:, 4096:8192].bitcast(BF16).rearrange("p (h n) -> p h n", h=8)
        mixT = big[:, 8192:16384].bitcast(BF16).rearrange("p (k n) -> p k n", k=KC)
        hT_own = R2[:, 0:8192].bitcast(BF16).rearrange("p (k n) -> p k n", k=KC)
        hA = hT_own[:, :, 0:512]
        hB = hT_own[:, :, 512:1024]
        hH = hHt[:]
        wbuf = [R2[:, 8192 + i * 4096:8192 + (i + 1) * 4096].bitcast(BF16).rearrange("p (k n) -> p k n", k=KC)
                for i in range(2)]
        woutb = [R2[:, c * 4096:(c + 1) * 4096].bitcast(BF16).rearrange("p (k n) -> p k n", k=KC)
                 for c in range(4)]
        R2_res = [("hT", n, k) for n in "AB" for k in range(KC)] + [("wbuf", 0), ("wbuf", 1)]

        rstd_bc = work[:, 512:1024]
        kn = work[:, 1024:1536].rearrange("p (h n) -> p h n", h=4)
        t1 = work[:, 1536:2048].rearrange("p (h n) -> p h n", h=4)
        t2 = work[:, 2048:2560].rearrange("p (h n) -> p h n", h=4)
        rot_bf = [work[:, o:o + 256].bitcast(BF16).rearrange("p (h n) -> p h n", h=4)
                  for o in (2560, 2816, 4608, 3328)]
        junk = work[:, 3072:3328].bitcast(BF16)
        t1b = work[:, 0:512].rearrange("p (h n) -> p h n", h=4)
        t2b = work[:, 512:1024].rearrange("p (h n) -> p h n", h=4)
        cg_sb = work[:, 0:1026]
        u_sb = work[:, 1026:2052]
        acc_c = work[:, 2052:3076]
        acc2 = work[:, 3076:4100]
        sgc = work[:, 4100:5124]
        PT = [work[:, i * 512:(i + 1) * 512].bitcast(BF16) for i in range(2)]
        tmpb = [work[:, 1024 + i * 256:1024 + (i + 1) * 256].bitcast(BF16) for i in range(2)]
        accs = [work[:, 1536 + i * 512:1536 + (i + 1) * 512] for i in range(4)]
        lns_sb = work[:, 1536:2048]
        o_sb = work[:, 2048:2560]
        rs_sb = work[:, 3584:4096]
        ot_sb = work[:, 4096:4608]
        fg_sb = work[:, 0:2048]
        gate_bc = work[:, 2048:4096]
        diagb = [work[:, 4096 + i * 512:4096 + (i + 1) * 512].rearrange("p (a n) -> p a n", a=4) for i in range(2)]
        xo = [big[:, i * 2048:(i + 1) * 2048] for i in range(2)]
        resb = [big[:, 4096 + i * 2048:4096 + (i + 1) * 2048] for i in range(2)]

        for name, dst, src in [("cc", cc_sb, cc), ("bmod", bmod_sb, bmod), ("ng", ng_sb, ng), ("qg", qg_sb, qg),
                               ("kg", kg_sb, kg), ("convw", convw_sb, convw), ("hmask", hmask_sb, hmask),
                               ("identf", ident_f, ident)]:
            P.dma("sync", "const", dst[:], src, writes=[name])
        for name in ["cc", "bmod", "ng", "qg", "kg", "convw", "hmask", "identf"]:
            P.lastw[name] = (("d", "const"), P.dcount["const"])
        P.dma("gpsimd", "constg", ident_bf[:], ident, writes=["identbf"])
        P.op("vector", lambda e: e.memset(ones_bf[:], 1.0), writes=["ones_bf"])
        P.op("vector", lambda e: e.memset(ones_f[:], 1.0), writes=["ones_f"])
        P.op("vector", lambda e: e.memset(eps_sb[:], EPS), writes=["eps"])
        P.op("scalar", lambda e: e.activation(out=s_bf[:].rearrange("p k r -> p (k r)"), in_=cc_sb[:], func=AF.Silu),
             reads=["cc"], writes=["s_bf"])
        P.op("vector", lambda e: e.reduce_max(out=smx[:, 0:1], in_=qg_sb[:], axis=mybir.AxisListType.X,
                                              apply_absolute_value=True), reads=["qg"], writes=["smx0"])
        P.op("vector", lambda e: e.reduce_max(out=smx[:, 1:2], in_=kg_sb[:], axis=mybir.AxisListType.X,
                                              apply_absolute_value=True), reads=["kg"], writes=["smx1"])
        P.op("vector", lambda e: e.scalar_tensor_tensor(
            out=smx[:, 2:3], in0=smx[:, 0:1], scalar=-math.sqrt(128.0), in1=smx[:, 1:2], op0=ALU.mult, op1=ALU.mult),
            reads=["smx0", "smx1"], writes=["smx2"])

        seq = [("ctx", CTX, 1), ("halo", 2, 0), (6, 512, 0), (7, 512, 0)]

        def tile_src(tile, N):
            if tile == "ctx":
                return ctxT_v
            if tile == "halo":
                return xTh_v
            return xT_v[:, :, tile * 512:(tile + 1) * 512]

        def xsrc_buf(i):
            tile, N, r = seq[i]
            if tile == "halo":
                return xh[:], "xh"
            return xbuf[0][:, :, 0:N], ("xbuf", 0)

        def issue_x_load(i):
            tile, N, r = seq[i]
            xb, xres = xsrc_buf(i)
            P.dma("sync", "xh" if tile == "halo" else "x0", xb, tile_src(tile, N),
                  writes=[xres, (xres, "h", 0), (xres, "h", 1)])
            if isinstance(tile, int):
                tp = tile % 2
                P.dma("sync", f"c{tp}", cosb[tp][:], cosF[tile * 512:(tile + 1) * 512, :].rearrange("(b p) d -> p b d", p=128),
                      writes=[("cos", tp)])
                P.dma("sync", f"s{tp}", sinb[tp][:], sinF[tile * 512:(tile + 1) * 512, :].rearrange("(b p) d -> p b d", p=128),
                      writes=[("sin", tp)])

        issue_x_load(0)
        issue_x_load(1)

        def adaln_tile(t, wb, wres, key, pb):
            P.dma("gpsimd", key, wb[:], w_mod_v[:, :, t * 512:(t + 1) * 512], writes=wres)
            pv = bank(pb)[:, 0:96].rearrange("p (c r) -> p c r", r=2)
            for c4 in range(4):
                cidx = t * 4 + c4
                for kc in range(KC):
                    last = (c4 == 3 and kc == KC - 1)
                    P.op("tensor",
                         lambda e, pv=pv, wb=wb, c4=c4, cidx=cidx, kc=kc: e.matmul(
                             pv[:, cidx, :], wb[:, kc, c4 * 128:(c4 + 1) * 128], s_bf[:, kc, :],
                             start=(kc == 0), stop=(kc == KC - 1)),
                         reads=wres + ["s_bf"], writes=[psr(pb)], signal=last)
            return pv

        def adaln_evac(pv, pb, c0, c1, tag):
            P.op("vector",
                 lambda e: e.tensor_tensor(
                     out=m_sb[:, c0:c1, :], in0=pv[:, c0:c1, :],
                     in1=bmod_sb[:, c0:c1].unsqueeze(2).broadcast_to([128, c1 - c0, 2]), op=ALU.add),
                 reads=[psr(pb), "bmod"], writes=[("m", tag)])

        for t in range(8):
            pv = adaln_tile(t, wbuf[t % 2], [("wbuf", t % 2)], f"w{t % 2}", 0)
        adaln_evac(pv, 0, 0, 32, 7)
        P.op("vector",
             lambda e: e.scalar_tensor_tensor(
                 out=gm[:], in0=m_sb[:, 16:32, :], scalar=1.0,
                 in1=ng_sb[:].unsqueeze(2).broadcast_to([128, 16, 2]), op0=ALU.add, op1=ALU.mult),
             reads=[("m", 7), "ng"], writes=["gm"])

        chk("A")
        P.dma("gpsimd", "w0", wbuf[0][:], w_in_v[:, :, 1024:1536], writes=[("wbuf", 0)])
        hdst = {"ctx": ("B", hB[:, :, 0:CTX]), "halo": ("H", hH), 6: ("A", hA), 7: ("B", hB)}
        kn2 = [kn, work[:, 4096:4608].rearrange("p (h n) -> p h n", h=4)]
        pending = []
        gblk = [0]

        def flush_pending(keep=0):
            while len(pending) > keep:
                pending.pop(0)()

        def front_a(i):
            tile, N, r = seq[i]
            xb, xres = xsrc_buf(i)
            hname, hap = hdst[tile]
            hres = [("hT", hname, kc) for kc in range(KC)]
            ssb = 3

            def f1(hf=None):
                k0, k1 = (0, KC) if hf is None else (hf * 8, hf * 8 + 8)
                P.op("scalar", lambda e: e.activation(out=hap[:, k0:k1, :], in_=xb[:, k0:k1, :], func=AF.Square),
                     reads=[xres], writes=hres[k0:k1])

            def f2():
                for kc in range(KC):
                    P.op("tensor", lambda e, kc=kc: e.matmul(
                        bank(ssb)[:, 0:N], ones_bf[:], hap[:, kc, :], start=(kc == 0), stop=(kc == KC - 1)),
                        reads=[("hT", hname, kc), "ones_bf"], writes=[psr(ssb)], signal=(kc == KC - 1))
                P.op("scalar", lambda e: e.activation(
                    out=rstd_bc[:, 0:N], in_=bank(ssb)[:, 0:N], func=AF.Ln, bias=eps_sb[:, 0:1], scale=1.0 / D),
                    reads=[psr(ssb), "eps"], writes=["rstd_bc"])
                P.op("scalar", lambda e: e.activation(out=rstd_bc[:, 0:N], in_=rstd_bc[:, 0:N], func=AF.Exp, scale=-0.5),
                     reads=["rstd_bc"], writes=["rstd_bc"])

            def f3(hf=None):
                k0, k1 = (0, KC) if hf is None else (hf * 8, hf * 8 + 8)
                P.op("vector", lambda e: e.tensor_tensor(
                    out=xb[:, k0:k1, :], in0=xb[:, k0:k1, :],
                    in1=rstd_bc[:, 0:N].unsqueeze(1).broadcast_to([128, k1 - k0, N]), op=ALU.mult),
                    reads=[xres, "rstd_bc"], writes=[(xres, "h", hf)] if hf is not None else [xres])

            return [f1, f2, f3]

        def front_b(i):
            tile, N, r = seq[i]
            xb, xres = xsrc_buf(i)
            hname, hap = hdst[tile]
            for kc in range(KC):
                if kc % 2 == 0:
                    P.op("scalar", lambda e, kc=kc: e.activation(
                        out=hap[:, kc, :], in_=xb[:, kc, :], func=AF.Identity,
                        bias=m_sb[:, kc, r:r + 1], scale=gm[:, kc, r:r + 1]),
                        reads=[xres, (xres, "h", kc // 8), "gm", ("m", 7)], writes=[("hT", hname, kc)])
                else:
                    P.op("vector", lambda e, kc=kc: e.tensor_scalar(
                        out=hap[:, kc, :], in0=xb[:, kc, :], scalar1=gm[:, kc, r:r + 1],
                        scalar2=m_sb[:, kc, r:r + 1], op0=ALU.mult, op1=ALU.add),
                        reads=[xres, (xres, "h", kc // 8), "gm", ("m", 7)], writes=[("hT", hname, kc)])
            if tile == "ctx":
                issue_x_load(2)
            elif tile == 6:
                issue_x_load(3)

        def kv(i):
            tile, N, r = seq[i]
            if tile == "halo":
                return []
            hname, hap = hdst[tile]
            blocks = []
            nblk = N // 128
            latent = isinstance(tile, int)
            key0 = 0 if tile == "ctx" else CTX + tile * 512
            tp = tile % 2 if latent else 0
            for blk in range(nblk):
                blocks.append(lambda blk=blk: kv_block(blk, hname, hap, latent, key0, tp))
            return blocks

        KVB = [4, 5, 0, 1]

        def kv_block(blk, hname, hap, latent, key0, tp, newpath=None, cs=None):
                if cs is None:
                    cs = (cosb[tp], sinb[tp], ("cos", tp), ("sin", tp))
                g = gblk[0]
                gblk[0] += 1
                kvb = KVB[g % 4]
                tb = 6 + (g % 2)
                knb = kn2[g % 2]
                rb = rot_bf[g % 3]
                rres = ("rot", g % 3)
                kres = ("kn", g % 2)
                if newpath is None:
                    for kc in range(KC):
                        P.op("tensor", lambda e, kc=kc, blk=blk, kvb=kvb: e.matmul(
                            bank(kvb), hap[:, kc, blk * 128:(blk + 1) * 128], wbuf[0][:, kc, :],
                            start=(kc == 0), stop=(kc == KC - 1)),
                            reads=[("hT", hname, kc), ("wbuf", 0)], writes=[psr(kvb)], signal=(kc == KC - 1))
                    srcap, sres = bank(kvb), psr(kvb)
                else:
                    xq, qres, jcol = newpath
                    for kc in range(KC):
                        P.op("tensor", lambda e, kc=kc, blk=blk, kvb=kvb: e.matmul(
                            bank(kvb), xq[:, kc, blk * 128:(blk + 1) * 128], wbuf[1][:, kc, :],
                            start=(kc == 0), stop=(kc == KC - 1)),
                            reads=[qres, ("wkv2", kc)], writes=[psr(kvb)], signal=(kc == KC - 1))
                    kr = kraw[g % 2]
                    sres = ("kraw", g % 2)
                    P.op("vector", lambda e, kvb=kvb, kr=kr, jcol=jcol: e.scalar_tensor_tensor(
                        out=kr, in0=bank(kvb), scalar=rsx[:, jcol:jcol + 1], in1=bias_sb, op0=ALU.mult, op1=ALU.add),
                        reads=[psr(kvb), ("rsx", jcol), "bias_sb"], writes=[sres])
                    srcap = kr
                chunk = key0 // 128 + blk
                for h in range(2):
                    P.op("scalar", lambda e, h=h: e.activation(
                        out=junk[:, h * 128:(h + 1) * 128], in_=srcap[:, h * 128:(h + 1) * 128], func=AF.Square,
                        accum_out=ss[:, h:h + 1]),
                        reads=[sres], writes=[("ss", h), ("junk", h)])
                P.op("scalar", lambda e: e.activation(out=rstd_s[:, 0:2], in_=ss[:, 0:2], func=AF.Ln,
                                                      bias=eps_sb[:, 0:1], scale=1.0 / 128),
                     reads=[("ss", 0), ("ss", 1), "eps"], writes=["rstd_s"])
                P.op("scalar", lambda e: e.activation(out=rstd_s[:, 0:2], in_=rstd_s[:, 0:2], func=AF.Exp, scale=-0.5),
                     reads=["rstd_s"], writes=["rstd_s"])
                if newpath is None:
                    P.op("scalar", lambda e, chunk=chunk: e.activation(
                        out=V[:, chunk, :], in_=srcap[:, 256:512], func=AF.Copy),
                        reads=[sres], writes=["V"])
                else:
                    P.op("scalar", lambda e, chunk=chunk: e.activation(
                        out=V[:, chunk, :], in_=srcap[:, 256:512], func=AF.Copy),
                        reads=[sres], writes=["V"])
                P.op("vector", lambda e, knb=knb: e.tensor_tensor(
                    out=knb[:, 0:2, :], in0=srcap[:, 0:256].rearrange("p (h n) -> p h n", h=2),
                    in1=rstd_s[:, 0:2].unsqueeze(2).broadcast_to([128, 2, 128]), op=ALU.mult),
                    reads=[sres, "rstd_s"], writes=[kres])
                if latent:
                    P.op("vector", lambda e, knb=knb: e.tensor_tensor(
                        out=knb[:, 0:2, :], in0=knb[:, 0:2, :], in1=kg_sb[:].unsqueeze(1).broadcast_to([128, 2, 128]),
                        op=ALU.mult), reads=[kres, "kg"], writes=[kres])
                    if newpath is not None and g % 2 == 1:
                        emit_rope(P, knb, t1[:, 2:4, :], t2[:, 2:4, :], rb, cs[0][:, blk, :], cs[1][:, blk, :], 2,
                                  [cs[2], cs[3]], rres, kres, eng="vector", tres=("t1u", "t2u"))
                    else:
                        emit_rope(P, knb, t1, t2, rb, cs[0][:, blk, :], cs[1][:, blk, :], 2,
                                  [cs[2], cs[3]], rres, kres)
                else:
                    P.op("vector", lambda e, rb=rb, knb=knb: e.tensor_tensor(
                        out=rb[:, 0:2, :], in0=knb[:, 0:2, :], in1=kg_sb[:].unsqueeze(1).broadcast_to([128, 2, 128]),
                        op=ALU.mult), reads=[kres, "kg"], writes=[rres])

                def trans(rb=rb, rres=rres, tb=tb, kpos=key0 + blk * 128):
                    for h in range(2):
                        P.op("tensor", lambda e, h=h: e.transpose(
                            bank_bf(tb)[:, h * 128:(h + 1) * 128], rb[:, h, :], ident_bf[:]),
                            reads=[rres, "identbf"], writes=[psr(tb)], signal=(h == 1))
                    P.op("vector", lambda e: e.tensor_copy(
                        out=KT[:, :, kpos:kpos + 128],
                        in_=bank_bf(tb)[:, 0:256].rearrange("p (h n) -> p h n", h=2)),
                        reads=[psr(tb)], writes=["KT"])

                flush_pending(1)
                pending.append(trans)

        xbf = [big[:, q * 4096:(q + 1) * 4096].bitcast(BF16).rearrange("p (k n) -> p k n", k=KC) for q in range(4)]
        kraw = [work[:, 0:512], work[:, 3456:3968]]
        bias_sb = work[:, 4864:5376]
        junkf = work[:, 3328:3456]
        shiftb = env["wg"][:].rearrange("p k (a n) -> p (k a) n", a=2)[:, 0:16, :]
        NT = 6
        wg_f = env["wg"][:].rearrange("p k n -> p (k n)").bitcast(F32)
        cosb2 = [wg_f[:, i * 1024:i * 1024 + 512].rearrange("p (b d) -> p b d", b=4) for i in range(2)]
        sinb2 = [wg_f[:, i * 1024 + 512:(i + 1) * 1024].rearrange("p (b d) -> p b d", b=4) for i in range(2)]
        CS2 = [("cos2", 0), ("sin2", 0), ("cos2", 1), ("sin2", 1)]

        def xq_res(q):
            return ("xq", q)

        def newpath_load(t):
            if t >= NT:
                return
            q = 2 + t % 2
            P.dma("gpsimd", f"xq{q}", xbf[q][:], xT_v[:, :, t * 512:(t + 1) * 512], writes=[xq_res(q)])

        def cs_load(t):
            if t >= NT:
                return
            tp = t % 2
            P.dma("sync", f"c2{tp}", cosb2[tp][:], cosF[t * 512:(t + 1) * 512, :].rearrange("(b p) d -> p b d", p=128),
                  writes=[("cos2", tp)])
            P.dma("sync", f"s2{tp}", sinb2[tp][:], sinF[t * 512:(t + 1) * 512, :].rearrange("(b p) d -> p b d", p=128),
                  writes=[("sin2", tp)])

        def _unused_cs_load(t):
            tp = t % 2
            P.dma("sync", f"c{tp}", cosb[tp][:], cosF[t * 512:(t + 1) * 512, :].rearrange("(b p) d -> p b d", p=128),
                  writes=[("cos", tp)])
            P.dma("sync", f"s{tp}", sinb[tp][:], sinF[t * 512:(t + 1) * 512, :].rearrange("(b p) d -> p b d", p=128),
                  writes=[("sin", tp)])

        def prep_newpath():
            for t in range(2):
                newpath_load(t)
            P.op("vector", lambda e: e.tensor_copy(
                out=shiftb, in_=m_sb[:, 0:16, 0:1].broadcast_to([128, 16, 128])),
                reads=[("m", 7)], writes=["wg"])
            for kc in range(KC):
                P.op("tensor", lambda e, kc=kc: e.matmul(bank(3), shiftb[:, kc, :], wbuf[0][:, kc, :],
                                                         start=(kc == 0), stop=(kc == KC - 1)),
                     reads=["wg", ("wbuf", 0)], writes=[psr(3)], signal=(kc == KC - 1))
            P.op("scalar", lambda e: e.activation(out=bias_sb, in_=bank(3), func=AF.Copy),
                 reads=[psr(3)], writes=["bias_sb"])
            P.inherit(CS2, ["wg"])
            cs_load(0)
            cs_load(1)
            P.inherit([("wkv2", kc) for kc in range(KC)], [("wbuf", 1)])
            for kc in range(KC):
                if kc % 2 == 0:
                    P.op("scalar", lambda e, kc=kc: e.activation(
                        out=wbuf[1][:, kc, :], in_=wbuf[0][:, kc, :], func=AF.Copy, scale=gm[:, kc, 0:1]),
                        reads=[("wbuf", 0), "gm"], writes=[("wkv2", kc)])
                else:
                    P.op("vector", lambda e, kc=kc: e.tensor_scalar(
                        out=wbuf[1][:, kc, :], in0=wbuf[0][:, kc, :], scalar1=gm[:, kc, 0:1], scalar2=None,
                        op0=ALU.mult), reads=[("wbuf", 0), "gm"], writes=[("wkv2", kc)])

        def gram(j):
            t, blk = divmod(j, 4)
            q = 2 + t % 2
            gb = 2
            jc = j % 8
            for kc in range(KC):
                P.op("tensor", lambda e, kc=kc: e.matmul(
                    bank(gb)[:, 0:128], xbf[q][:, kc, blk * 128:(blk + 1) * 128],
                    xbf[q][:, kc, blk * 128:(blk + 1) * 128], start=(kc == 0), stop=(kc == KC - 1)),
                    reads=[xq_res(q)], writes=[psr(gb)], signal=(kc == KC - 1))
            P.op("vector", lambda e: e.scalar_tensor_tensor(
                out=junkf, in0=bank(gb)[:, 0:128], scalar=1.0, in1=ident_f[:], op0=ALU.mult, op1=ALU.mult,
                accum_out=ssx[:, jc:jc + 1]),
                reads=[psr(gb), "identf"], writes=[("ssx", jc), "junkf"])
            P.op("scalar", lambda e: e.activation(out=rsx[:, jc:jc + 1], in_=ssx[:, jc:jc + 1], func=AF.Ln,
                                                  bias=eps_sb[:, 0:1], scale=1.0 / D),
                 reads=[("ssx", jc), "eps"], writes=[("rsx", jc)])
            P.op("scalar", lambda e: e.activation(out=rsx[:, jc:jc + 1], in_=rsx[:, jc:jc + 1], func=AF.Exp, scale=-0.5),
                 reads=[("rsx", jc)], writes=[("rsx", jc)])

        def wload_q(n):
            P.dma("gpsimd", f"xq{2 + n}", xbf[2 + n][:], w_in_v[:, :, n * 512:(n + 1) * 512], writes=[xq_res(2 + n)])

        def run_newpath(old_steps):
            nb = NT * 4
            gram(0)
            for j in range(nb):
                if j + 1 < nb:
                    gram(j + 1)
                t, blk = divmod(j, 4)
                q = 2 + t % 2
                kv_block(blk, None, None, True, CTX + t * 512, t % 2, newpath=(xbf[q], xq_res(q), j % 8),
                         cs=(cosb2[t % 2], sinb2[t % 2], ("cos2", t % 2), ("sin2", t % 2)))
                if blk == 3:
                    newpath_load(t + 2)
                    cs_load(t + 2)
                    if t >= NT - 2:
                        wload_q(t - (NT - 2))
                if old_steps:
                    old_steps.pop(0)()
            while old_steps:
                old_steps.pop(0)()

        prep_newpath()
        fa = [front_a(i) for i in range(4)]
        kb = [kv(i) for i in range(4)]
        F1, F2, F3 = 0, 1, 2
        CTXI, HALO, OWN0, OWN1 = 0, 1, 2, 3
        for st in (fa[CTXI][F1], fa[HALO][F1], fa[CTXI][F2], fa[CTXI][F3], fa[HALO][F2], fa[HALO][F3],
                   lambda: front_b(CTXI), lambda: front_b(HALO), kb[CTXI][0], kb[CTXI][1]):
            st()
        sched = {
            1: [lambda: fa[OWN0][F1](0)],
            2: [lambda: fa[OWN0][F1](1)],
            4: [fa[OWN0][F2]],
            5: [lambda: fa[OWN0][F3](0)],
            6: [lambda: fa[OWN0][F3](1)],
            7: [lambda: front_b(OWN0)],
            8: [kb[OWN0][0]],
            9: [kb[OWN0][1]],
            10: [kb[OWN0][2], lambda: fa[OWN1][F1](0)],
            11: [kb[OWN0][3], lambda: fa[OWN1][F1](1)],
            13: [fa[OWN1][F2]],
            14: [lambda: fa[OWN1][F3](0)],
            15: [lambda: fa[OWN1][F3](1)],
            16: [lambda: front_b(OWN1)],
            17: [kb[OWN1][0]],
            18: [kb[OWN1][1]],
            19: [kb[OWN1][2]],
            20: [kb[OWN1][3]],
        }
        old_steps = []
        for slot in range(24):
            steps = sched.get(slot, [])
            old_steps.append(lambda steps=steps: [s() for s in steps])
        run_newpath(old_steps)
        P.inherit(["wg"], CS2)

        chk("BC")
        P.inherit(["t1b", "t2b"], ["rstd_bc", ("kraw", 0), ("kraw", 1)])
        P.inherit([("rot", 3)], ["junkf", ("kraw", 1)])
        P.inherit(["t1", "t2"], ["t1u", "t2u"])
        P.inherit([("wbuf", 1)], [("wkv2", kc) for kc in range(KC)])
        P.inherit(["qT", "sga"], [("xbuf", 0), ("xq", 0), ("xq", 1)])
        P.inherit(["mixT"], [("xbuf", 1), ("xq", 2), ("xq", 3)])
        wsrc = [w_in_v[:, :, 0:512], w_in_v[:, :, 512:1024], w_in_v[:, :, 1536:2048], w_in_v[:, :, 2048:2560]] + \
               [w_conv_v[:, :, i * 512:(i + 1) * 512] for i in range(8)]

        def wtile(n):
            if n < 2:
                return xbf[2 + n], xq_res(2 + n), f"xq{2 + n}"
            return wbuf[n % 2], ("wbuf", n % 2), f"w{n % 2}"

        def wload(n):
            if n < len(wsrc):
                wb_, wres_, key_ = wtile(n)
                P.dma("gpsimd", key_, wb_[:], wsrc[n], writes=[wres_])

        wload(2)
        wload(3)
        for half in range(2):
            for blk in range(8):
                g = gblk[0]
                gblk[0] += 1
                qb = [2, 3, 4, 5, 0, 1][g % 6]
                tb = 6 + (g % 2)
                knb = kn2[g % 2]
                rb = rot_bf[g % 4]
                rres = ("rot", g % 4)
                kres = ("kn", g % 2)
                for kc in range(KC):
                    P.op("tensor", lambda e, kc=kc, blk=blk, qb=qb, half=half: e.matmul(
                        bank(qb), hT_own[:, kc, blk * 128:(blk + 1) * 128], xbf[2 + half][:, kc, :],
                        start=(kc == 0), stop=(kc == KC - 1)),
                        reads=[("hT", "A" if blk < 4 else "B", kc), xq_res(2 + half)], writes=[psr(qb)],
                        signal=(kc == KC - 1))
                if half == 0 and blk == 0:
                    flush_pending(0)
                for h in range(4):
                    P.op("scalar", lambda e, qb=qb, h=h: e.activation(
                        out=junk[:, h * 128:(h + 1) * 128], in_=bank(qb)[:, h * 128:(h + 1) * 128], func=AF.Square,
                        accum_out=ss[:, h:h + 1]),
                        reads=[psr(qb)], writes=[("ss", h), ("junk", h)])
                P.op("scalar", lambda e: e.activation(out=rstd_s[:, 0:4], in_=ss[:, 0:4], func=AF.Ln,
                                                      bias=eps_sb[:, 0:1], scale=1.0 / 128),
                     reads=[("ss", 0), ("ss", 1), ("ss", 2), ("ss", 3), "eps"], writes=["rstd_s"])
                P.op("scalar", lambda e: e.activation(out=rstd_s[:, 0:4], in_=rstd_s[:, 0:4], func=AF.Exp, scale=-0.5),
                     reads=["rstd_s"], writes=["rstd_s"])
                P.op("vector", lambda e, qb=qb, knb=knb: e.tensor_tensor(
                    out=knb[:], in0=bank(qb).rearrange("p (h n) -> p h n", h=4),
                    in1=rstd_s[:, 0:4].unsqueeze(2).broadcast_to([128, 4, 128]), op=ALU.mult),
                    reads=[psr(qb), "rstd_s"], writes=[kres])
                P.op("vector", lambda e, knb=knb: e.tensor_tensor(
                    out=knb[:], in0=knb[:], in1=qg_sb[:].unsqueeze(1).broadcast_to([128, 4, 128]), op=ALU.mult),
                    reads=[kres, "qg"], writes=[kres])
                tp = (6 + blk // 4) % 2
                if g % 2 == 0:
                    emit_rope(P, knb, t1, t2, rb, cosb[tp][:, blk % 4, :], sinb[tp][:, blk % 4, :], 4,
                              [("cos", tp), ("sin", tp)], rres, kres)
                else:
                    emit_rope(P, knb, t1b, t2b, rb, cosb[tp][:, blk % 4, :], sinb[tp][:, blk % 4, :], 4,
                              [("cos", tp), ("sin", tp)], rres, kres, eng="vector", tres=("t1b", "t2b"))

                def transq(rb=rb, rres=rres, tb=tb, half=half, blk=blk):
                    for h in range(4):
                        P.op("tensor", lambda e, h=h: e.transpose(
                            bank_bf(tb)[:, h * 128:(h + 1) * 128], rb[:, h, :], ident_bf[:]),
                            reads=[rres, "identbf"], writes=[psr(tb)], signal=(h == 3))
                    P.op("vector", lambda e: e.tensor_copy(
                        out=qT[:, half * 4:(half + 1) * 4, blk * 128:(blk + 1) * 128],
                        in_=bank_bf(tb)[:, 0:512].rearrange("p (h n) -> p h n", h=4)),
                        reads=[psr(tb)], writes=["qT"])

                flush_pending(2)
                pending.append(transq)

        chk("D1")
        bctr = [0]

        def next_bank():
            b = bctr[0] % 7
            bctr[0] += 1
            return b

        pv7 = bank(7)[:, 0:96].rearrange("p (c r) -> p c r", r=2)

        def gate_dma(n):
            if n < 8:
                P.dma("gpsimd", "wg", wg[:], w_mod_v[:, :, 4096 + n * 256:4096 + (n + 1) * 256], writes=["wg"])

        def gate_mm(n):
            for c2 in range(2):
                cidx = 32 + n * 2 + c2
                for kc in range(KC):
                    P.op("tensor", lambda e, c2=c2, cidx=cidx, kc=kc: e.matmul(
                        pv7[:, cidx, :], wg[:, kc, c2 * 128:(c2 + 1) * 128], s_bf[:, kc, :],
                        start=(kc == 0), stop=(kc == KC - 1)),
                        reads=["wg", "s_bf"], writes=[psr(7)], signal=(c2 == 1 and kc == KC - 1))
            if n == 7:
                adaln_evac(pv7, 7, 32, 48, 11)

        for grp in range(2):
            wb = wbuf[grp % 2]
            if grp == 0:
                gate_dma(0)
            else:
                gate_mm(0)
                gate_dma(1)
            for c4 in range(4):
                head = grp * 4 + c4
                for tt in range(2):
                    b = next_bank()
                    for kc in range(KC):
                        P.op("tensor", lambda e, wb=wb, kc=kc, c4=c4, tt=tt, b=b: e.matmul(
                            bank(b), wb[:, kc, c4 * 128:(c4 + 1) * 128], hT_own[:, kc, tt * 512:(tt + 1) * 512],
                            start=(kc == 0), stop=(kc == KC - 1)),
                            reads=[("hT", "AB"[tt], kc), ("wbuf", grp % 2)], writes=[psr(b)], signal=(kc == KC - 1))
                    P.op("scalar", lambda e, b=b, head=head, tt=tt: e.activation(
                        out=sga[:, head, tt * 512:(tt + 1) * 512], in_=bank(b), func=AF.Silu),
                        reads=[psr(b)], writes=["sga"])
                if grp == 0 and c4 == 0:
                    flush_pending(0)
            wload(grp + 4)

        chk("D2")
        W_BC = ["rstd_bc", ("kn", 0), ("kn", 1), "t1", "t2", ("rot", 0), ("rot", 1), ("rot", 2), ("rot", 3), ("junk", 0), ("junk", 1), ("junk", 2), ("junk", 3), "t1u", "t2u",
                "t1b", "t2b", ("kraw", 0), ("kraw", 1), "bias_sb", "junkf"]
        W_D3 = ["cg", "u", "acc_c", "acc2", "sgc"]
        P.inherit(W_D3, W_BC)
        for i in range(8):
            wpar = i % 2
            wb4 = wbuf[wpar].rearrange("p k (g n) -> p k g n", g=4)
            if i < 7:
                gate_mm(i + 1)
                gate_dma(i + 2)
            banks = {}
            bh = next_bank()
            for g in (1, 2, 0, 3):
                for tt in range(2):
                    b = next_bank()
                    banks[(g, tt)] = b
                    for kc in range(KC):
                        P.op("tensor", lambda e, wb4=wb4, kc=kc, g=g, tt=tt, b=b: e.matmul(
                            bank(b), wb4[:, kc, g, :], hT_own[:, kc, tt * 512:(tt + 1) * 512],
                            start=(kc == 0), stop=(kc == KC - 1)),
                            reads=[("hT", "AB"[tt], kc), ("wbuf", wpar)], writes=[psr(b)], signal=(kc == KC - 1))
                        if g in (1, 2) and tt == 1:
                            off = 0 if g == 1 else 2
                            P.op("tensor", lambda e, wb4=wb4, kc=kc, g=g, off=off, bh=bh: e.matmul(
                                bank(bh)[:, off:off + 2], wb4[:, kc, g, :], hH[:, kc, :],
                                start=(kc == 0), stop=(kc == KC - 1)),
                                reads=[("hT", "H", kc), ("wbuf", wpar)], writes=[psr(bh)], signal=(kc == KC - 1))
                    if g == 1:
                        P.op("scalar", lambda e, b=b, tt=tt: e.activation(
                            out=cg_sb[:, 1 + tt * 512:1 + (tt + 1) * 512], in_=bank(b), func=AF.Copy),
                            reads=[psr(b)], writes=["cg"])
                    elif g == 2:
                        P.op("vector", lambda e, b=b, tt=tt: e.tensor_tensor(
                            out=u_sb[:, 1 + tt * 512:1 + (tt + 1) * 512], in0=bank(b),
                            in1=cg_sb[:, 1 + tt * 512:1 + (tt + 1) * 512], op=ALU.mult),
                            reads=[psr(b), "cg"], writes=["u"])
                    elif g == 3:
                        P.op("scalar", lambda e, b=b, tt=tt: e.activation(
                            out=sgc[:, tt * 512:(tt + 1) * 512], in_=bank(b), func=AF.Silu),
                            reads=[psr(b)], writes=["sgc"])
                if g == 2:
                    P.op("scalar", lambda e, bh=bh: e.activation(
                        out=cg_sb[:, 0:1026:1025], in_=bank(bh)[:, 0:2], func=AF.Copy),
                        reads=[psr(bh)], writes=["cg"])
                    P.op("vector", lambda e, bh=bh: e.tensor_tensor(
                        out=u_sb[:, 0:1026:1025], in0=bank(bh)[:, 2:4], in1=cg_sb[:, 0:1026:1025], op=ALU.mult),
                        reads=[psr(bh), "cg"], writes=["u"])
                    P.op("vector", lambda e: e.tensor_tensor(
                        out=u_sb[:, 0:1026:1025], in0=u_sb[:, 0:1026:1025], in1=hmask_sb[:], op=ALU.mult),
                        reads=["u", "hmask"], writes=["u"])
                    cw = convw_sb[:, i * 3:(i + 1) * 3]
                    P.op("vector", lambda e, cw=cw: e.tensor_scalar(
                        out=acc_c, in0=u_sb[:, 1:1025], scalar1=cw[:, 1:2], scalar2=None, op0=ALU.mult),
                        reads=["u", "convw"], writes=["acc_c"])
                    P.op("vector", lambda e, cw=cw: e.scalar_tensor_tensor(
                        out=acc_c, in0=u_sb[:, 0:1024], scalar=cw[:, 0:1], in1=acc_c, op0=ALU.mult, op1=ALU.add),
                        reads=["u", "convw", "acc_c"], writes=["acc_c"])
                    P.op("vector", lambda e, cw=cw: e.scalar_tensor_tensor(
                        out=acc_c, in0=u_sb[:, 2:1026], scalar=cw[:, 2:3], in1=acc_c, op0=ALU.mult, op1=ALU.add),
                        reads=["u", "convw", "acc_c"], writes=["acc_c"])
                if g == 0:
                    for tt in range(2):
                        b = banks[(0, tt)]
                        P.op("vector", lambda e, b=b, tt=tt: e.tensor_tensor(
                            out=acc2[:, tt * 512:(tt + 1) * 512], in0=bank(b), in1=acc_c[:, tt * 512:(tt + 1) * 512],
                            op=ALU.mult), reads=[psr(b), "acc_c"], writes=["acc2"])
            wload(i + 6)
            P.op("gpsimd", lambda e, i=i: e.tensor_tensor(
                out=mixT[:, 8 + i, :], in0=acc2, in1=sgc, op=ALU.mult),
                reads=["acc2", "sgc"], writes=["mixT"])

        chk("D3")
        W_E = [("PT", 0), ("PT", 1), ("tmpb", 0), ("tmpb", 1), "rs", "ot", "lns", "o_sb"]
        P.inherit(W_E, W_D3)
        for c in range(4):
            P.dma("gpsimd", f"wout{c}", woutb[c][:], w_out_v[:, :, c * 512:(c + 1) * 512],
                  writes=R2_res + [("wout", c)])

        sm_scale = 1.0 / math.sqrt(128.0)
        iters = [(g, tt, hq) for g in range(2) for tt in range(2) for hq in range(4)]
        items = [(it, p) for it in range(len(iters)) for p in range(NCHUNK // 2)]

        def emit_S(idx):
            it, p = items[idx]
            g, tt, hq = iters[it]
            head = g * 4 + hq
            sp = idx % 3
            for cl in range(2):
                chunk = p * 2 + cl
                b = sp * 2 + cl
                P.op("tensor", lambda e, b=b, g=g, chunk=chunk, head=head, tt=tt: e.matmul(
                    bank(b), KT[:, g, chunk * 128:(chunk + 1) * 128], qT[:, head, tt * 512:(tt + 1) * 512],
                    start=True, stop=True),
                    reads=["KT", "qT"], writes=[psr(b)], signal=(cl == 1))

        def emit_norm(it):
            g, tt, hq = iters[it]
            head = g * 4 + hq
            ob = 6
            sb_ = 7
            P.op("scalar", lambda e: e.activation(out=lns_sb, in_=bank(sb_), func=AF.Ln), reads=[psr(sb_)], writes=["lns"])
            P.op("vector", lambda e: e.tensor_copy(out=o_sb, in_=bank(ob)), reads=[psr(ob)], writes=["o_sb"])
            P.op("scalar", lambda e: e.activation(out=rs_sb, in_=lns_sb, func=AF.Exp, scale=-1.0),
                 reads=["lns"], writes=["rs"])
            P.op("vector", lambda e: e.tensor_tensor(out=ot_sb, in0=o_sb, in1=rs_sb, op=ALU.mult),
                 reads=["o_sb", "rs"], writes=["ot"])
            P.op("gpsimd", lambda e: e.tensor_tensor(
                out=mixT[:, head, tt * 512:(tt + 1) * 512], in0=ot_sb, in1=sga[:, head, tt * 512:(tt + 1) * 512],
                op=ALU.mult), reads=["ot", "sga"], writes=["mixT"])

        def emit_summm(idx):
            it, p = items[idx]
            tq = idx % 2
            sb_ = 7
            P.op("tensor", lambda e: e.matmul(bank(sb_), ones_bf[:], tmpb[tq], start=(p == 0), stop=(p == NP - 1)),
                 reads=[("tmpb", tq), "ones_bf"], writes=[psr(sb_)])

        NP = NCHUNK // 2
        emit_S(0)
        emit_S(1)
        for idx, (it, p) in enumerate(items):
            g, tt, hq = iters[it]
            sp = idx % 3
            pp = idx % 2
            if idx + 2 < len(items):
                emit_S(idx + 2)
            P.op("scalar", lambda e, sp=sp, pp=pp: e.activation(out=PT[pp], in_=PS[sp][:], func=AF.Exp, scale=sm_scale,
                                                                bias=smx[:, 2:3]),
                 reads=[psr(sp * 2), psr(sp * 2 + 1), "smx2"], writes=[("PT", pp)])
            ob = 6
            if idx > 0:
                emit_summm(idx - 1)
                if items[idx - 1][1] == NP - 1:
                    emit_norm(items[idx - 1][0])
            for cl in range(2):
                chunk = p * 2 + cl
                P.op("tensor", lambda e, ob=ob, chunk=chunk, g=g, pp=pp, cl=cl: e.matmul(
                    bank(ob), V[:, chunk, g * 128:(g + 1) * 128], PT[pp][:, cl * 512:(cl + 1) * 512],
                    start=(chunk == 0), stop=(chunk == NCHUNK - 1)),
                    reads=["V", ("PT", pp)], writes=[psr(ob)], signal=(cl == 1))
            tq = idx % 2
            P.op("vector", lambda e, pp=pp, tq=tq: e.tensor_tensor(
                out=tmpb[tq], in0=PT[pp][:, 0:512], in1=PT[pp][:, 512:1024], op=ALU.add),
                reads=[("PT", pp)], writes=[("tmpb", tq)])
        emit_summm(len(items) - 1)
        emit_norm(len(iters) - 1)

        chk("E")
        W_F = ["fg", "gate_bc", ("diag", 0), ("diag", 1)]
        P.inherit(W_F, W_E)
        P.inherit([("xo", 0), ("xo", 1)], ["qT"])
        P.inherit([("res", 0), ("res", 1)], ["sga"])
        P.dma("sync", "fgk", fg_sb, fg, writes=["fg"])
        for blk0 in range(2):
            P.dma("sync", f"xo{blk0}", xo[blk0], xown[blk0 * 128:(blk0 + 1) * 128, :], writes=[("xo", blk0)])
        def build_gate_bc():
            for q4 in range(4):
                db = diagb[q4 % 2]
                for a in range(4):
                    kc = q4 * 4 + a
                    P.op("vector", lambda e, db=db, a=a, kc=kc: e.tensor_scalar(
                        out=db[:, a, :], in0=ident_f[:], scalar1=m_sb[:, 32 + kc, 0:1], scalar2=None, op0=ALU.mult),
                        reads=["identf", ("m", 11)], writes=[("diag", q4 % 2)])
                for a in range(4):
                    P.op("tensor", lambda e, db=db, a=a: e.matmul(
                        bank(7)[:, a * 128:(a + 1) * 128], ones_f[:], db[:, a, :], start=True, stop=True),
                        reads=[("diag", q4 % 2), "ones_f"], writes=[psr(7)], signal=(a == 3))
                P.op("scalar", lambda e, q4=q4: e.activation(out=gate_bc[:, q4 * 512:(q4 + 1) * 512], in_=bank(7), func=AF.Copy),
                     reads=[psr(7)], writes=["gate_bc"])


        for blk in range(8):
            par = blk % 2
            for c in range(4):
                b = par * 4 + c
                for kc in range(KC):
                    P.op("tensor", lambda e, b=b, kc=kc, blk=blk, c=c: e.matmul(
                        bank(b), mixT[:, kc, blk * 128:(blk + 1) * 128], woutb[c][:, kc, :],
                        start=(kc == 0), stop=(kc == KC - 1)),
                        reads=["mixT", ("wout", c)], writes=[psr(b)], signal=(kc == KC - 1))
            rsb = resb[par]
            if blk == 0:
                build_gate_bc()
            for hh in range(2):
                P.op("vector", lambda e, rsb=rsb, par=par, hh=hh: e.tensor_tensor(
                    out=rsb[:, hh * 1024:(hh + 1) * 1024], in0=PS[par * 2 + hh][:],
                    in1=gate_bc[:, hh * 1024:(hh + 1) * 1024], op=ALU.mult),
                    reads=[psr(par * 4 + hh * 2), psr(par * 4 + hh * 2 + 1), "gate_bc"], writes=[("res", par)])
            P.op("vector", lambda e, rsb=rsb, par=par: e.tensor_tensor(out=rsb, in0=rsb, in1=xo[par], op=ALU.add),
                 reads=[("res", par), ("xo", par)], writes=[("res", par)])
            P.op("scalar", lambda e, rsb=rsb, par=par: e.activation(out=xo[par], in_=rsb, func=AF.Square,
                                                                   accum_out=ss[:, 0:1]),
                 reads=[("res", par)], writes=[("ss", 0), ("xo", par)])
            if blk + 2 < 8:
                P.dma("sync", f"xo{par}", xo[par], xown[(blk + 2) * 128:(blk + 3) * 128, :], writes=[("xo", par)])
            P.op("scalar", lambda e: e.activation(out=rstd_s[:, 0:1], in_=ss[:, 0:1], func=AF.Ln,
                                                  bias=eps_sb[:, 0:1], scale=1.0 / D),
                 reads=[("ss", 0), "eps"], writes=["rstd_s"])
            P.op("scalar", lambda e: e.activation(out=rstd_s[:, 0:1], in_=rstd_s[:, 0:1], func=AF.Exp, scale=-0.5),
                 reads=["rstd_s"], writes=["rstd_s"])
            P.op("vector", lambda e, rsb=rsb: e.scalar_tensor_tensor(
                out=rsb, in0=rsb, scalar=rstd_s[:, 0:1], in1=fg_sb, op0=ALU.mult, op1=ALU.mult),
                reads=[("res", par), "rstd_s", "fg"], writes=[("res", par)])
            P.dma("sync", f"o{par}", out[blk * 128:(blk + 1) * 128, :], rsb, reads=[("res", par)], writes=[])
        P.final_wait("sync", ["o0", "o1"])


def emit_rope(P, kn, t1, t2, rb, cos_blk, sin_blk, H, tab_res, rot_res, kn_res="kn", eng="gpsimd", tres=("t1", "t2")):
    knv = kn[:, 0:H, :].rearrange("p h (a b j) -> p h a b j", a=2, b=2)
    t2v = t2[:, 0:H, :].rearrange("p h (a b j) -> p h a b j", a=2, b=2)
    sv = sin_blk.rearrange("p (a b j) -> p a b j", a=2, b=2)
    P.op(eng, lambda e: e.tensor_tensor(
        out=t1[:, 0:H, :], in0=kn[:, 0:H, :], in1=cos_blk.unsqueeze(1).broadcast_to([128, H, 128]), op=ALU.mult),
        reads=[kn_res] + tab_res, writes=[tres[0]])
    for bsel in range(2):
        P.op(eng, lambda e, bsel=bsel: e.tensor_tensor(
            out=t2v[:, :, :, bsel, :], in0=knv[:, :, :, 1 - bsel, :],
            in1=sv[:, :, bsel, :].unsqueeze(1).broadcast_to([128, H, 2, 32]), op=ALU.mult),
            reads=[kn_res] + tab_res, writes=[(tres[1], bsel)])
    P.op(eng, lambda e: e.tensor_tensor(out=rb[:, 0:H, :], in0=t1[:, 0:H, :], in1=t2[:, 0:H, :], op=ALU.add),
         reads=[tres[0], (tres[1], 0), (tres[1], 1)], writes=[rot_res, tres[1]])


_NC_CACHE = {}


def _rope_tables():
    quarter = 32
    inv = (10000.0 ** (-np.arange(quarter, dtype=np.float32) / quarter)).astype(np.float32)
    t = np.arange(SEQ)
    row = (t // 64).astype(np.float32)
    col = (t % 64).astype(np.float32)
    ar = row[:, None] * inv[None, :]
    ac = col[:, None] * inv[None, :]
    cr, sr, c_c, s_c = np.cos(ar), np.sin(ar), np.cos(ac), np.sin(ac)
    cosF = np.concatenate([cr, cr, c_c, c_c], axis=1).astype(np.float32)
    sinF = np.concatenate([-sr, sr, -s_c, s_c], axis=1).astype(np.float32)
    return cosF, sinF


def kernel(x, c, ctx, c_ctx, w_mod, b_mod, norm_g, w_in, q_norm_g, k_norm_g, conv_w, w_out, final_norm_g):
    f = lambda a: np.ascontiguousarray(np.asarray(a, dtype=np.float32))
    x, c, ctx, c_ctx = f(x), f(c), f(ctx), f(c_ctx)
    w_mod0, w_in0, w_out0 = f(w_mod[0]), f(w_in[0]), f(w_out[0])
    b_mod0, ng0, qg0, kg0, cw0, fg0 = f(b_mod[0]), f(norm_g[0]), f(q_norm_g[0]), f(k_norm_g[0]), f(conv_w[0]), f(final_norm_g)
    if "nc" not in _NC_CACHE:
        _NC_CACHE["nc"] = build_program()
    nc = _NC_CACHE["nc"]
    cosF, sinF = _rope_tables()
    shared = {
        "w_mod": w_mod0, "w_in": np.ascontiguousarray(w_in0[:, :2560]), "w_out": w_out0,
        "w_conv": np.ascontiguousarray(
            w_in0[:, 2560:].reshape(D, 4, 8, 128).transpose(0, 2, 1, 3).reshape(D, 4096)),
        "bmod": np.ascontiguousarray(b_mod0.reshape(48, 128).T),
        "ng": np.ascontiguousarray(ng0.reshape(16, 128).T),
        "qg": np.ascontiguousarray(np.broadcast_to(qg0[None, :], (128, 128))),
        "kg": np.ascontiguousarray(np.broadcast_to(kg0[None, :], (128, 128))),
        "convw": np.ascontiguousarray(cw0.reshape(3, 8, 128).transpose(2, 1, 0).reshape(128, 24)),
        "fg": np.ascontiguousarray(np.broadcast_to(fg0[None, :], (128, D))),
        "ident": np.eye(128, dtype=np.float32),
    }
    in_maps = []
    for core in range(8):
        b, j = core // 4, core % 4
        t0 = j * OWN
        order = np.concatenate([np.arange(0, t0), np.arange(t0 + OWN, SEQ), np.arange(t0, t0 + OWN)])
        xb = x[b]
        xTc = np.ascontiguousarray(xb[order].T)
        halo = np.zeros((2, D), np.float32)
        hm = np.zeros((128, 2), np.float32)
        if t0 > 0:
            halo[0] = xb[t0 - 1]
            hm[:, 0] = 1.0
        if t0 + OWN < SEQ:
            halo[1] = xb[t0 + OWN]
            hm[:, 1] = 1.0
        ccm = np.stack([c[b].reshape(16, 128).T, c_ctx.reshape(16, 128).T], axis=2).reshape(128, 32)
        m = dict(shared)
        m.update({
            "xT": xTc, "xTh": np.ascontiguousarray(halo.T), "xown": np.ascontiguousarray(xb[t0:t0 + OWN]),
            "ctxT": np.ascontiguousarray(ctx[b].T), "cc": np.ascontiguousarray(ccm),
            "cosF": np.ascontiguousarray(cosF[order]), "sinF": np.ascontiguousarray(sinF[order]), "hmask": hm,
        })
        in_maps.append(m)
    if _NC_CACHE.get("prep_only"):
        return nc, in_maps
    res = run_bass_kernel_spmd(nc, in_maps, core_ids=list(range(8)))
    outp = np.empty((2, SEQ, D), np.float32)
    for core in range(8):
        b, j = core // 4, core % 4
        outp[b, j * OWN:(j + 1) * OWN] = res.results[core]["out"]
    return outp
```
